# Optimizing a Trainium2 kernel written in Bass

```python
import math
import jax
import jax.numpy as jnp
from jax import lax
import numpy as np

D_MODEL = 2048
BATCH = 16
SEQ = 256
DEPTH = 2
DEC_BATCH = 8
DEC_SEQ = 1024
PAST_LEN = 512

GRID_W = 64
HEAD_DIM = 128
N_MIXERS = 4
GROUP_WIDTH = D_MODEL // N_MIXERS
RET_HEADS = GROUP_WIDTH // HEAD_DIM
GDN_HEADS = GROUP_WIDTH // HEAD_DIM
HG_HEADS = GROUP_WIDTH // HEAD_DIM
S5_CH = 16
S5_GROUPS = GROUP_WIDTH // S5_CH
S5_N = 64
GDN_CONV = 3
CHUNK = 64
HG_CHUNK = 16
ROPE_BASE = 10000.0
FFN_HIDDEN = -(-8 * D_MODEL // (3 * 256)) * 256
N_MOD = 6
EPS = 1e-6
LB_FLOOR = 1e-30
PROJ_WIDTH = 14 * GROUP_WIDTH + 4 * GDN_HEADS

kernel_name = 'hybrid_bidir_flow_trunk_step'


def _rmsnorm(x, w):
    xf = x.astype(jnp.float32)
    y = xf * lax.rsqrt(jnp.mean(xf * xf, axis=-1, keepdims=True) + EPS)
    return (y * w.astype(jnp.float32)).astype(x.dtype)


def _head_layernorm(x):
    mu = jnp.mean(x, axis=-1, keepdims=True)
    xc = x - mu
    return xc * lax.rsqrt(jnp.mean(xc * xc, axis=-1, keepdims=True) + EPS)


def _l2norm(x):
    return x * lax.rsqrt(jnp.sum(x * x, axis=-1, keepdims=True) + EPS)


def _heads(t, n_heads):
    b, l, _ = t.shape
    return t.reshape(b, l, n_heads, -1).transpose(0, 2, 1, 3)


def _merge(t):
    b, h, l, d = t.shape
    return t.transpose(0, 2, 1, 3).reshape(b, l, h * d)


def _chunks(t, size):
    return t.reshape(t.shape[:2] + (t.shape[2] // size, size) + t.shape[3:])


def _rev(t):
    return jnp.flip(t, axis=2)


def _axial_rope(l):
    n_rows = l // GRID_W
    t_row = jnp.repeat(jnp.arange(n_rows, dtype=jnp.float32), GRID_W)
    t_col = jnp.tile(jnp.arange(GRID_W, dtype=jnp.float32), n_rows)
    n_freq = HEAD_DIM // 4
    inv = ROPE_BASE ** (-jnp.arange(n_freq, dtype=jnp.float32) / n_freq)
    ang = jnp.concatenate([t_row[:, None] * inv, t_col[:, None] * inv], axis=-1)
    return jnp.cos(ang), jnp.sin(ang)


def _apply_rope(x, cos, sin):
    x1, x2 = jnp.split(x, 2, axis=-1)
    return jnp.concatenate([x1 * cos - x2 * sin, x1 * sin + x2 * cos], axis=-1)


def _dwconv(x, w):
    ch = x.shape[-1]
    pad = (GDN_CONV - 1) // 2
    return lax.conv_general_dilated(x, w[:, None, :], window_strides=(1,), padding=[(pad, pad)],
                                    dimension_numbers=('NWC', 'WIO', 'NWC'), feature_group_count=ch)


def _retention(q, k, v, log_gamma, s0):
    b, h, l, d = q.shape
    qc, kc, vc = _chunks(q, CHUNK), _chunks(k, CHUNK), _chunks(v, CHUNK)
    pos = jnp.arange(CHUNK, dtype=jnp.float32)
    rel = pos[:, None] - pos[None, :]
    intra = jnp.where(rel >= 0, jnp.exp(jnp.maximum(rel, 0.0) * log_gamma[:, None, None]), 0.0)
    q_dec = jnp.exp((pos + 1.0) * log_gamma[:, None])
    k_dec = jnp.exp((CHUNK - 1.0 - pos) * log_gamma[:, None])
    c_dec = jnp.exp(CHUNK * log_gamma)[None, :, None, None]
    scores = jnp.einsum('bhnid,bhnjd->bhnij', qc, kc) * intra[None, :, None]
    o_intra = jnp.einsum('bhnij,bhnje->bhnie', scores, vc)
    kv = jnp.einsum('bhnjd,hj,bhnje->bhnde', kc, k_dec, vc)

    def step(s, kv_n):
        return c_dec * s + kv_n, s

    s_fin, s_prev = lax.scan(step, s0, jnp.moveaxis(kv, 2, 0))
    o_inter = jnp.einsum('bhnid,hi,bhnde->bhnie', qc, q_dec, jnp.moveaxis(s_prev, 0, 2))
    return (o_intra + o_inter).reshape(b, h, l, d), s_fin


def _gated_delta(q, k, v, log_alpha, beta, s0):
    b, h, l, _ = q.shape
    qc, kc, vc = _chunks(q, CHUNK), _chunks(k, CHUNK), _chunks(v, CHUNK)
    g = jnp.cumsum(_chunks(log_alpha, CHUNK), axis=-1)
    bc = _chunks(beta, CHUNK)
    idx = jnp.arange(CHUNK)
    incl = idx[:, None] >= idx[None, :]
    strict = idx[:, None] > idx[None, :]
    decay = jnp.exp(jnp.where(incl, g[..., :, None] - g[..., None, :], -jnp.inf))
    kb = kc * bc[..., None]
    a_mat = jnp.where(strict, jnp.einsum('bhnid,bhnjd->bhnij', kb, kc) * decay, 0.0)
    eye = jnp.eye(CHUNK, dtype=jnp.float32)
    t_mat = lax.linalg.triangular_solve(a_mat + eye, jnp.broadcast_to(eye, a_mat.shape),
                                        left_side=True, lower=True, unit_diagonal=True)
    u = jnp.einsum('bhnij,bhnje->bhnie', t_mat, vc * bc[..., None])
    w = jnp.einsum('bhnij,bhnjd->bhnid', t_mat, kb * jnp.exp(g)[..., None])
    attn = jnp.einsum('bhnid,bhnjd->bhnij', qc, kc) * decay
    qg = qc * jnp.exp(g)[..., None]
    kt = kc * jnp.exp(g[..., -1:] - g)[..., None]
    cd = jnp.exp(g[..., -1])

    def step(s, xs):
        u_n, w_n, qg_n, at_n, kt_n, cd_n = xs
        v_new = u_n - jnp.einsum('bhcd,bhde->bhce', w_n, s)
        o = jnp.einsum('bhcd,bhde->bhce', qg_n, s) + jnp.einsum('bhcs,bhse->bhce', at_n, v_new)
        s = s * cd_n[..., None, None] + jnp.einsum('bhcd,bhce->bhde', kt_n, v_new)
        return s, o

    xs = (jnp.moveaxis(u, 2, 0), jnp.moveaxis(w, 2, 0), jnp.moveaxis(qg, 2, 0),
          jnp.moveaxis(attn, 2, 0), jnp.moveaxis(kt, 2, 0), jnp.moveaxis(cd, 2, 0))
    s_fin, o = lax.scan(step, s0, xs)
    return jnp.moveaxis(o, 0, 2).reshape(b, h, l, -1), s_fin


def _gla(q, k, v, log_f, s0):
    b, h, l, _ = q.shape
    qc, kc, vc, fc = (_chunks(q, HG_CHUNK), _chunks(k, HG_CHUNK), _chunks(v, HG_CHUNK), _chunks(log_f, HG_CHUNK))
    cum = jnp.cumsum(fc, axis=-2)
    idx = jnp.arange(HG_CHUNK)
    incl = (idx[:, None] >= idx[None, :])[:, :, None]
    dec = jnp.exp(jnp.where(incl, cum[..., :, None, :] - cum[..., None, :, :], -jnp.inf))
    attn = jnp.einsum('bhnid,bhnjd,bhnijd->bhnij', qc, kc, dec)
    o_intra = jnp.einsum('bhnij,bhnje->bhnie', attn, vc)
    q_in = qc * jnp.exp(cum)
    k_tail = kc * jnp.exp(cum[..., -1:, :] - cum)
    cd = jnp.exp(cum[..., -1, :])
    kv = jnp.einsum('bhnjd,bhnje->bhnde', k_tail, vc)

    def step(s, xs):
        kv_n, cd_n = xs
        return cd_n[..., None] * s + kv_n, s

    s_fin, s_prev = lax.scan(step, s0, (jnp.moveaxis(kv, 2, 0), jnp.moveaxis(cd, 2, 0)))
    o_inter = jnp.einsum('bhnid,bhnde->bhnie', q_in, jnp.moveaxis(s_prev, 0, 2))
    return (o_intra + o_inter).reshape(b, h, l, -1), s_fin


def _cplx_combine(e1, e2):
    a1r, a1i, b1r, b1i = e1
    a2r, a2i, b2r, b2i = e2
    return (a1r * a2r - a1i * a2i, a1r * a2i + a1i * a2r,
            a2r * b1r - a2i * b1i + b2r, a2r * b1i + a2i * b1r + b2i)


def _s5(u, a_re, a_im, log_step, b_re, b_im, c_re, c_im, x0_re, x0_im):
    dt = jnp.exp(log_step)[:, None]
    mag = jnp.exp(a_re * dt)
    ab_re = mag * jnp.cos(a_im * dt)
    ab_im = mag * jnp.sin(a_im * dt)
    den = a_re * a_re + a_im * a_im
    nr = ab_re - 1.0
    f_re = (nr * a_re + ab_im * a_im) / den
    f_im = (ab_im * a_re - nr * a_im) / den
    bb_re = f_re[..., None] * b_re - f_im[..., None] * b_im
    bb_im = f_re[..., None] * b_im + f_im[..., None] * b_re
    bu_re = jnp.einsum('blgc,gnc->blgn', u, bb_re)
    bu_im = jnp.einsum('blgc,gnc->blgn', u, bb_im)
    bu_re = bu_re.at[:, 0].add(ab_re * x0_re - ab_im * x0_im)
    bu_im = bu_im.at[:, 0].add(ab_re * x0_im + ab_im * x0_re)
    a_full_re = jnp.broadcast_to(ab_re, bu_re.shape)
    a_full_im = jnp.broadcast_to(ab_im, bu_im.shape)
    _, _, x_re, x_im = lax.associative_scan(_cplx_combine, (a_full_re, a_full_im, bu_re, bu_im), axis=1)
    y = jnp.einsum('blgn,gcn->blgc', x_re, c_re) - jnp.einsum('blgn,gcn->blgc', x_im, c_im)
    return y, x_re[:, -1], x_im[:, -1]


def _mixer(h, lp, lower_bound, s_ret, s_gdn, s_hg, s_s5_re, s_s5_im, rope):
    f32 = jnp.float32
    bsz, l, _ = h.shape
    w = GROUP_WIDTH
    widths = (w, w, w, w, w, w, w, w, 2 * GDN_HEADS, 2 * GDN_HEADS, w, 2 * w, w, w)
    offsets, acc = [], 0
    for wd in widths:
        acc += wd
        offsets.append(acc)
    proj = (h @ lp['in_proj']).astype(f32)
    (r_q, r_k, r_v, r_g, d_q, d_k, d_v, d_g, d_a, d_b,
     h_q, h_f, h_i, h_g, s_u) = jnp.split(proj, offsets, axis=-1)

    q = _heads(r_q, RET_HEADS)
    k = _heads(r_k, RET_HEADS)
    v = _heads(r_v, RET_HEADS)
    if rope is not None:
        q = _apply_rope(q, *rope)
        k = _apply_rope(k, *rope)
    k = k * HEAD_DIM ** -0.5
    log_gamma = jax.nn.log_sigmoid(lp['ret_decay_logit'].astype(f32))
    o_f, rs_f = _retention(q, k, v, log_gamma[0], s_ret[:, 0])
    o_b, rs_b = _retention(_rev(q), _rev(k), _rev(v), log_gamma[1], s_ret[:, 1])
    ret_out = _merge(_head_layernorm(o_f + _rev(o_b))) * jax.nn.silu(r_g)

    qkv = jax.nn.silu(_dwconv(jnp.concatenate([d_q, d_k, d_v], axis=-1), lp['gdn_conv'].astype(f32)))
    gq, gk, gv = jnp.split(qkv, 3, axis=-1)
    q = _l2norm(_heads(gq, GDN_HEADS)) * HEAD_DIM ** -0.5
    k = _l2norm(_heads(gk, GDN_HEADS))
    v = _heads(gv, GDN_HEADS)
    a = d_a.reshape(bsz, l, 2, GDN_HEADS)
    bt = d_b.reshape(bsz, l, 2, GDN_HEADS)
    log_alpha = (-jnp.exp(lp['gdn_a_log'].astype(f32))
                 * jax.nn.softplus(a + lp['gdn_dt_bias'].astype(f32))).transpose(2, 0, 3, 1)
    beta = jax.nn.sigmoid(bt).transpose(2, 0, 3, 1)
    o_f, gs_f = _gated_delta(q, k, v, log_alpha[0], beta[0], s_gdn[:, 0])
    o_b, gs_b = _gated_delta(_rev(q), _rev(k), _rev(v), _rev(log_alpha[1]), _rev(beta[1]), s_gdn[:, 1])
    gdn_out = _merge(_rmsnorm(o_f + _rev(o_b), lp['gdn_norm_w'])) * jax.nn.silu(d_g)

    q = _heads(jax.nn.silu(h_q), HG_HEADS) * HEAD_DIM ** -0.5
    v = _heads(h_i, HG_HEADS)
    fx = h_f.reshape(bsz, l, 2, w)
    log_f = jnp.logaddexp(jnp.log(jnp.maximum(lower_bound, LB_FLOOR)),
                          jnp.log1p(-lower_bound) + jax.nn.log_sigmoid(fx))
    key_in = (1.0 - lower_bound) * jax.nn.sigmoid(-fx)
    o_f, hs_f = _gla(q, _heads(key_in[:, :, 0], HG_HEADS), v, _heads(log_f[:, :, 0], HG_HEADS), s_hg[:, 0])
    o_b, hs_b = _gla(_rev(q), _rev(_heads(key_in[:, :, 1], HG_HEADS)), _rev(v),
                     _rev(_heads(log_f[:, :, 1], HG_HEADS)), s_hg[:, 1])
    hg_out = _merge(_rmsnorm(o_f + _rev(o_b), lp['hg_norm_w'])) * jax.nn.silu(h_g)

    u = s_u.reshape(bsz, l, S5_GROUPS, S5_CH)
    p = [lp[nm].astype(f32) for nm in ('s5_a_re', 's5_a_im', 's5_log_step', 's5_b_re', 's5_b_im', 's5_c_re', 's5_c_im')]
    y_f, xr_f, xi_f = _s5(u, p[0][0], p[1][0], p[2][0], p[3][0], p[4][0], p[5][0], p[6][0], s_s5_re[:, 0], s_s5_im[:, 0])
    y_b, xr_b, xi_b = _s5(jnp.flip(u, 1), p[0][1], p[1][1], p[2][1], p[3][1], p[4][1], p[5][1], p[6][1],
                          s_s5_re[:, 1], s_s5_im[:, 1])
    y = (y_f + jnp.flip(y_b, 1)).reshape(bsz, l, w) + lp['s5_d'].astype(f32) * s_u
    z = jax.nn.gelu(y)
    s5_out = z * jax.nn.sigmoid(z @ lp['s5_glu_w'].astype(f32) + lp['s5_glu_b'].astype(f32))

    mixed = jnp.concatenate([ret_out, gdn_out, hg_out, s5_out], axis=-1).astype(h.dtype) @ lp['out_proj']
    new_states = (jnp.stack([rs_f, rs_b], axis=1), jnp.stack([gs_f, gs_b], axis=1),
                  jnp.stack([hs_f, hs_b], axis=1), jnp.stack([xr_f, xr_b], axis=1),
                  jnp.stack([xi_f, xi_b], axis=1))
    return mixed, new_states


def _layer(x, mod, lp, lower_bound, states, rope):
    shift1, scale1, gate1, shift2, scale2, gate2 = jnp.split(mod[:, None, :], N_MOD, axis=-1)
    h = _rmsnorm(x, lp['norm1_w']) * (1.0 + scale1) + shift1
    mixed, new_states = _mixer(h, lp, lower_bound, states[0], states[1], states[2], states[3], states[4], rope)
    x = x + gate1 * mixed
    h = _rmsnorm(x, lp['norm2_w']) * (1.0 + scale2) + shift2
    ffn = (jax.nn.silu(h @ lp['ffn_w1']) * (h @ lp['ffn_w3'])) @ lp['ffn_w2']
    x = x + gate2 * ffn
    return x, new_states


def setup_inputs(seed: int = 0) -> dict:
    key = jax.random.key(seed)
    ks = iter(jax.random.split(key, 48))
    f32 = jnp.float32
    w = GROUP_WIDTH

    def nrm(shape, scale):
        return jax.random.normal(next(ks), shape, f32) * scale

    def unif(shape, lo, hi):
        return jax.random.uniform(next(ks), shape, f32, lo, hi)

    st_shape = (DEC_BATCH, DEPTH, 2, RET_HEADS, HEAD_DIM, HEAD_DIM)
    s5_st_shape = (DEC_BATCH, DEPTH, 2, S5_GROUPS, S5_N)
    gamma = 1.0 - 2.0 ** (-5.0 - jnp.arange(RET_HEADS, dtype=f32))
    dt = jnp.exp(unif((DEPTH, 2, GDN_HEADS), math.log(1e-3), math.log(0.1)))
    n_idx = jnp.arange(S5_N, dtype=f32)
    return {
        'x_prompt': nrm((BATCH, SEQ, D_MODEL), 1.0),
        'x_sample': nrm((DEC_BATCH, DEC_SEQ, D_MODEL), 1.0),
        'state_ret': nrm(st_shape, 0.5),
        'state_gdn': nrm(st_shape, 0.5),
        'state_hgrn': nrm(st_shape, 0.5),
        'state_s5_re': nrm(s5_st_shape, 0.1),
        'state_s5_im': nrm(s5_st_shape, 0.1),
        'c': nrm((DEC_BATCH, D_MODEL), 1.0),
        'c_ctx': nrm((D_MODEL,), 1.0),
        'norm1_w': 1.0 + nrm((DEPTH, D_MODEL), 0.01),
        'norm2_w': 1.0 + nrm((DEPTH, D_MODEL), 0.01),
        'final_norm_w': 1.0 + nrm((D_MODEL,), 0.01),
        'ada_w': nrm((DEPTH, D_MODEL, N_MOD * D_MODEL), 0.5 * D_MODEL ** -0.5),
        'ada_b': nrm((DEPTH, N_MOD * D_MODEL), 0.01),
        'in_proj': nrm((DEPTH, D_MODEL, PROJ_WIDTH), D_MODEL ** -0.5),
        'out_proj': nrm((DEPTH, D_MODEL, D_MODEL), D_MODEL ** -0.5),
        'ret_decay_logit': jnp.log(gamma / (1.0 - gamma)) + nrm((DEPTH, 2, RET_HEADS), 0.1),
        'gdn_conv': nrm((DEPTH, GDN_CONV, 3 * w), GDN_CONV ** -0.5),
        'gdn_a_log': jnp.log(unif((DEPTH, 2, GDN_HEADS), 1.0, 16.0)),
        'gdn_dt_bias': dt + jnp.log(-jnp.expm1(-dt)),
        'gdn_norm_w': 1.0 + nrm((DEPTH, HEAD_DIM), 0.01),
        'hg_lb_param': nrm((DEPTH, 2, w), 0.1),
        'hg_norm_w': 1.0 + nrm((DEPTH, HEAD_DIM), 0.01),
        's5_a_re': -0.5 + nrm((DEPTH, 2, S5_GROUPS, S5_N), 0.01),
        's5_a_im': jnp.pi * n_idx + nrm((DEPTH, 2, S5_GROUPS, S5_N), 0.01),
        's5_b_re': nrm((DEPTH, 2, S5_GROUPS, S5_N, S5_CH), (2 * S5_CH) ** -0.5),
        's5_b_im': nrm((DEPTH, 2, S5_GROUPS, S5_N, S5_CH), (2 * S5_CH) ** -0.5),
        's5_c_re': nrm((DEPTH, 2, S5_GROUPS, S5_CH, S5_N), S5_N ** -0.5),
        's5_c_im': nrm((DEPTH, 2, S5_GROUPS, S5_CH, S5_N), S5_N ** -0.5),
        's5_log_step': unif((DEPTH, 2, S5_GROUPS), math.log(1e-3), math.log(0.1)),
        's5_d': nrm((DEPTH, w), 1.0),
        's5_glu_w': nrm((DEPTH, w, w), w ** -0.5),
        's5_glu_b': nrm((DEPTH, w), 0.01),
        'ffn_w1': nrm((DEPTH, D_MODEL, FFN_HIDDEN), D_MODEL ** -0.5),
        'ffn_w3': nrm((DEPTH, D_MODEL, FFN_HIDDEN), D_MODEL ** -0.5),
        'ffn_w2': nrm((DEPTH, FFN_HIDDEN, D_MODEL), FFN_HIDDEN ** -0.5),
    }


def reference(x_prompt, x_sample, state_ret, state_gdn, state_hgrn, state_s5_re, state_s5_im, c, c_ctx,
              norm1_w, norm2_w, final_norm_w, ada_w, ada_b, in_proj, out_proj, ret_decay_logit,
              gdn_conv, gdn_a_log, gdn_dt_bias, gdn_norm_w, hg_lb_param, hg_norm_w,
              s5_a_re, s5_a_im, s5_b_re, s5_b_im, s5_c_re, s5_c_im, s5_log_step, s5_d,
              s5_glu_w, s5_glu_b, ffn_w1, ffn_w3, ffn_w2):
    f32 = jnp.float32
    lb_soft = jax.nn.softmax(hg_lb_param.astype(f32), axis=0)
    lower_bounds = jnp.cumsum(lb_soft, axis=0) - lb_soft[0]
    rope = _axial_rope(x_sample.shape[1])
    bp = x_prompt.shape[0]
    zero_mat = jnp.zeros((bp, 2, RET_HEADS, HEAD_DIM, HEAD_DIM), f32)
    zero_s5 = jnp.zeros((bp, 2, S5_GROUPS, S5_N), f32)
    ctx_init = (zero_mat, zero_mat, zero_mat, zero_s5, zero_s5)
    hp, hs = x_prompt, x_sample
    ctx_states = []
    for i in range(DEPTH):
        lp = {
            'norm1_w': norm1_w[i], 'norm2_w': norm2_w[i], 'in_proj': in_proj[i], 'out_proj': out_proj[i],
            'ret_decay_logit': ret_decay_logit[i], 'gdn_conv': gdn_conv[i], 'gdn_a_log': gdn_a_log[i],
            'gdn_dt_bias': gdn_dt_bias[i], 'gdn_norm_w': gdn_norm_w[i], 'hg_norm_w': hg_norm_w[i],
            's5_a_re': s5_a_re[i], 's5_a_im': s5_a_im[i], 's5_log_step': s5_log_step[i],
            's5_b_re': s5_b_re[i], 's5_b_im': s5_b_im[i], 's5_c_re': s5_c_re[i], 's5_c_im': s5_c_im[i],
            's5_d': s5_d[i], 's5_glu_w': s5_glu_w[i], 's5_glu_b': s5_glu_b[i],
            'ffn_w1': ffn_w1[i], 'ffn_w3': ffn_w3[i], 'ffn_w2': ffn_w2[i],
        }
        mod_ctx = jax.nn.silu(c_ctx)[None, :] @ ada_w[i] + ada_b[i]
        mod_lat = jax.nn.silu(c) @ ada_w[i] + ada_b[i]
        hp, st = _layer(hp, mod_ctx, lp, lower_bounds[i], ctx_init, None)
        ctx_states.append(st)
        lat_init = (state_ret[:, i].astype(f32), state_gdn[:, i].astype(f32), state_hgrn[:, i].astype(f32),
                    state_s5_re[:, i].astype(f32), state_s5_im[:, i].astype(f32))
        hs, _ = _layer(hs, mod_lat, lp, lower_bounds[i], lat_init, rope)
    y_prompt = _rmsnorm(hp, final_norm_w)
    y_sample = _rmsnorm(hs, final_norm_w)
    sdt = x_prompt.dtype
    new_ret = jnp.stack([s[0] for s in ctx_states], axis=1).astype(sdt)
    new_gdn = jnp.stack([s[1] for s in ctx_states], axis=1).astype(sdt)
    new_hgrn = jnp.stack([s[2] for s in ctx_states], axis=1).astype(sdt)
    new_s5_re = jnp.stack([s[3] for s in ctx_states], axis=1).astype(sdt)
    new_s5_im = jnp.stack([s[4] for s in ctx_states], axis=1).astype(sdt)
    return (y_prompt, y_sample, new_ret, new_gdn, new_hgrn, new_s5_re, new_s5_im)
```

```python
import numpy as np
import ml_dtypes
from contextlib import ExitStack
import concourse.bass as bass
import concourse.mybir as mybir
from concourse.bass_utils import run_bass_kernel_spmd

F32 = mybir.dt.float32
BF16 = mybir.dt.bfloat16
I32 = mybir.dt.int32
AF = mybir.ActivationFunctionType
ALU = mybir.AluOpType


class Prog:
    ENG = ('pe', 'dve', 'act', 'pool', 'sp')
    NDMA = {'sp': 12}
    EPOCH = 16384
    NEPOCH = {'pe': 6, 'dve': 3, 'act': 3, 'pool': 2, 'sp': 1}

    def __init__(self, nc):
        self.nc = nc
        self.es = ExitStack()
        self.eng = {'pe': nc.tensor, 'dve': nc.vector, 'act': nc.scalar, 'pool': nc.gpsimd, 'sp': nc.sync}
        self.csem = {e: [self.es.enter_context(nc.semaphore("c_%s%d" % (e, i))) for i in range(self.NEPOCH[e])] for e in self.ENG}
        self.dsem = {e: [self.es.enter_context(nc.semaphore("d_%s%d" % (e, i))) for i in range(n)]
                     for e, n in self.NDMA.items()}
        self.dval = {e: [0] * n for e, n in self.NDMA.items()}
        self.dnext = {e: 0 for e in self.NDMA}
        self.ops = {e: [] for e in self.ENG}
        self.seen = {e: {} for e in self.ENG}
        self.lastw = {}
        self.readers = {}
        self.pending = {}
        self.cnt_c = {}

    def sb(self, name, shape, dt):
        return self.es.enter_context(self.nc.sbuf_tensor("sb_" + name, list(shape), dt))

    def ps(self, name, shape, dt):
        return self.es.enter_context(self.nc.psum_tensor("ps_" + name, list(shape), dt))

    def _resolve(self, sid, val):
        if sid[0] == 'c':
            r = self.rank[sid[1]][val]
            ep = (r - 1) // self.EPOCH
            return self.csem[sid[1]][ep], r - ep * self.EPOCH
        return self.dsem[sid[1]][sid[2]], val

    def _need(self, e, ev, waits):
        if ev is None:
            return
        sid, val = ev
        if e == 'pe' and sid == ('c', 'pe'):
            return
        if self.seen[e].get(sid, 0) >= val:
            return
        self.seen[e][sid] = val
        waits[sid] = max(waits.get(sid, 0), val)

    def _deps(self, e, reads, writes, self_war=False):
        waits = {}
        for k in reads:
            self._need(e, self.lastw.get(k), waits)
        for k in writes:
            self._need(e, self.lastw.get(k), waits)
            for ev in self.readers.get(k, {}).items():
                if ev[0] == ('c', e) and not self_war:
                    continue
                self._need(e, ev, waits)
        return waits

    def _commit(self, ev, reads, writes):
        for k in reads:
            d = self.readers.setdefault(k, {})
            d[ev[0]] = max(d.get(ev[0], 0), ev[1])
        for k in writes:
            self.lastw[k] = ev
            self.readers[k] = {}

    def barrier(self):
        evs = [(('c', e), self.cnt_c[e]) for e in self.ENG if self.cnt_c.get(e, 0)]
        for e in self.NDMA:
            for i in range(self.NDMA[e]):
                if self.dval[e][i]:
                    evs.append((('d', e, i), self.dval[e][i]))
        for e in self.ENG:
            self.pending[e] = list(evs)

    def _pend(self, e, waits):
        for ev in self.pending.pop(e, []):
            self._need(e, ev, waits)

    def op(self, e, fn, reads=(), writes=()):
        if e != 'pe':
            bk = [k for k in reads if isinstance(k, tuple) and k and k[0] in ('bank', 'stat', 'psr')]
            if bk:
                writes = list(writes) + [k for k in bk if k not in writes]
        waits = self._deps(e, reads, writes, self_war=(e != 'pe'))
        self._pend(e, waits)
        n = self.cnt_c.get(e, 0) + 1
        self.cnt_c[e] = n
        ev = (('c', e), n)
        self.ops[e].append((waits, fn, n, 1))
        self._commit(ev, reads, writes)

    def I(self, e, meth, *args, reads=(), writes=(), **kw):
        self.op(e, lambda en: getattr(en, meth)(*args, **kw), reads=reads, writes=writes)

    def dma(self, e, out, in_, reads=(), writes=(), **kw):
        waits = self._deps(e, reads, writes, self_war=True)
        self._pend(e, waits)
        i = self.dnext[e]
        self.dnext[e] = (i + 1) % self.NDMA[e]
        sem = self.dsem[e][i]
        prev = self.dval[e][i]
        if prev:
            self._need(e, (('d', e, i), prev), waits)
        self.dval[e][i] = prev + 16
        ev = (('d', e, i), prev + 16)
        self.ops[e].append((waits, lambda en: en.dma_start(out=out, in_=in_, **kw), sem, 16))
        self._commit(ev, reads, writes)

    def finish(self):
        fin = []
        for e in self.NDMA:
            for i in range(self.NDMA[e]):
                if self.dval[e][i]:
                    fin.append((('d', e, i), self.dval[e][i]))
        fin += [(('c', e), self.cnt_c[e]) for e in self.ENG if self.cnt_c.get(e, 0)]
        needed = {e: set() for e in self.ENG}
        for e in self.ENG:
            for waits, fn, n, inc in self.ops[e]:
                for sid, val in waits.items():
                    if sid[0] == 'c':
                        needed[sid[1]].add(val)
        for sid, val in fin:
            if sid[0] == 'c':
                needed[sid[1]].add(val)
        self.rank = {e: {n: i + 1 for i, n in enumerate(sorted(needed[e]))} for e in self.ENG}
        nc = self.nc
        prog = self
        with nc.Block() as block:
            def run(name):
                def body(en):
                    for waits, fn, n, inc in prog.ops[name]:
                        for sid, val in waits.items():
                            s, v = prog._resolve(sid, val)
                            en.wait_ge(s, v)
                        ins = fn(en)
                        if inc == 16:
                            ins.then_inc(n, 16)
                        elif n in prog.rank[name]:
                            r = prog.rank[name][n]
                            ins.then_inc(prog.csem[name][(r - 1) // prog.EPOCH], 1)
                    if name == 'sp':
                        for sid, val in fin:
                            s, v = prog._resolve(sid, val)
                            en.wait_ge(s, v)
                return body
            block.tensor(run('pe'))
            block.vector(run('dve'))
            block.scalar(run('act'))
            block.gpsimd(run('pool'))
            block.sync(run('sp'))
        self.es.close()


AW = 28928
SMALL0 = 12288 + 10 * 1536
T = 1536
NT = 12
D = 2048
KC = 16
FF = 5632
NJ = 44
PW = 7184
EPS = 1e-6
SEQS = [(0, 256, 0), (256, 256, 0), (512, 1024, 1)]
TBMOD = [0, 1, 1]


class WStream:
    def __init__(self, P, nstage=3, nbf=5):
        self.P = P
        self.ns, self.nb = nstage, nbf
        self.stage = [P.sb("wst%d" % i, [128, 2048], F32) for i in range(nstage)]
        self.bf = [P.sb("wbf%d" % i, [128, 2048], BF16) for i in range(nbf)]
        self.plan = []
        self.issued = 0
        self.used = 0

    def add(self, name, ap, a, b):
        self.plan.append((name, ap, a, b))

    def _issue(self):
        n = self.issued
        name, ap, a, b = self.plan[n]
        s, t = n % self.ns, n % self.nb
        st = self.stage[s][:, 0:a * b]
        P = self.P
        P.dma('sp', st.rearrange("p (a b) -> p a b", b=b), ap, reads=[], writes=[('wst', s)])
        bfv = self.bf[t][:, 0:a * b]
        if name[0] == 'ada' or n % 2 == 1:
            P.I('act', 'copy', bfv, st, reads=[('wst', s)], writes=[('wbf', t)])
        else:
            P.I('pool', 'tensor_copy', bfv, st, reads=[('wst', s)], writes=[('wbf', t)])
        self.issued += 1

    def next(self, name):
        n = self.used
        assert self.plan[n][0] == name, (self.plan[n][0], name)
        while self.issued < min(len(self.plan), n + self.nb - 1):
            self._issue()
        _, _, a, b = self.plan[n]
        t = n % self.nb
        self.used += 1
        return self.bf[t][:, 0:a * b].rearrange("p (a b) -> p a b", b=b), ('wbf', t)


def build_program(opts=None):
    opts = opts or {}
    mixers = opts.get('mixers', ('ret', 'gdn', 'hg', 's5'))
    nlayers = opts.get('nlayers', 2)
    GDN_HEADS[0] = opts.get('gdn_heads', 4)
    nc = bass.Bass("TRN2", target_bir_lowering=False)
    P = Prog(nc)

    def din(name, shape, dt=F32):
        return nc.dram_tensor(name, list(shape), dt, kind="ExternalInput").ap()

    def dout(name, shape, dt=F32):
        return nc.dram_tensor(name, list(shape), dt, kind="ExternalOutput").ap()

    xin = din("xin", [T, D])
    cT_d = din("cT", [128, KC, 2])
    n1w_d = din("n1w", [2, 128, KC])
    n2w_d = din("n2w", [2, 128, KC])
    fnw_d = din("fnw", [128, KC])
    adab_d = din("adab", [2, 128, 96])
    consts_d = din("consts", [128, NCONST])
    ada_w = din("ada_w", [2, D, 6 * D])
    in_proj = din("in_proj", [2, D, PW])
    out_proj = din("out_proj", [2, D, D])
    ffn_w1 = din("ffn_w1", [2, D, FF])
    ffn_w3 = din("ffn_w3", [2, D, FF])
    ffn_w2 = din("ffn_w2", [2, FF, D])
    y_d = dout("y", [T, D])
    xT = nc.dram_tensor("xT_scr", [D, T], F32).ap()

    hT = P.sb("hT", [128, KC, T], BF16)
    arena = P.sb("arena", [128, AW], F32)
    big2 = arena[:, 0:16896].bitcast(BF16).rearrange("p (j t) -> p j t", t=T)
    catT = big2[:, 0:16, :]
    W = WStream(P, nstage=2, nbf=3)
    consts = P.sb("consts", [128, NCONST], F32)
    ident_f = consts[:, C_IDENT:C_IDENT + 128]
    ident_b = P.sb("ident_b", [128, 128], BF16)
    ones_b = P.sb("ones_b", [128, 128], BF16)
    cT = P.sb("cTs", [128, KC, 2], F32)
    csil = P.sb("csil", [128, KC, 2], BF16)
    n1w = P.sb("n1w", [128, 2, KC], F32)
    n2w = P.sb("n2w", [128, 2, KC], F32)
    fnw = P.sb("fnw", [128, KC], F32)
    adab = P.sb("adab", [128, 2, 96], F32)
    mod = P.sb("mod", [128, 96, 2], F32)
    ws = P.sb("ws", [128, 2, KC, 2], F32)
    rstd = arena[:, 24064:25600].rearrange("p (a b) -> p a b", b=512)
    xs = [arena[:, 16896 + 512 * i:16896 + 512 * (i + 1)] for i in range(3)]
    xt = [arena[:, 18432 + 512 * i:18432 + 512 * (i + 1)] for i in range(2)]
    sq = [arena[:, 19456 + 256 * i:19456 + 256 * (i + 1)].bitcast(BF16) for i in range(2)]
    tokbuf = [arena[:, 19968 + 2048 * i:19968 + 2048 * (i + 1)] for i in range(2)]
    banks = [P.ps("bank%d" % i, [128, 512], F32) for i in range(8)]
    st = {'bank': 0, 'xs': 0, 'xt': 0, 'sq': 0, 'alt': 0, 'nbanks': 4}

    def bank():
        b = st['bank']
        st['bank'] = (b + 1) % st['nbanks']
        return banks[b], ('bank', b)

    def rr(name, n):
        i = st.get(name, 0)
        st[name] = (i + 1) % n
        return i

    def evac_eng():
        st['alt'] ^= 1
        return 'act' if st['alt'] else 'dve'

    def copy_op(out, in_, reads, writes, eng=None):
        eng = eng or evac_eng()
        if eng == 'act':
            P.I('act', 'copy', out, in_, reads=reads, writes=writes)
        else:
            P.I(eng, 'tensor_copy', out, in_, reads=reads, writes=writes)

    ctx_pre = {}
    def wview(ap2d):
        return ap2d.rearrange("(kc p) n -> p kc n", p=128)

    def add_ada(l, cb):
        W.add(('ada', l, cb), wview(ada_w[l])[:, :, cb * 128:(cb + 1) * 128], KC, 128)

    for l in range(nlayers):
        if l == 0:
            for cb in range(96):
                add_ada(l, cb)
        for name, c0, n in unit_cols(mixers):
            W.add(('in', l, name), wview(in_proj[l])[:, :, c0:c0 + n], KC, n)
        if 's5' in mixers:
            if 'glu_w' not in ctx_pre:
                ctx_pre['glu_w'] = din("s5_glu_w", [2, 512, 512])
            glu_w = ctx_pre['glu_w']
            for oc in range(4):
                W.add(('glu', l, oc), glu_w[l][:, oc * 128:(oc + 1) * 128].rearrange("(kc p) n -> p kc n", p=128), 4, 128)
        for fc in range(KC):
            W.add(('out', l, fc), wview(out_proj[l])[:, :, fc * 128:(fc + 1) * 128], KC, 128)
        for g in range(2):
            for j in range(22):
                jj = g * 22 + j
                W.add(('w1', l, jj), wview(ffn_w1[l])[:, :, jj * 128:(jj + 1) * 128], KC, 128)
                W.add(('w3', l, jj), wview(ffn_w3[l])[:, :, jj * 128:(jj + 1) * 128], KC, 128)
                if l + 1 < nlayers and jj < 32:
                    for cb in range(3 * jj, 3 * jj + 3):
                        add_ada(l + 1, cb)
            for fc in range(KC):
                for hf in range(2):
                    r0 = (g * 22 + hf * 11) * 128
                    W.add(('w2', l, g, fc, hf),
                          ffn_w2[l][r0:r0 + 11 * 128, fc * 128:(fc + 1) * 128].rearrange("(j p) n -> p j n", p=128), 11, 128)

    P.dma('sp', consts[:], consts_d, writes=['consts'])
    P.dma('sp', cT[:], cT_d, writes=['cT'])
    P.dma('sp', n1w[:], n1w_d.rearrange("l p k -> p l k"), writes=['n1w'])
    P.dma('sp', n2w[:], n2w_d.rearrange("l p k -> p l k"), writes=['n2w'])
    P.dma('sp', fnw[:], fnw_d, writes=['fnw'])
    P.dma('sp', adab[:], adab_d.rearrange("l p k -> p l k"), writes=['adab'])
    P.I('dve', 'tensor_copy', ident_b[:], ident_f, reads=['consts'], writes=['ident_b'])
    P.I('dve', 'memset', ones_b[:], 1.0, writes=['ones_b'])
    P.I('act', 'activation', csil[:], cT[:], AF.Silu, reads=['cT'], writes=['csil'])

    def stats_add(tb, src, srckeys, c0, n, first, last):
        i = rr('sq', 2)
        sqt = sq[i]
        P.I('act', 'activation', sqt[:, 0:n], src, AF.Square, reads=srckeys, writes=[('sq', i)])
        bk = banks[5 + tb]
        P.I('pe', 'matmul', bk[:, c0:c0 + n], ones_b[:], sqt[:, 0:n], start=first, stop=last,
            reads=[('sq', i), 'ones_b'], writes=[('stat', tb)])

    def stats_fin(tb):
        bk = banks[5 + tb]
        r = rstd[:, tb, :]
        P.I('act', 'activation', r, bk[:], AF.Ln, bias=EPS, scale=1.0 / D, reads=[('stat', tb)], writes=[('rstd', tb)])
        P.I('act', 'activation', r, r, AF.Exp, scale=-0.5, reads=[('rstd', tb)], writes=[('rstd', tb)])

    for tt in range(NT):
        tb, c0 = tt // 4, (tt % 4) * 128
        i = tt % 2
        tk = tokbuf[i]
        P.dma('sp', tk[:], xin[tt * 128:(tt + 1) * 128, :], writes=[('tok', i)])
        for q in range(4):
            bk, bkey = bank()
            for a in range(4):
                kc = q * 4 + a
                P.I('pe', 'transpose', bk[:, a * 128:(a + 1) * 128], tk[:, kc * 128:(kc + 1) * 128], ident_f,
                    reads=[('tok', i), 'consts'], writes=[bkey])
            j = rr('xs', 3)
            xsj = xs[j]
            copy_op(xsj[:], bk[:], reads=[bkey], writes=[('xs', j)])
            for a in range(4):
                kc = q * 4 + a
                stats_add(tb, xsj[:, a * 128:(a + 1) * 128], [('xs', j)], c0, 128, kc == 0, kc == KC - 1)
            P.dma('sp', xT[q * 512:(q + 1) * 512, tt * 128:(tt + 1) * 128].rearrange("(a p) t -> p a t", p=128),
                  xsj[:].rearrange("p (a t) -> p a t", t=128), reads=[('xs', j)],
                  writes=[('xT', q * 4 + a, tb) for a in range(4)])
        if tt % 4 == 3:
            stats_fin(tb)

    MODBANK = (banks[4], ('bank', 4))

    def mods_blocks(l, cbs):
        bk, bkey = MODBANK
        for cb in cbs:
            wv, wkey = W.next(('ada', l, cb))
            for kc in range(KC):
                P.I('pe', 'matmul', bk[:, cb * 2:cb * 2 + 2], wv[:, kc, :], csil[:, kc, :], start=(kc == 0), stop=(kc == KC - 1),
                    reads=[wkey, 'csil'], writes=[bkey])

    def mods(l):
        bk, bkey = MODBANK
        if l == 0:
            mods_blocks(l, range(96))
        P.I('dve', 'tensor_tensor', mod[:], bk[:, 0:192].rearrange("p (m r) -> p m r", r=2),
            adab[:, l, :].unsqueeze(2).to_broadcast([128, 96, 2]), op=ALU.add, reads=[bkey, 'adab'], writes=['mod'])
        for sub, nw in ((0, n1w), (1, n2w)):
            m0 = (1 + 3 * sub) * 16
            P.I('dve', 'scalar_tensor_tensor', ws[:, sub, :, :], mod[:, m0:m0 + 16, :], 1.0,
                nw[:, l, :].unsqueeze(2).to_broadcast([128, KC, 2]), op0=ALU.add, op1=ALU.mult,
                reads=['mod', 'n1w', 'n2w'], writes=['ws'])

    def normalize(l, sub):
        for tb in range(3):
            r = TBMOD[tb]
            for fc in range(KC):
                j = rr('xs', 3)
                xsj = xs[j]
                P.dma('sp', xsj[:], xT[fc * 128:(fc + 1) * 128, tb * 512:(tb + 1) * 512], reads=[('xT', fc, tb)], writes=[('xs', j)])
                k = rr('xt', 2)
                xtk = xt[k]
                P.I('dve', 'tensor_tensor', xtk[:], xsj[:], rstd[:, tb, :], op=ALU.mult,
                    reads=[('xs', j), ('rstd', tb)], writes=[('xt', k)])
                sh = mod[:, (3 * sub) * 16 + fc, r:r + 1]
                sc = ws[:, sub, fc, r:r + 1]
                P.I('act', 'activation', hT[:, fc, tb * 512:(tb + 1) * 512], xtk[:], AF.Identity, bias=sh, scale=sc,
                    reads=[('xt', k), 'mod', 'ws'], writes=[('hT', fc, tb)])

    def residual_epilogue(bk, bkey, gate_m, fc, tb, do_stats):
        r = TBMOD[tb]
        j = rr('xs', 3)
        xsj = xs[j]
        P.dma('sp', xsj[:], xT[fc * 128:(fc + 1) * 128, tb * 512:(tb + 1) * 512], reads=[('xT', fc, tb)], writes=[('xs', j)])
        g = mod[:, gate_m * 16 + fc, r:r + 1]
        P.I('dve', 'scalar_tensor_tensor', xsj[:], bk[:], g, xsj[:], op0=ALU.mult, op1=ALU.add,
            reads=[bkey, ('xs', j), 'mod'], writes=[('xs', j)])
        P.dma('sp', xT[fc * 128:(fc + 1) * 128, tb * 512:(tb + 1) * 512], xsj[:], reads=[('xs', j)], writes=[('xT', fc, tb)])
        if do_stats:
            stats_add(tb, xsj[:], [('xs', j)], 0, 512, fc == 0, fc == KC - 1)
            if fc == KC - 1:
                stats_fin(tb)

    def out_projection(l):
        for fc in range(KC):
            wv, wkey = W.next(('out', l, fc))
            for tb in range(3):
                bk, bkey = bank()
                for kc in range(KC):
                    P.I('pe', 'matmul', bk[:], wv[:, kc, :], catT[:, kc, tb * 512:(tb + 1) * 512], start=(kc == 0), stop=(kc == KC - 1),
                        reads=[wkey, ('cat', kc)], writes=[bkey])
                residual_epilogue(bk, bkey, 2, fc, tb, True)

    def ffn(l):
        for g in range(2):
            for j in range(22):
                jj = g * 22 + j
                w1, k1 = W.next(('w1', l, jj))
                w3, k3 = W.next(('w3', l, jj))
                for tb in range(3):
                    ba, ka = bank()
                    bb, kb = bank()
                    for wv, wk, bk, bkey in ((w1, k1, ba, ka), (w3, k3, bb, kb)):
                        for kc in range(KC):
                            P.I('pe', 'matmul', bk[:], wv[:, kc, :], hT[:, kc, tb * 512:(tb + 1) * 512], start=(kc == 0), stop=(kc == KC - 1),
                                reads=[wk, ('hT', kc, tb)], writes=[bkey])
                    k = rr('xt', 2)
                    xtk = xt[k]
                    P.I('act', 'activation', xtk[:], ba[:], AF.Silu, reads=[ka], writes=[('xt', k)])
                    P.I('dve', 'tensor_tensor', big2[:, j, tb * 512:(tb + 1) * 512], xtk[:], bb[:], op=ALU.mult,
                        reads=[kb, ('xt', k)], writes=[('u', j, tb)])
                if l + 1 < nlayers and jj < 32:
                    mods_blocks(l + 1, range(3 * jj, 3 * jj + 3))
            for fc in range(KC):
                wa, ka = W.next(('w2', l, g, fc, 0))
                wb, kb = W.next(('w2', l, g, fc, 1))
                for tb in range(3):
                    bk, bkey = bank()
                    for j in range(22):
                        wv, wk = (wa, ka) if j < 11 else (wb, kb)
                        P.I('pe', 'matmul', bk[:], wv[:, j % 11, :], big2[:, j, tb * 512:(tb + 1) * 512], start=(j == 0), stop=(j == 21),
                            reads=[wk, ('u', j, tb)], writes=[bkey])
                    residual_epilogue(bk, bkey, 5, fc, tb, g == 1)

    def final_out():
        for tt in range(NT):
            tb, c0 = tt // 4, (tt % 4) * 128
            i = tt % 2
            tk = tokbuf[i]
            for q in range(4):
                j = rr('xs', 3)
                xsj = xs[j]
                P.dma('sp', xsj[:].rearrange("p (a t) -> p a t", t=128),
                      xT[q * 512:(q + 1) * 512, tt * 128:(tt + 1) * 128].rearrange("(a p) t -> p a t", p=128),
                      reads=[('xT', q * 4 + a, tb) for a in range(4)], writes=[('xs', j)])
                bk, bkey = bank()
                for a in range(4):
                    kc = q * 4 + a
                    P.I('dve', 'scalar_tensor_tensor', xsj[:, a * 128:(a + 1) * 128], xsj[:, a * 128:(a + 1) * 128], fnw[:, kc:kc + 1],
                        rstd[:, tb, c0:c0 + 128], op0=ALU.mult, op1=ALU.mult, reads=[('xs', j), 'fnw', ('rstd', tb)], writes=[('xs', j)])
                    P.I('pe', 'transpose', bk[:, a * 128:(a + 1) * 128], xsj[:, a * 128:(a + 1) * 128], ident_f,
                        reads=[('xs', j), 'consts'], writes=[bkey])
                P.I('act', 'copy', tk[:, q * 512:(q + 1) * 512], bk[:], reads=[bkey], writes=[('tok', i)])
            P.dma('sp', y_d[tt * 128:(tt + 1) * 128, :], tk[:], reads=[('tok', i)], writes=[('y', tt)])

    ctx = dict(P=P, nc=nc, W=W, hT=hT, catT=catT, big2=big2, banks=banks, bank=bank, consts=consts, ident_b=ident_b,
               ident_f=ident_f, ones_b=ones_b, din=din, dout=dout, rr=rr, st=st, evac_eng=evac_eng, copy_op=copy_op,
               arena=arena, sq=sq, opts=opts)
    mixer_setup(ctx, mixers)
    dbg = opts.get('debug')
    for l in range(nlayers):
        mods(l)
        if dbg and l == 0:
            P.dma('sp', dout("dbg_mod", [128, 96, 2]), mod[:], reads=['mod'], writes=['dbg_mod'])
        normalize(l, 0)
        mixer_units(ctx, l, mixers)
        out_projection(l)
        normalize(l, 1)
        if dbg and l == 0:
            dh = dout("dbg_h2", [128, KC, T], BF16)
            P.dma('sp', dh, hT[:], reads=[('hT', fc, tb) for fc in range(KC) for tb in range(3)], writes=['dbg_h2'])
        ffn(l)
        if dbg and l == 0:
            dx = dout("dbg_x", [D, T])
            P.dma('sp', dx, xT, reads=[('xT', fc, tb) for fc in range(KC) for tb in range(3)], writes=['dbg_x'])
    final_out()
    if 's5' in mixers:
        s5_finish(ctx)
    P.finish()
    return nc


C_IDENT, C_MF, C_MB, C_PERM, C_MFS, C_MBS, C_NEGCM = 0, 128, 256, 384, 512, 640, 768
NCONST = 1280
HD = 128
QSCALE = HD ** -0.5
OFF = dict(rq=0, rk=512, rv=1024, rg=1536, dq=2048, dk=2560, dv=3072, dg=3584, dab=4096, hq=4112, hf=4624, hi=5648,
           hg=6160, su=6672)


def make_consts():
    c = np.zeros((128, NCONST), np.float32)
    c[:, C_IDENT:C_IDENT + 128] = np.eye(128, dtype=np.float32)
    j = np.arange(128)[:, None]
    i = np.arange(128)[None, :]
    same = (j // 32) == (i // 32)
    c[:, C_MF:C_MF + 128] = (same & (j <= i))
    c[:, C_MB:C_MB + 128] = (same & (j >= i))
    c[:, C_PERM:C_PERM + 128] = (j == (i + 64) % 128)
    c[:, C_MFS:C_MFS + 128] = (same & (j < i))
    c[:, C_MBS:C_MBS + 128] = (same & (j > i))
    for cc in range(4):
        c[:, C_NEGCM + 128 * cc + 32 * cc:C_NEGCM + 128 * cc + 32 * cc + 32] = -1.0
    return c


def make_rope():
    l, gw = 1024, 64
    n_freq = HD // 4
    t_row = np.repeat(np.arange(l // gw, dtype=np.float32), gw)
    t_col = np.tile(np.arange(gw, dtype=np.float32), l // gw)
    inv = (10000.0 ** (-np.arange(n_freq, dtype=np.float32) / n_freq)).astype(np.float32)
    ang = np.concatenate([t_row[:, None] * inv, t_col[:, None] * inv], axis=-1).astype(np.float32)
    cos, sin = np.cos(ang).T, np.sin(ang).T
    C = np.concatenate([cos, cos], 0)
    S = np.concatenate([-sin, sin], 0)
    return np.ascontiguousarray(C, np.float32), np.ascontiguousarray(S, np.float32)


def make_rm():
    rm = np.ones((128, T), np.float32)
    rm[:, 0::32] = 0.0
    return rm


GDN_HEADS = [4]


def unit_cols(mixers):
    cols = []
    for h in range(4):
        if 'ret' in mixers:
            for nm in ('rq', 'rk', 'rv', 'rg'):
                cols.append(((nm, h), OFF[nm] + 128 * h, 128))
    if 'gdn' in mixers:
        cols.append((('dab', 0), OFF['dab'], 16))
        for h in range(GDN_HEADS[0]):
            for nm in ('dq', 'dk', 'dv', 'dg'):
                cols.append(((nm, h), OFF[nm] + 128 * h, 128))
    for h in range(4):
        if 'hg' in mixers:
            cols.append((('hq', h), OFF['hq'] + 128 * h, 128))
            cols.append((('hi', h), OFF['hi'] + 128 * h, 128))
            cols.append((('hg', h), OFF['hg'] + 128 * h, 128))
            cols.append((('hf0', h), OFF['hf'] + 128 * h, 128))
            cols.append((('hf1', h), OFF['hf'] + 512 + 128 * h, 128))
    if 's5' in mixers:
        for cc in range(4):
            cols.append((('su', cc), OFF['su'] + 128 * cc, 128))
    return cols


def mixer_setup(ctx, mixers):
    P, din, dout = ctx['P'], ctx['din'], ctx['dout']
    rm_d = din("rm", [128, T])
    ctx['rope_c'] = din("rope_c", [128, 1024])
    ctx['rope_s'] = din("rope_s", [128, 1024])
    rmb = P.sb("rmb", [128, T], BF16)
    ctx['rmb'] = rmb
    slot0 = ctx['arena'][:, 12288:12288 + T]
    P.dma('sp', slot0, rm_d, writes=[('slot', 0)])
    P.I('dve', 'tensor_copy', rmb[:], slot0, reads=[('slot', 0)], writes=['rmb'])
    rdl_d = din("rdl", [128, 16])
    lg = P.sb("ret_lg", [128, 16], F32)
    ctx['ret_lg'] = lg
    P.dma('sp', lg[:], rdl_d, writes=['ret_lg'])
    P.I('act', 'activation', lg[:], lg[:], AF.Exp, scale=-1.0, reads=['ret_lg'], writes=['ret_lg'])
    P.I('act', 'activation', lg[:], lg[:], AF.Ln, bias=1.0, reads=['ret_lg'], writes=['ret_lg'])
    P.I('dve', 'tensor_scalar', lg[:], lg[:], -1.0, None, op0=ALU.mult, reads=['ret_lg'], writes=['ret_lg'])
    hlb_d = din("hlb", [128, 2, 8])
    hp = P.sb("hg_p", [128, 2, 8], F32)
    oml = P.sb("hg_oml", [128, 2, 8], F32)
    noml = P.sb("hg_noml", [128, 2, 8], F32)
    lbf = P.sb("hg_lbf", [128, 2, 8], F32)
    ctx.update(hg_oml=oml, hg_noml=noml, hg_lbf=lbf)
    P.dma('sp', hp[:], hlb_d, writes=['hg_p'])
    P.I('dve', 'tensor_tensor', hp[:, 1, :], hp[:, 1, :], hp[:, 0, :], op=ALU.subtract, reads=['hg_p'], writes=['hg_p'])
    P.I('act', 'activation', hp[:, 1, :], hp[:, 1, :], AF.Sigmoid, reads=['hg_p'], writes=['hg_p'])
    P.I('dve', 'memset', hp[:, 0, :], 0.0, reads=['hg_p'], writes=['hg_p'])
    P.I('dve', 'tensor_scalar', oml[:], hp[:], -1.0, 1.0, op0=ALU.mult, op1=ALU.add, reads=['hg_p'], writes=['hg_oml'])
    P.I('dve', 'tensor_scalar', noml[:], hp[:], 1.0, -1.0, op0=ALU.mult, op1=ALU.add, reads=['hg_p'], writes=['hg_noml'])
    P.I('dve', 'tensor_scalar', lbf[:], hp[:], 1e-30, None, op0=ALU.max, reads=['hg_p'], writes=['hg_lbf'])
    hnw = P.sb("hg_nw", [128, 2], F32)
    ctx['hg_nw'] = hnw
    P.dma('sp', hnw[:], din("hgnw", [128, 2]), writes=['hg_nw'])
    gp = P.sb("gdn_par", [128, 2, 2, 8], F32)
    ctx['gdn_par'] = gp
    P.dma('sp', gp[:], din("gdnp", [128, 2, 2, 8]), writes=['gdn_par'])
    P.I('act', 'activation', gp[:, :, 1, :], gp[:, :, 1, :], AF.Exp, reads=['gdn_par'], writes=['gdn_par'])
    P.I('dve', 'tensor_scalar', gp[:, :, 1, :], gp[:, :, 1, :], -1.0, None, op0=ALU.mult, reads=['gdn_par'], writes=['gdn_par'])
    gcw = P.sb("gdn_cw", [128, 2, 3, 12], F32)
    ctx['gdn_cw'] = gcw
    P.dma('sp', gcw[:], din("gdncw", [128, 2, 3, 12]), writes=['gdn_cw'])
    gnw = P.sb("gdn_nw", [128, 2], F32)
    ctx['gdn_nw'] = gnw
    P.dma('sp', gnw[:], din("gdnnw", [128, 2]), writes=['gdn_nw'])
    ones_f = P.sb("ones_f", [128, 128], F32)
    ctx['ones_f'] = ones_f
    P.I('dve', 'memset', ones_f[:], 1.0, writes=['ones_f'])
    ctx['st_gdn'] = din("st_gdn", [2, 2, 4, 128, 128])
    ctx['new_gdn'] = dout("new_gdn", [2, 2, 2, 4, 128, 128])
    ctx['s5p_d'] = din("s5p", [128, 2, 3, 32])
    ctx['s5x0_d'] = din("s5x0", [128, 2, 2, 32])
    ctx['s5bx'] = din("s5bx", [2, 2, 2, 16, 128, 128])
    ctx['s5cx'] = din("s5cx", [2, 2, 2, 16, 128, 128])
    s5d = P.sb("s5_d", [128, 2, 2, 4], F32)
    ctx['s5_d'] = s5d
    P.dma('sp', s5d[:], din("s5dg", [128, 2, 2, 4]), writes=['s5_d'])
    tau = P.sb("s5_tau", [128, 65], F32)
    ctx['s5_tau'] = tau
    P.dma('sp', tau[:], din("s5tau", [128, 65]), writes=['s5_tau'])
    ctx['s5_fs'] = [P.sb("s5_fs%d" % i, [128, 128], F32) for i in range(2)]
    for i in range(2):
        P.I('dve', 'memset', ctx['s5_fs'][i][:], 0.0, writes=[('s5_fs', i)])
    ctx['new_s5'] = [dout("new_s5re", [128, 128]), dout("new_s5im", [128, 128])]
    ctx['st_ret'] = din("st_ret", [2, 2, 4, 128, 128])
    ctx['st_hg'] = din("st_hg", [2, 2, 4, 128, 128])
    ctx['new_ret'] = dout("new_ret", [2, 2, 2, 4, 128, 128])
    ctx['new_hg'] = dout("new_hg", [2, 2, 2, 4, 128, 128])


def persist(ctx, name, shape, dt):
    if name not in ctx:
        ctx[name] = ctx['P'].sb(name, shape, dt)
    return ctx[name]


def mixer_units(ctx, l, mixers):
    P, catT, st = ctx['P'], ctx['catT'], ctx['st']
    P.barrier()
    st['nbanks'] = 8
    for name, kc0 in (('ret', 0), ('gdn', 4), ('hg', 8), ('s5', 12)):
        if name not in mixers:
            for kc in range(kc0, kc0 + 4):
                P.I('dve', 'memset', catT[:, kc, :], 0.0, writes=[('cat', kc)])
    if 'ret' in mixers:
        for h in range(4):
            gla_unit(ctx, l, 'ret', h)
    if 'gdn' in mixers:
        P.barrier()
        gdn_gates(ctx, l)
        for h in range(4):
            gdn_unit(ctx, l, h)
            if ctx['opts'].get('gdn_heads', 4) <= h + 1:
                break
        P.barrier()
    if 'hg' in mixers:
        for h in range(4):
            gla_unit(ctx, l, 'hg', h)
    if 's5' in mixers:
        P.barrier()
        st['nbanks'] = 5
        st['bank'] = 0
        s5_layer(ctx, l)
    P.barrier()
    st['nbanks'] = 4
    st['bank'] = 0


def gla_unit(ctx, l, kind, h):
    P, W, hT, catT, arena, bank = ctx['P'], ctx['W'], ctx['hT'], ctx['catT'], ctx['arena'], ctx['bank']
    consts, ident_b, ones_b, rmb, sq, rr, copy_op = ctx['consts'], ctx['ident_b'], ctx['ones_b'], ctx['rmb'], ctx['sq'], ctx['rr'], ctx['copy_op']

    def SF(i):
        return arena[:, 12288 + T * i:12288 + T * (i + 1)], ('slot', i)

    def SB(i, half):
        return arena[:, 12288 + T * i:12288 + T * (i + 1)].bitcast(BF16)[:, half * T:(half + 1) * T], ('slot', i)
    small = arena[:, SMALL0:SMALL0 + 1280]
    QS, kQS = SF(0)
    KIN, kKIN = SF(1)
    LOGF, kLOGF = SF(2)
    OACC, kOACC = SF(2)
    CUM, kCUM = SF(3)
    TMP, kTMP = SF(4)
    SG1, kSG1 = SF(5)
    QT = [SB(6, 0), SB(7, 0)]
    KT = [SB(6, 1), SB(7, 1)]
    VTOK, kVTOK = SB(8, 0)
    GS, kGS = SB(8, 1)
    KTOK = [SB(9, 0), SB(9, 1)]
    Sst = [small[:, 128 * i:128 * (i + 1)] for i in range(6)]
    Sbf = [small[:, 768 + 64 * i:768 + 64 * (i + 1)].bitcast(BF16) for i in range(6)]
    TOT = [persist(ctx, 'gla_tot%d' % d, [128, 48], F32) for d in range(2)]
    CD = [persist(ctx, 'gla_cd%d' % d, [128, 48], F32) for d in range(2)]
    PTR = [persist(ctx, 'gla_pt%d' % i, [128, 128], BF16) for i in range(6)]
    MASK = [consts[:, C_MF:C_MF + 128], consts[:, C_MB:C_MB + 128]]
    kc_out = (0 if kind == 'ret' else 8) + h
    sfx = 'r' if kind == 'ret' else 'h'

    def proj(name, evac):
        wv, wkey = W.next(('in', l, (name, h)))
        for tb in range(3):
            bk, bkey = bank()
            for kc in range(KC):
                P.I('pe', 'matmul', bk[:], wv[:, kc, :], hT[:, kc, tb * 512:(tb + 1) * 512], start=(kc == 0), stop=(kc == KC - 1),
                    reads=[wkey, ('hT', kc, tb)], writes=[bkey])
            evac(tb, bk, bkey)

    def tbs(ap, tb):
        return ap[:, tb * 512:(tb + 1) * 512]

    def to_tok(srcb, ksrc, dst, kdst):
        for g in range(3):
            bk, bkey = bank()
            bkb = bk[:].bitcast(BF16)
            for a in range(4):
                tt = g * 4 + a
                P.I('pe', 'transpose', bkb[:, a * 128:(a + 1) * 128], srcb[:, tt * 128:(tt + 1) * 128], ident_b[:],
                    reads=[ksrc, 'ident_b'], writes=[bkey])
            copy_op(dst[:, g * 512:(g + 1) * 512], bkb[:, 0:512], reads=[bkey], writes=[kdst])

    if kind == 'ret':
        def rope_evac(dst, kdst, scale):
            def f(tb, bk, bkey):
                P.I('act', 'activation', tbs(dst, tb), bk[:], AF.Identity, scale=scale, reads=[bkey], writes=[kdst])
                if tb == 0:
                    return
                c0 = (tb - 1) * 512
                b2, k2 = bank()
                P.I('pe', 'matmul', b2[:], consts[:, C_PERM:C_PERM + 128], tbs(dst, tb), start=True, stop=True,
                    reads=[kdst, 'consts'], writes=[k2])
                P.I('dve', 'tensor_tensor', tbs(TMP, tb), tbs(dst, tb), RC[:, c0:c0 + 512], op=ALU.mult, reads=[kdst, kRC], writes=[kTMP])
                P.I('dve', 'tensor_tensor', tbs(dst, tb), b2[:], RS[:, c0:c0 + 512], op=ALU.mult, reads=[k2, kRS], writes=[kdst])
                P.I('dve', 'tensor_tensor', tbs(dst, tb), tbs(dst, tb), tbs(TMP, tb), op=ALU.add, reads=[kdst, kTMP], writes=[kdst])
            return f
        RC, kRC = LOGF[:, 0:1024], kLOGF
        RS, kRS = CUM[:, 0:1024], kCUM
        P.dma('sp', RC, ctx['rope_c'], writes=[kRC])
        P.dma('sp', RS, ctx['rope_s'], writes=[kRS])
        proj('rq', rope_evac(QS, kQS, 1.0))
        proj('rk', rope_evac(KIN, kKIN, QSCALE))
        qscale = 1.0
        vname, gname = 'rv', 'rg'
    else:
        proj('hq', lambda tb, bk, bkey: P.I('act', 'activation', tbs(QS, tb), bk[:], AF.Silu, reads=[bkey], writes=[kQS]))
        qscale = QSCALE
        vname, gname = 'hi', 'hg'
    VTB, kVTB = TMP.bitcast(BF16)[:, 0:T], kTMP
    proj(vname, lambda tb, bk, bkey: copy_op(tbs(VTB, tb), bk[:], reads=[bkey], writes=[kVTB]))
    to_tok(VTB, kVTB, VTOK, kVTOK)
    proj(gname, lambda tb, bk, bkey: P.I('act', 'activation', tbs(GS, tb), bk[:], AF.Silu, reads=[bkey], writes=[kGS]))

    for d in range(2):
        pidx = l * 8 + d * 4 + h
        if kind == 'ret':
            lgap = ctx['ret_lg'][:, pidx:pidx + 1]
            P.I('act', 'activation', LOGF, rmb[:], AF.Identity, scale=0.0, bias=lgap, reads=['rmb', 'ret_lg'], writes=[kLOGF])
            kin, kkin = KIN, kKIN
        else:
            sg, ksg = SG1, kSG1
            proj('hf%d' % d, lambda tb, bk, bkey: P.I('act', 'activation', tbs(SG1, tb), bk[:], AF.Sigmoid, reads=[bkey], writes=[kSG1]))
            oml = ctx['hg_oml'][:, l, d * 4 + h:d * 4 + h + 1]
            noml = ctx['hg_noml'][:, l, d * 4 + h:d * 4 + h + 1]
            lbf = ctx['hg_lbf'][:, l, d * 4 + h:d * 4 + h + 1]
            P.I('dve', 'tensor_scalar', LOGF, sg, oml, lbf, op0=ALU.mult, op1=ALU.add, reads=[ksg, 'hg_oml', 'hg_lbf'], writes=[kLOGF])
            P.I('act', 'activation', LOGF, LOGF, AF.Ln, reads=[kLOGF], writes=[kLOGF])
            P.I('dve', 'tensor_scalar', KIN, sg, noml, oml, op0=ALU.mult, op1=ALU.add, reads=[ksg, 'hg_oml', 'hg_noml'], writes=[kKIN])
            kin, kkin = KIN, kKIN
        P.I('dve', 'tensor_tensor_scan', CUM, rmb[:], LOGF, 0.0, op0=ALU.mult, op1=ALU.add, reads=['rmb', kLOGF], writes=[kCUM])
        c3 = CUM.rearrange("p (c k) -> p c k", k=32)
        kt = ('gla_tot', d)
        P.I('dve', 'tensor_copy', TOT[d][:], CUM[:, 31::32], reads=[kCUM], writes=[kt])
        totb = TOT[d][:].unsqueeze(2).to_broadcast([128, 48, 32])
        if d == 1:
            P.I('dve', 'tensor_tensor', TMP, LOGF, CUM, op=ALU.subtract, reads=[kLOGF, kCUM], writes=[kTMP])
            P.I('dve', 'tensor_tensor', c3, TMP.rearrange("p (c k) -> p c k", k=32), totb, op=ALU.add, reads=[kTMP, kt], writes=[kCUM])
        P.I('act', 'activation', TMP, CUM, AF.Exp, reads=[kCUM], writes=[kTMP])
        P.I('dve', 'scalar_tensor_tensor', QT[d][0], QS, qscale, TMP, op0=ALU.mult, op1=ALU.mult, reads=[kQS, kTMP], writes=[QT[d][1]])
        P.I('act', 'activation', TMP, CUM, AF.Exp, scale=-1.0, reads=[kCUM, QT[d][1]], writes=[kTMP])
        P.I('dve', 'tensor_tensor', KT[d][0], kin, TMP, op=ALU.mult, reads=[kkin, kTMP], writes=[KT[d][1]])
        P.I('dve', 'tensor_tensor', TMP.rearrange("p (c k) -> p c k", k=32), totb, c3, op=ALU.subtract, reads=[kCUM, kt, KT[d][1]], writes=[kTMP])
        P.I('act', 'activation', TMP, TMP, AF.Exp, reads=[kTMP], writes=[kTMP])
        KTLb, kKTLb = LOGF.bitcast(BF16)[:, 0:T], kLOGF
        P.I('dve', 'tensor_tensor', KTLb, kin, TMP, op=ALU.mult, reads=[kkin, kTMP, kLOGF], writes=[kKTLb])
        P.I('act', 'activation', CD[d][:], TOT[d][:], AF.Exp, reads=[kt], writes=[('gla_cd', d)])
        to_tok(KTLb, kKTLb, KTOK[d][0], KTOK[d][1])

    VTOK3, kVTOK3 = SB(5, 0)
    P.I('dve', 'tensor_scalar', VTOK3, VTOK, consts[:, C_MF + 127:C_MF + 128], None, op0=ALU.mult,
        reads=[kVTOK, 'consts'], writes=[kVTOK3])
    P.I('dve', 'memset', OACC[:, 0:2], 0.0, reads=[], writes=[kOACC])
    st_in = ctx['st_ret'] if kind == 'ret' else ctx['st_hg']
    st_out = ctx['new_ret'] if kind == 'ret' else ctx['new_hg']
    written = set()

    def chain(d, si):
        s0, sl, row = SEQS[si]
        ci = d * 3 + si
        S, kS = Sst[ci], ('gla_S', ci)
        Sb, kSb = Sbf[ci], ('gla_Sb', ci)
        if row == 0:
            P.I('dve', 'memset', S, 0.0, writes=[kS])
        else:
            P.dma('sp', S, st_in[l, d, h], writes=[kS])
        copy_op(Sb, S, reads=[kS], writes=[kSb], eng='act')
        tiles = list(range(s0 // 128, (s0 + sl) // 128))
        if d == 1:
            tiles = tiles[::-1]
        for tt in tiles:
            t0 = tt * 128
            scb, ksc = ctx['banks'][ci // 4][:, (ci % 4) * 128:(ci % 4) * 128 + 128], ('bank', ci // 4)
            P.I('pe', 'matmul', scb[:, 0:128], KT[d][0][:, t0:t0 + 128], QT[d][0][:, t0:t0 + 128], start=True, stop=True,
                reads=[KT[d][1], QT[d][1]], writes=[ksc])
            PT, kPT = PTR[ci], ('gla_pt', ci)
            yield
            P.I('dve', 'tensor_tensor', PT[:], scb[:, 0:128], MASK[d], op=ALU.mult, reads=[ksc, 'consts'], writes=[kPT])
            yield
            ob, kob = ctx['banks'][2 + ci], ('bank', 2 + ci)
            P.I('pe', 'matmul', ob[:, 0:128], VTOK[:, t0:t0 + 128], PT[:], start=True, stop=False, reads=[kVTOK, kPT], writes=[kob])
            chunks = [0, 1, 2, 3] if d == 0 else [3, 2, 1, 0]
            for n, c in enumerate(chunks):
                gc = tt * 4 + c
                P.I('pe', 'matmul', ob[:, 32 * c:32 * c + 32], Sb, QT[d][0][:, t0 + 32 * c:t0 + 32 * c + 32], start=False, stop=(n == 3),
                    reads=[kSb, QT[d][1]], writes=[kob])
                kvb, kkv = scb, ksc
                if c < 3:
                    P.I('pe', 'matmul', kvb[:, 0:128], KTOK[d][0][32 * c:32 * c + 32, t0:t0 + 128], VTOK[32 * c:32 * c + 32, t0:t0 + 128],
                        start=True, stop=True, reads=[KTOK[d][1], kVTOK], writes=[kkv])
                else:
                    P.I('pe', 'matmul', kvb[:, 0:128], KTOK[d][0][64:128, t0:t0 + 128], VTOK3[64:128, t0:t0 + 128],
                        start=True, stop=True, reads=[KTOK[d][1], kVTOK3], writes=[kkv])
                yield
                P.I('dve', 'scalar_tensor_tensor', S, S, CD[d][:, gc:gc + 1], kvb[:, 0:128], op0=ALU.mult, op1=ALU.add,
                    reads=[kS, kkv, ('gla_cd', d)], writes=[kS])
                yield
                copy_op(Sb, S, reads=[kS], writes=[kSb], eng='act')
                yield
            if tt not in written:
                written.add(tt)
                copy_op(OACC[:, t0:t0 + 128], ob[:, 0:128], reads=[kob, kOACC], writes=[(kOACC[0], kOACC[1], tt)], eng='act')
            else:
                P.I('dve', 'tensor_tensor', OACC[:, t0:t0 + 128], OACC[:, t0:t0 + 128], ob[:, 0:128], op=ALU.add,
                    reads=[kob, kOACC, (kOACC[0], kOACC[1], tt)], writes=[(kOACC[0], kOACC[1], tt)])
            yield
        if row == 0:
            P.dma('sp', st_out[si, l, d, h], S, reads=[kS], writes=[('st_out', kind, si, l, d, h)])

    gens = [chain(d, si) for si in range(3) for d in range(2)]
    ctx['st']['nbanks'], ctx['st']['bank'] = 2, 0
    while gens:
        for g in list(gens):
            try:
                next(g)
            except StopIteration:
                gens.remove(g)
    ctx['st']['nbanks'] = 8

    for tb in range(3):
        okeys = [(kOACC[0], kOACC[1], tt) for tt in range(tb * 4, tb * 4 + 4)]
        okr = okeys + [kOACC]
        o = tbs(OACC, tb)
        if kind == 'ret':
            ob16 = tbs(TMP.bitcast(BF16)[:, 0:T], tb)
            copy_op(ob16, o, reads=okr, writes=[kTMP], eng='dve')
            bm, kbm = bank()
            P.I('pe', 'matmul', bm[:], ones_b[:], ob16, start=True, stop=True, reads=[kTMP, 'ones_b'], writes=[kbm])
            P.I('dve', 'scalar_tensor_tensor', o, bm[:], -1.0 / HD, o, op0=ALU.mult, op1=ALU.add, reads=[kbm] + okr, writes=okeys)
        i = rr('sq', 2)
        P.I('act', 'activation', sq[i][:, 0:512], o, AF.Square, reads=okr, writes=[('sq', i)])
        bs, kbs = bank()
        P.I('pe', 'matmul', bs[:], ones_b[:], sq[i][:, 0:512], start=True, stop=True, reads=[('sq', i), 'ones_b'], writes=[kbs])
        rt = tbs(CUM, tb)
        P.I('act', 'activation', rt, bs[:], AF.Ln, bias=EPS, scale=1.0 / HD, reads=[kbs], writes=[kCUM])
        P.I('act', 'activation', rt, rt, AF.Exp, scale=-0.5, reads=[kCUM], writes=[kCUM])
        if kind == 'ret':
            P.I('dve', 'tensor_tensor', o, o, rt, op=ALU.mult, reads=okr + [kCUM], writes=okeys)
        else:
            P.I('dve', 'scalar_tensor_tensor', o, o, ctx['hg_nw'][:, l:l + 1], rt, op0=ALU.mult, op1=ALU.mult,
                reads=okr + [kCUM, 'hg_nw'], writes=okeys)
        P.I('dve', 'tensor_tensor', catT[:, kc_out, tb * 512:(tb + 1) * 512], o, tbs(GS, tb), op=ALU.mult,
            reads=okr + [kGS], writes=[('cat', kc_out)])


def gdn_gates(ctx, l):
    P, W, hT, arena, bank, consts, ident_f = ctx['P'], ctx['W'], ctx['hT'], ctx['arena'], ctx['bank'], ctx['consts'], ctx['ident_f']
    small = arena[:, SMALL0:SMALL0 + 1280]
    names = ('LA', 'BETA', 'EG', 'EGL', 'BEG', 'EGL3')
    G = {n: small[:, 96 * i:96 * (i + 1)].rearrange("p (t c) -> p t c", c=8) for i, n in enumerate(names)}
    ctx['gdn_g'] = G
    ABT = arena[0:16, 12288:12288 + T]
    kABT = ('slot', 0)
    ABK = arena[:, 12288 + T:12288 + T + 192].rearrange("p (t c) -> p t c", c=16)
    kABK = ('slot', 1)
    wv, wkey = W.next(('in', l, ('dab', 0)))
    for tb in range(3):
        bk, bkey = bank()
        for kc in range(KC):
            P.I('pe', 'matmul', bk[0:16, :], wv[:, kc, :], hT[:, kc, tb * 512:(tb + 1) * 512], start=(kc == 0), stop=(kc == KC - 1),
                reads=[wkey, ('hT', kc, tb)], writes=[bkey])
        P.I('act', 'copy', ABT[:, tb * 512:(tb + 1) * 512], bk[0:16, :], reads=[bkey], writes=[kABT])
    bk, bkey = bank()
    for tt in range(NT):
        P.I('pe', 'transpose', bk[:, tt * 16:(tt + 1) * 16], ABT[:, tt * 128:(tt + 1) * 128], ident_f[0:16, 0:16],
            reads=[kABT, 'consts'], writes=[bkey])
    P.I('dve', 'tensor_copy', ABK, bk[:, 0:192].rearrange("p (t c) -> p t c", c=16), reads=[bkey], writes=[kABK])
    gp = ctx['gdn_par']
    LA, BETA = G['LA'], G['BETA']
    dtb = gp[:, l, 0, :].unsqueeze(1).to_broadcast([128, NT, 8])
    nea = gp[:, l, 1, :].unsqueeze(1).to_broadcast([128, NT, 8])
    kg = 'gdn_gate'
    P.I('dve', 'tensor_tensor', LA, ABK[:, :, 0:8], dtb, op=ALU.add, reads=[kABK, 'gdn_par'], writes=[kg])
    P.I('act', 'activation', LA, LA, AF.Exp, reads=[kg], writes=[kg])
    P.I('act', 'activation', LA, LA, AF.Ln, bias=1.0, reads=[kg], writes=[kg])
    P.I('dve', 'tensor_tensor', LA, LA, nea, op=ALU.mult, reads=[kg, 'gdn_par'], writes=[kg])
    P.I('act', 'activation', BETA, ABK[:, :, 8:16], AF.Sigmoid, reads=[kABK], writes=[kg])
    b1, k1 = bank()
    U = [consts[:, C_MF:C_MF + 128], consts[:, C_MB:C_MB + 128]]
    LS = [consts[:, C_MBS:C_MBS + 128], consts[:, C_MFS:C_MFS + 128]]
    for tt in range(NT):
        for d in range(2):
            P.I('pe', 'matmul', b1[:, tt * 8 + d * 4:tt * 8 + d * 4 + 4], U[d], LA[:, tt, d * 4:d * 4 + 4], start=True, stop=True,
                reads=[kg, 'consts'], writes=[k1])
            P.I('pe', 'matmul', b1[:, 96 + tt * 8 + d * 4:96 + tt * 8 + d * 4 + 4], LS[d], LA[:, tt, d * 4:d * 4 + 4], start=True, stop=True,
                reads=[kg, 'consts'], writes=[k1])
    P.I('act', 'activation', G['EG'], b1[:, 0:96].rearrange("p (t c) -> p t c", c=8), AF.Exp, reads=[k1], writes=[kg])
    P.I('act', 'activation', G['EGL'], b1[:, 96:192].rearrange("p (t c) -> p t c", c=8), AF.Exp, reads=[k1], writes=[kg])
    P.I('dve', 'tensor_tensor', G['BEG'], G['EG'], BETA, op=ALU.mult, reads=[kg], writes=[kg])
    P.I('dve', 'tensor_scalar', G['EGL3'], G['EGL'], consts[:, C_MF + 127:C_MF + 128], None, op0=ALU.mult, reads=[kg, 'consts'], writes=[kg])


def gdn_unit(ctx, l, h):
    P, W, hT, catT, arena, bank = ctx['P'], ctx['W'], ctx['hT'], ctx['catT'], ctx['arena'], ctx['bank']
    consts, ident_b, ones_b, ones_f, sq, rr, copy_op = ctx['consts'], ctx['ident_b'], ctx['ones_b'], ctx['ones_f'], ctx['sq'], ctx['rr'], ctx['copy_op']
    ident_f = ctx['ident_f']
    G = ctx['gdn_g']
    kg = 'gdn_gate'

    def SF(i):
        return arena[:, 12288 + T * i:12288 + T * (i + 1)], ('slot', i)

    def SB(i, half):
        return arena[:, 12288 + T * i:12288 + T * (i + 1)].bitcast(BF16)[:, half * T:(half + 1) * T], ('slot', i)
    X, kX = SF(0)
    Y, kY = SF(1)
    OACC, kOACC = SF(2)
    QT, kQT = SB(3, 0)
    KT, kKT = SB(3, 1)
    VTOK, kVTOK = SB(4, 0)
    KTOK, kKTOK = SB(4, 1)
    GS, kGS = SB(5, 0)
    VTB, kVTB = SB(5, 1)
    BV, kBV = SB(6, 0)
    KBG, kKBG = SB(6, 1)
    KTL, kKTL = SB(7, 0)
    KTL3, kKTL3 = SB(7, 1)
    mat = arena[:, 12288 + 8 * T:12288 + 10 * T]
    smallr = arena[:, SMALL0:SMALL0 + 1280]
    cnt = {'n': 0}

    def MF32(name):
        o = cnt['n']
        cnt['n'] += 128
        return mat[:, o:o + 128], ('gm', name)

    def MB16(name, n=128):
        o = cnt['n']
        cnt['n'] += n // 2
        return mat[:, o:o + n // 2].bitcast(BF16), ('gm', name)
    cw = ctx['gdn_cw']
    MINC = [consts[:, C_MF:C_MF + 128], consts[:, C_MB:C_MB + 128]]
    MSTR = [consts[:, C_MFS:C_MFS + 128], consts[:, C_MBS:C_MBS + 128]]
    U = MINC
    LS = [consts[:, C_MBS:C_MBS + 128], consts[:, C_MFS:C_MFS + 128]]

    def tbs(ap, tb):
        return ap[:, tb * 512:(tb + 1) * 512]

    def proj(name, evac):
        wv, wkey = W.next(('in', l, (name, h)))
        for tb in range(3):
            bk, bkey = bank()
            for kc in range(KC):
                P.I('pe', 'matmul', bk[:], wv[:, kc, :], hT[:, kc, tb * 512:(tb + 1) * 512], start=(kc == 0), stop=(kc == KC - 1),
                    reads=[wkey, ('hT', kc, tb)], writes=[bkey])
            evac(tb, bk, bkey)

    def to_tok(srcb, ksrc, dst, kdst):
        for g in range(3):
            bk, bkey = bank()
            bkb = bk[:].bitcast(BF16)
            for a in range(4):
                tt = g * 4 + a
                P.I('pe', 'transpose', bkb[:, a * 128:(a + 1) * 128], srcb[:, tt * 128:(tt + 1) * 128], ident_b[:],
                    reads=[ksrc, 'ident_b'], writes=[bkey])
            copy_op(dst[:, g * 512:(g + 1) * 512], bkb[:, 0:512], reads=[bkey], writes=[kdst])

    def conv_silu(which):
        ci = which * 4 + h
        w0, w1, w2 = (cw[:, l, t, ci:ci + 1] for t in range(3))
        P.I('act', 'activation', Y, X, AF.Identity, scale=w1, reads=[kX, 'gdn_cw'], writes=[kY])
        for s0, sl, _ in SEQS:
            a, b = s0, s0 + sl
            P.I('dve', 'scalar_tensor_tensor', Y[:, a + 1:b], X[:, a:b - 1], w0, Y[:, a + 1:b], op0=ALU.mult, op1=ALU.add,
                reads=[kX, kY, 'gdn_cw'], writes=[kY])
            P.I('dve', 'scalar_tensor_tensor', Y[:, a:b - 1], X[:, a + 1:b], w2, Y[:, a:b - 1], op0=ALU.mult, op1=ALU.add,
                reads=[kX, kY, 'gdn_cw'], writes=[kY])
        P.I('act', 'activation', Y, Y, AF.Silu, reads=[kY], writes=[kY])

    def l2n(dst, kdst, scale):
        for tb in range(3):
            i = rr('sq', 2)
            P.I('act', 'activation', sq[i][:, 0:512], tbs(Y, tb), AF.Square, reads=[kY], writes=[('sq', i)])
            bs, kbs = bank()
            P.I('pe', 'matmul', bs[:], ones_b[:], sq[i][:, 0:512], start=True, stop=True, reads=[('sq', i), 'ones_b'], writes=[kbs])
            P.I('act', 'activation', tbs(X, tb), bs[:], AF.Ln, bias=EPS, reads=[kbs], writes=[kX])
            P.I('act', 'activation', tbs(X, tb), tbs(X, tb), AF.Exp, scale=-0.5, reads=[kX], writes=[kX])
            P.I('dve', 'scalar_tensor_tensor', tbs(dst, tb), tbs(Y, tb), scale, tbs(X, tb), op0=ALU.mult, op1=ALU.mult,
                reads=[kY, kX], writes=[kdst])

    xevac = lambda tb, bk, bkey: copy_op(tbs(X, tb), bk[:], reads=[bkey], writes=[kX])
    P.barrier()
    if ctx['opts'].get('gdn_stop', 99) <= 1:
        for nm in ('dq', 'dk', 'dv', 'dg'):
            W.next(('in', l, (nm, h)))
        return
    proj('dq', xevac)
    conv_silu(0)
    l2n(QT, kQT, QSCALE)
    proj('dk', xevac)
    conv_silu(1)
    l2n(KT, kKT, 1.0)
    proj('dv', xevac)
    conv_silu(2)
    P.I('dve', 'tensor_copy', VTB, Y, reads=[kY], writes=[kVTB])
    to_tok(VTB, kVTB, VTOK, kVTOK)
    to_tok(KT, kKT, KTOK, kKTOK)
    proj('dg', lambda tb, bk, bkey: P.I('act', 'activation', tbs(GS, tb), bk[:], AF.Silu, reads=[bkey], writes=[kGS]))

    STOP = ctx['opts'].get('gdn_stop', 99)
    if STOP <= 2:
        return
    st_in, st_out = ctx['st_gdn'], ctx['new_gdn']
    written = set()
    P.I('dve', 'memset', OACC[:, 0:2], 0.0, writes=[kOACC])
    v3 = lambda ap: ap.rearrange("p (t e) -> p t e", e=128)

    P.barrier()
    regA = arena[:, 12288:12288 + 2 * T]
    regB = arena[:, 12288 + 8 * T:12288 + 10 * T]

    def carve(reg, off, n):
        return reg[:, off:off + n]

    def mkset(i):
        if i == 0:
            sc, rs = carve(regA, 0, 1088), carve(regA, 1088, 640)
        elif i == 1:
            sc, rs = carve(regB, 0, 1088), carve(regB, 1088, 640)
        else:
            sc, rs = carve(regA, 1728, 1088), carve(regB, 1728, 640)
        k = lambda n: ('gm', i, n)
        f32 = lambda r, o: r[:, o:o + 128]
        b16 = lambda r, o, n=128: r[:, o:o + n // 2].bitcast(BF16)
        return dict(idx=i,
            LAU=(f32(sc, 0), k('lau')), IDB=(f32(sc, 128), k('idb')), E=(f32(sc, 256), k('e')), EI=(f32(sc, 384), k('ei')),
            ES=(f32(sc, 512), k('es')), PF=(f32(sc, 640), k('pf')),
            Bm=[(b16(sc, 768), k('b0')), (b16(sc, 832), k('b1'))], Am=[(b16(sc, 896), k('a0')), (b16(sc, 960), k('a1'))],
            PB=(b16(sc, 1024), k('pb')),
            TT=(b16(rs, 0), k('tt')), ATT=(b16(rs, 64), k('att')), NW=(b16(rs, 128, 512), k('nw')), QG=(b16(rs, 384), k('qg')),
            EGB=(f32(rs, 448), k('egb')), VN=(b16(rs, 576), k('vn')))
    SETS = [mkset(i) for i in range(3)]
    STS = [dict(S=(smallr[:, 576 + 128 * si:576 + 128 * (si + 1)], ('gm', 'S%d' % si)),
                SB=(smallr[:, 960 + 64 * si:960 + 64 * (si + 1)].bitcast(BF16), ('gm', 'sb%d' % si))) for si in range(3)]

    for d in range(2):
        col = d * 4 + h
        bc = lambda n: G[n][:, :, col:col + 1].to_broadcast([128, NT, 128])
        P.I('dve', 'tensor_tensor', v3(BV), v3(VTOK), bc('BETA'), op=ALU.mult, reads=[kVTOK, kg], writes=[kBV])
        P.I('dve', 'tensor_tensor', v3(KBG), v3(KTOK), bc('BEG'), op=ALU.mult, reads=[kKTOK, kg], writes=[kKBG])
        P.I('dve', 'tensor_tensor', v3(KTL), v3(KTOK), bc('EGL'), op=ALU.mult, reads=[kKTOK, kg], writes=[kKTL])
        P.I('dve', 'tensor_tensor', v3(KTL3), v3(KTOK), bc('EGL3'), op=ALU.mult, reads=[kKTOK, kg], writes=[kKTL3])

        def prep(tt, Z, d=d, col=col):
            t0 = tt * 128
            la = G['LA'][:, tt, col:col + 1]
            beta = G['BETA'][:, tt, col:col + 1]
            (LAU, kLAU), (IDB, kIDB), (E, kE), (EI, kEI), (ES, kES), (PF, kPF) = Z['LAU'], Z['IDB'], Z['E'], Z['EI'], Z['ES'], Z['PF']
            Bm, Am = Z['Bm'], Z['Am']
            PB, kPB = Z['PB']
            P.I('act', 'activation', LAU, U[d], AF.Identity, scale=la, reads=['consts', kg], writes=[kLAU])
            P.I('act', 'activation', IDB, ident_f, AF.Identity, scale=beta, reads=['consts', kg], writes=[kIDB])
            bg, kbg = bank()
            P.I('pe', 'matmul', bg[:, 0:128], LS[d], LAU, start=True, stop=True, reads=['consts', kLAU], writes=[kbg])
            P.I('pe', 'matmul', bg[:, 128:256], ones_f[:], IDB, start=True, stop=True, reads=['ones_f', kIDB], writes=[kbg])
            P.I('pe', 'matmul', bg[:, 256:384], ones_f[:], LAU, start=True, stop=True, reads=['ones_f', kLAU], writes=[kbg])
            bq, kbq = bank()
            P.I('pe', 'matmul', bq[:, 0:128], KT[:, t0:t0 + 128], KT[:, t0:t0 + 128], start=True, stop=True, reads=[kKT], writes=[kbq])
            P.I('pe', 'matmul', bq[:, 128:256], KT[:, t0:t0 + 128], QT[:, t0:t0 + 128], start=True, stop=True, reads=[kKT, kQT], writes=[kbq])
            yield
            P.I('act', 'activation', E, bg[:, 0:128], AF.Exp, reads=[kbg], writes=[kE])
            EGB, kEGB = Z['EGB']
            P.I('act', 'activation', EGB, bg[:, 256:384], AF.Exp, reads=[kbg], writes=[kEGB])
            P.I('pool', 'tensor_tensor', EI, E, MINC[d], op=ALU.mult, reads=[kE, 'consts'], writes=[kEI])
            P.I('pool', 'tensor_tensor', ES, E, MSTR[d], op=ALU.mult, reads=[kE, 'consts'], writes=[kES])
            yield
            P.I('dve', 'tensor_tensor', ES, ES, bg[:, 128:256], op=ALU.mult, reads=[kES, kbg], writes=[kES])
            (B0, kB0), (A0, kA0) = Bm[0], Am[0]
            P.I('dve', 'tensor_tensor', B0, bq[:, 0:128], ES, op=ALU.mult, reads=[kbq, kES], writes=[kB0])
            ATT, kATT = Z['ATT']
            P.I('dve', 'tensor_tensor', ATT, bq[:, 128:256], EI, op=ALU.mult, reads=[kbq, kEI], writes=[kATT])
            QG, kQG = Z['QG']
            P.I('dve', 'tensor_tensor', QG, QT[:, t0:t0 + 128], EGB, op=ALU.mult, reads=[kQT, kEGB], writes=[kQG])
            bt, kbt = bank()
            btb = bt[:].bitcast(BF16)
            P.I('pe', 'transpose', btb[:, 0:128], B0, ident_b[:], reads=[kB0, 'ident_b'], writes=[kbt])
            yield
            copy_op(A0, btb[:, 0:128], reads=[kbt], writes=[kA0], eng='act')
            P.I('dve', 'tensor_tensor', PF, ident_f, B0, op=ALU.subtract, reads=['consts', kB0], writes=[kPF])
            P.I('act', 'copy', PB, PF, reads=[kPF], writes=[kPB])
            cur = 0
            for stg in range(4):
                (Bc, kBc), (Ac, kAc) = Bm[cur], Am[cur]
                (Bn, kBn), (An, kAn) = Bm[1 - cur], Am[1 - cur]
                bn, kbn = bank()
                P.I('pe', 'matmul', bn[:, 0:128], Bc, Ac, start=True, stop=True, reads=[kBc, kAc], writes=[kbn])
                bn2, kbn2 = bank()
                if stg < 3:
                    P.I('pe', 'matmul', bn2[:, 0:128], Ac, Bc, start=True, stop=True, reads=[kBc, kAc], writes=[kbn2])
                yield
                copy_op(An, bn[:, 0:128], reads=[kbn], writes=[kAn], eng='act')
                if stg < 3:
                    copy_op(Bn, bn2[:, 0:128], reads=[kbn2], writes=[kBn], eng='dve')
                bp, kbp = bank()
                P.I('pe', 'matmul', bp[:, 0:128], An, PB, start=True, stop=True, reads=[kAn, kPB], writes=[kbp])
                yield
                P.I('dve', 'tensor_tensor', PF, PF, bp[:, 0:128], op=ALU.add, reads=[kPF, kbp], writes=[kPF])
                if stg < 3:
                    P.I('act', 'copy', PB, PF, reads=[kPF], writes=[kPB])
                cur = 1 - cur
            TT, kTT = Z['TT']
            P.I('act', 'copy', TT, PF, reads=[kPF], writes=[kTT])
            bw, kbw = bank()
            P.I('pe', 'matmul', bw[:, 0:128], KBG[:, t0:t0 + 128], TT, start=True, stop=True, reads=[kKBG, kTT], writes=[kbw])
            yield
            NW, kNW = Z['NW']
            P.I('dve', 'tensor_tensor', NW.rearrange("p (c i) -> p c i", i=128), bw[:, 0:128].unsqueeze(1).to_broadcast([128, 4, 128]),
                consts[:, C_NEGCM:C_NEGCM + 512].rearrange("p (c i) -> p c i", i=128), op=ALU.mult, reads=[kbw, 'consts'], writes=[kNW])

        def recur(tt, Z, ST, d=d):
            t0 = tt * 128
            (S, kS), (Sb, kSb) = ST['S'], ST['SB']
            (TT, kTT), (ATT, kATT), (NW, kNW), (QG, kQG), (EGB, kEGB), (VN, kVN) = Z['TT'], Z['ATT'], Z['NW'], Z['QG'], Z['EGB'], Z['VN']
            bi = Z['idx']
            vn, kvn = ctx['banks'][2 + 2 * bi], ('bank', 2 + 2 * bi)
            ob, kob = ctx['banks'][3 + 2 * bi], ('bank', 3 + 2 * bi)
            P.I('pe', 'matmul', vn[:, 0:128], TT, BV[:, t0:t0 + 128], start=True, stop=False, reads=[kTT, kBV], writes=[kvn])
            chunks = [0, 1, 2, 3] if d == 0 else [3, 2, 1, 0]
            for n, c in enumerate(chunks):
                P.I('pe', 'matmul', vn[:, 0:128], NW[:, 128 * c:128 * c + 128], Sb, start=False, stop=(n == 3), reads=[kNW, kSb], writes=[kvn])
                P.I('pe', 'matmul', ob[:, 32 * c:32 * c + 32], Sb, QG[:, 32 * c:32 * c + 32], start=(n == 0), stop=False,
                    reads=[kSb, kQG], writes=[kob])
                yield
                copy_op(VN, vn[:, 0:128], reads=[kvn], writes=[kVN], eng='act')
                yield
                kvb, kkv = ctx['banks'][0][:, bi * 128:bi * 128 + 128], ('bank', 0)
                if c < 3:
                    P.I('pe', 'matmul', kvb[:, 0:128], KTL[32 * c:32 * c + 32, t0:t0 + 128], VN[32 * c:32 * c + 32, :], start=True, stop=True,
                        reads=[kKTL, kVN], writes=[kkv])
                else:
                    P.I('pe', 'matmul', kvb[:, 0:128], KTL3[64:128, t0:t0 + 128], VN[64:128, :], start=True, stop=True,
                        reads=[kKTL3, kVN], writes=[kkv])
                yield
                cc = 32 * c + (31 if d == 0 else 0)
                P.I('dve', 'scalar_tensor_tensor', S, S, EGB[:, cc:cc + 1], kvb[:, 0:128], op0=ALU.mult, op1=ALU.add,
                    reads=[kS, kkv, kEGB], writes=[kS])
                yield
                copy_op(Sb, S, reads=[kS], writes=[kSb], eng='act')
                yield
            P.I('pe', 'matmul', ob[:, 0:128], VN, ATT, start=False, stop=True, reads=[kVN, kATT], writes=[kob])
            yield
            if tt not in written:
                written.add(tt)
                copy_op(OACC[:, t0:t0 + 128], ob[:, 0:128], reads=[kob, kOACC], writes=[(kOACC[0], kOACC[1], tt)], eng='act')
            else:
                P.I('dve', 'tensor_tensor', OACC[:, t0:t0 + 128], OACC[:, t0:t0 + 128], ob[:, 0:128], op=ALU.add,
                    reads=[kob, kOACC, (kOACC[0], kOACC[1], tt)], writes=[(kOACC[0], kOACC[1], tt)])

        seqt = []
        for si, (s0, sl, row) in enumerate(SEQS):
            tiles = list(range(s0 // 128, (s0 + sl) // 128))
            if d == 1:
                tiles = tiles[::-1]
            seqt.append([(si, tt, j == 0, j == len(tiles) - 1) for j, tt in enumerate(tiles)])
        groups = [[seqt[0][0], seqt[1][0], seqt[2][0]], [seqt[0][1], seqt[1][1], seqt[2][1]],
                  seqt[2][2:5], seqt[2][5:8]]

        def rr_run(gens):
            live = list(gens)
            while live:
                for g in list(live):
                    try:
                        next(g)
                    except StopIteration:
                        live.remove(g)

        def recur_full(i, si, tt, first, last):
            ST = STS[si]
            (S, kS), (Sb, kSb) = ST['S'], ST['SB']
            if first:
                if SEQS[si][2] == 0:
                    P.I('dve', 'memset', S, 0.0, writes=[kS])
                else:
                    P.dma('sp', S, st_in[l, d, h], writes=[kS])
                copy_op(Sb, S, reads=[kS], writes=[kSb], eng='act')
            yield from recur(tt, SETS[i], ST)
            if last and SEQS[si][2] == 0:
                P.dma('sp', st_out[si, l, d, h], S, reads=[kS], writes=[('st_out', 'gdn', si, l, d, h)])

        for grp in groups:
            rr_run([prep(tt, SETS[i]) for i, (si, tt, first, last) in enumerate(grp)])
            rg = [recur_full(i, si, tt, first, last) for i, (si, tt, first, last) in enumerate(grp)]
            ctx['st']['nbanks'], ctx['st']['bank'] = 2, 0
            if len(set(si for si, _, _, _ in grp)) == len(grp):
                rr_run(rg)
            else:
                for g in rg:
                    rr_run([g])
            ctx['st']['nbanks'] = 8

    for tb in range(3):
        if STOP <= 5:
            break
        okeys = [(kOACC[0], kOACC[1], tt) for tt in range(tb * 4, tb * 4 + 4)]
        okr = okeys + [kOACC]
        o = tbs(OACC, tb)
        i = rr('sq', 2)
        P.I('act', 'activation', sq[i][:, 0:512], o, AF.Square, reads=okr, writes=[('sq', i)])
        bs, kbs = bank()
        P.I('pe', 'matmul', bs[:], ones_b[:], sq[i][:, 0:512], start=True, stop=True, reads=[('sq', i), 'ones_b'], writes=[kbs])
        rt = tbs(arena[:, 12288 + 6 * T:12288 + 7 * T], tb)
        P.I('act', 'activation', rt, bs[:], AF.Ln, bias=EPS, scale=1.0 / HD, reads=[kbs], writes=[kBV])
        P.I('act', 'activation', rt, rt, AF.Exp, scale=-0.5, reads=[kBV], writes=[kBV])
        P.I('dve', 'scalar_tensor_tensor', o, o, ctx['gdn_nw'][:, l:l + 1], rt, op0=ALU.mult, op1=ALU.mult,
            reads=okr + [kBV, 'gdn_nw'], writes=okeys)
        P.I('dve', 'tensor_tensor', catT[:, 4 + h, tb * 512:(tb + 1) * 512], o, tbs(GS, tb), op=ALU.mult,
            reads=okr + [kGS], writes=[('cat', 4 + h)])


TWO_PI = float(2 * np.pi)


def s5_layer(ctx, l):
    P, W, hT, catT, arena, bank, banks = ctx['P'], ctx['W'], ctx['hT'], ctx['catT'], ctx['arena'], ctx['bank'], ctx['banks']
    consts, ident_b, copy_op, rr = ctx['consts'], ctx['ident_b'], ctx['copy_op'], ctx['rr']
    tau, fs = ctx['s5_tau'], ctx['s5_fs']

    def SF(i):
        return arena[:, 12288 + T * i:12288 + T * (i + 1)], ('slot', i)

    def SB(i, half):
        return arena[:, 12288 + T * i:12288 + T * (i + 1)].bitcast(BF16)[:, half * T:(half + 1) * T], ('slot', i)
    base = 12288

    def R(o, n, key):
        return arena[:, base + o:base + o + n], ('s5', key)
    UTF, kUTF = R(0, 1536, 'utf')
    UTB, kUTB = arena[:, base + 1536:base + 2304].bitcast(BF16), ('s5', 'utb')
    COSs = [R(2304, 1024, 'cos0'), R(4352, 1024, 'cos1')]
    SINs = [R(3328, 1024, 'sin0'), R(5376, 1024, 'sin1')]
    BR, kBR = R(6400, 1536, 'br')
    BI, kBI = R(7936, 1536, 'bi')
    T1, kT1 = R(9472, 512, 't1')
    T2, kT2 = R(9984, 512, 't2')
    P1, kP1 = R(10496, 512, 'p1')
    P2, kP2 = R(11008, 512, 'p2')
    XR, kXR = arena[:, base + 11520:base + 12288].bitcast(BF16), ('s5', 'xr')
    XI, kXI = arena[:, base + 12288:base + 13056].bitcast(BF16), ('s5', 'xi')
    MXs = [R(13056, 1024, 'mx0'), R(14080, 1024, 'mx1')]
    GT1, kGT1 = R(6400, 1536, 'br')
    GT2, kGT2 = R(7936, 1536, 'bi')
    small = arena[:, SMALL0:SMALL0 + 1280]
    names = ('AR', 'AI', 'DT', 'MAG', 'TH', 'CT', 'STH', 'ABR', 'ABI', 'FR', 'FI', 'NFI', 'X0R', 'X0I', 'TA', 'TB')
    S = {n: small[:, 32 * i:32 * (i + 1)] for i, n in enumerate(names)}
    kp = 's5_par'
    KI = small[:, 512:544].bitcast(I32)
    P.dma('sp', small[:, 0:96].rearrange("p (a b) -> p a b", b=32), ctx['s5p_d'][:, l], writes=[kp])
    P.dma('sp', small[:, 384:448].rearrange("p (a b) -> p a b", b=32), ctx['s5x0_d'][:, l], writes=[kp])

    def E(eng, meth, *a, **k):
        P.I(eng, meth, *a, reads=[kp] + k.pop('r', []), writes=[kp] + k.pop('w', []), **k)

    def sincos(dst_sin, dst_cos, ang, tmp, ki, n):
        for dst, sh in ((dst_sin, 0.0), (dst_cos, 0.25)):
            E('dve', 'tensor_scalar', tmp, ang, 1.0 / TWO_PI, sh, op0=ALU.mult, op1=ALU.add)
            E('dve', 'tensor_copy', ki, tmp)
            E('dve', 'tensor_copy', dst, ki)
            E('dve', 'tensor_tensor', tmp, tmp, dst, op=ALU.subtract)
            E('act', 'activation', dst, tmp, AF.Sin, scale=TWO_PI)
    E('act', 'activation', S['DT'], small[:, 64:96], AF.Exp)
    E('dve', 'tensor_tensor', S['TA'], S['AR'], S['DT'], op=ALU.mult)
    E('act', 'activation', S['MAG'], S['TA'], AF.Exp)
    E('dve', 'tensor_tensor', S['TH'], S['AI'], S['DT'], op=ALU.mult)
    sincos(S['STH'], S['CT'], S['TH'], S['TA'], KI, 32)
    E('dve', 'tensor_scalar', S['TA'], S['TH'], 1.0 / TWO_PI, None, op0=ALU.mult)
    E('dve', 'tensor_copy', KI, S['TA'])
    E('dve', 'tensor_copy', S['TB'], KI)
    E('dve', 'scalar_tensor_tensor', S['TH'], S['TB'], -TWO_PI, S['TH'], op0=ALU.mult, op1=ALU.add)
    E('dve', 'tensor_tensor', S['ABR'], S['MAG'], S['CT'], op=ALU.mult)
    E('dve', 'tensor_tensor', S['ABI'], S['MAG'], S['STH'], op=ALU.mult)
    E('dve', 'tensor_tensor', S['TA'], S['AR'], S['AR'], op=ALU.mult)
    E('dve', 'tensor_tensor', S['TB'], S['AI'], S['AI'], op=ALU.mult)
    E('dve', 'tensor_tensor', S['TA'], S['TA'], S['TB'], op=ALU.add)
    E('dve', 'reciprocal', S['TA'], S['TA'])
    E('dve', 'tensor_scalar', S['TB'], S['ABR'], -1.0, None, op0=ALU.add)
    E('dve', 'tensor_tensor', S['FR'], S['TB'], S['AR'], op=ALU.mult)
    E('dve', 'tensor_tensor', S['FI'], S['ABI'], S['AI'], op=ALU.mult)
    E('dve', 'tensor_tensor', S['FR'], S['FR'], S['FI'], op=ALU.add)
    E('dve', 'tensor_tensor', S['FR'], S['FR'], S['TA'], op=ALU.mult)
    E('dve', 'tensor_tensor', S['FI'], S['ABI'], S['AR'], op=ALU.mult)
    E('dve', 'tensor_tensor', S['TB'], S['TB'], S['AI'], op=ALU.mult)
    E('dve', 'tensor_tensor', S['FI'], S['FI'], S['TB'], op=ALU.subtract)
    E('dve', 'tensor_tensor', S['FI'], S['FI'], S['TA'], op=ALU.mult)
    E('dve', 'tensor_scalar', S['NFI'], S['FI'], -1.0, None, op0=ALU.mult)
    ETM = small[:, 576:641]
    EKI = small[:, 648:713].bitcast(I32)
    EAN = small[:, 720:785]
    ESNs = [small[:, 792:857], small[:, 936:1001]]
    ECSs = [small[:, 864:929], small[:, 1008:1073]]
    INI = small[:, 1080:1088]
    ke = 's5_e'
    ybanks = [(banks[5 + tb], ('bank', 5 + tb)) for tb in range(3)]

    for cc in range(4):
        wv, wkey = W.next(('in', l, (('su', cc))))
        for tb in range(3):
            bk, bkey = bank()
            for kc in range(KC):
                P.I('pe', 'matmul', bk[:], wv[:, kc, :], hT[:, kc, tb * 512:(tb + 1) * 512], start=(kc == 0), stop=(kc == KC - 1),
                    reads=[wkey, ('hT', kc, tb)], writes=[bkey])
            P.I('act', 'copy', UTF[:, tb * 512:(tb + 1) * 512], bk[:], reads=[bkey], writes=[kUTF])
            P.I('dve', 'tensor_copy', UTB[:, tb * 512:(tb + 1) * 512], UTF[:, tb * 512:(tb + 1) * 512], reads=[kUTF], writes=[kUTB])
        pairs = [(sc, d) for sc in range(4 * cc, 4 * cc + 4) for d in range(2)]
        segs = [(0, 2, 256, 0), (512, 1, 512, 0), (1024, 1, 512, 512)]

        def prep(k):
            sc, d = pairs[k]
            col = d * 16 + sc
            cs = lambda n: S[n][:, col:col + 1]
            (MX, kMX), (COS, kCOS), (SIN, kSIN) = MXs[k % 2], COSs[k % 2], SINs[k % 2]
            ESN, ECS, kee = ESNs[k % 2], ECSs[k % 2], ('s5_e', k % 2)
            mxf = MX[:, 0:512].rearrange("p (a b) -> p a b", b=128)
            mxb = MX[:, 512:1024].bitcast(BF16).rearrange("p (a b) -> p a b", b=128)
            P.dma('sp', mxf[:, 0:2, :], ctx['s5bx'][:, l, d, sc].rearrange("r s c -> s r c"), writes=[kMX])
            P.dma('sp', mxf[:, 2:4, :], ctx['s5cx'][:, l, d, sc].rearrange("r s c -> s r c"), writes=[kMX])
            P.I('act', 'activation', P1[:, 0:128], mxf[:, 0, :], AF.Identity, scale=cs('FR'), reads=[kMX, kp], writes=[kP1])
            P.I('dve', 'scalar_tensor_tensor', mxb[:, 0, :], mxf[:, 1, :], cs('NFI'), P1[:, 0:128], op0=ALU.mult, op1=ALU.add,
                reads=[kMX, kP1, kp], writes=[kMX])
            P.I('act', 'activation', P1[:, 128:256], mxf[:, 1, :], AF.Identity, scale=cs('FR'), reads=[kMX, kp], writes=[kP1])
            P.I('dve', 'scalar_tensor_tensor', mxb[:, 1, :], mxf[:, 0, :], cs('FI'), P1[:, 128:256], op0=ALU.mult, op1=ALU.add,
                reads=[kMX, kP1, kp], writes=[kMX])
            bk, bkey = bank()
            bkb = bk[:].bitcast(BF16)
            for i in range(2):
                P.I('pe', 'transpose', bkb[:, i * 128:(i + 1) * 128], mxb[:, i, :], ident_b[:], reads=[kMX, 'ident_b'], writes=[bkey])
            P.I('act', 'copy', mxb[:, 2:4, :], bkb[:, 0:256].rearrange("p (a b) -> p a b", b=128), reads=[bkey], writes=[kMX])
            P.I('dve', 'tensor_copy', mxb[:, 4, :], mxf[:, 2, :], reads=[kMX], writes=[kMX])
            P.I('dve', 'tensor_scalar', mxb[:, 5, :], mxf[:, 3, :], -1.0, None, op0=ALU.mult, reads=[kMX], writes=[kMX])
            P.I('dve', 'tensor_scalar', EAN, tau[:], cs('TH'), None, op0=ALU.mult, reads=['s5_tau', kp], writes=[ke])
            for dst, sh in ((ESN, 0.0), (ECS, 0.25)):
                P.I('dve', 'tensor_scalar', ETM, EAN, 1.0 / TWO_PI, sh, op0=ALU.mult, op1=ALU.add, reads=[ke], writes=[ke])
                P.I('dve', 'tensor_copy', EKI, ETM, reads=[ke], writes=[ke])
                P.I('dve', 'tensor_copy', dst, EKI, reads=[ke], writes=[kee])
                P.I('dve', 'tensor_tensor', ETM, ETM, dst, op=ALU.subtract, reads=[ke, kee], writes=[ke])
                P.I('act', 'activation', dst, ETM, AF.Sin, scale=TWO_PI, reads=[ke], writes=[kee])
            hi = lambda t: t[:, 32:64].unsqueeze(2).to_broadcast([128, 32, 32])
            lo = lambda t: t[:, 0:32].unsqueeze(1).to_broadcast([128, 32, 32])
            c3 = COS.rearrange("p (a b) -> p a b", b=32)
            s3 = SIN.rearrange("p (a b) -> p a b", b=32)
            q1 = P1.rearrange("p (a b) -> p a b", b=32)[:, 0:16, :]
            for hh in range(2):
                hs = slice(16 * hh, 16 * hh + 16)
                hi_ = lambda t: t[:, 32 + 16 * hh:48 + 16 * hh].unsqueeze(2).to_broadcast([128, 16, 32])
                lo_ = lambda t: t[:, 0:32].unsqueeze(1).to_broadcast([128, 16, 32])
                p1v = P1.rearrange("p (a b) -> p a b", b=32)
                p2v = P2.rearrange("p (a b) -> p a b", b=32)
                P.I('pool', 'tensor_tensor', c3[:, hs, :], hi_(ECS), lo_(ECS), op=ALU.mult, reads=[kee], writes=[kCOS])
                P.I('pool', 'tensor_tensor', p1v, hi_(ESN), lo_(ESN), op=ALU.mult, reads=[kee], writes=[kP1])
                P.I('pool', 'tensor_tensor', c3[:, hs, :], c3[:, hs, :], p1v, op=ALU.subtract, reads=[kP1, kCOS], writes=[kCOS])
                P.I('pool', 'tensor_tensor', s3[:, hs, :], hi_(ESN), lo_(ECS), op=ALU.mult, reads=[kee], writes=[kSIN])
                P.I('pool', 'tensor_tensor', p2v, hi_(ECS), lo_(ESN), op=ALU.mult, reads=[kee], writes=[kP2])
                P.I('pool', 'tensor_tensor', s3[:, hs, :], s3[:, hs, :], p2v, op=ALU.add, reads=[kP2, kSIN], writes=[kSIN])

        def demod(k):
            sc, d = pairs[k]
            (MX, kMX), (COS, kCOS), (SIN, kSIN) = MXs[k % 2], COSs[k % 2], SINs[k % 2]
            mxb = MX[:, 512:1024].bitcast(BF16).rearrange("p (a b) -> p a b", b=128)
            BBTr, BBTi = mxb[:, 2, :], mxb[:, 3, :]
            for tb in range(3):
                pr, kpr = bank()
                pi, kpi = bank()
                P.I('pe', 'matmul', pr[:], BBTr, UTB[:, tb * 512:(tb + 1) * 512], start=True, stop=True, reads=[kMX, kUTB], writes=[kpr])
                P.I('pe', 'matmul', pi[:], BBTi, UTB[:, tb * 512:(tb + 1) * 512], start=True, stop=True, reads=[kMX, kUTB], writes=[kpi])
                t0, nrep, ln, ta0 = segs[tb]
                v = lambda ap: ap[:, t0:t0 + nrep * ln].rearrange("p (r n) -> p r n", n=ln)
                w = lambda ap: ap.rearrange("p (r n) -> p r n", n=ln)
                tv = lambda tab: tab[:, ta0:ta0 + ln].unsqueeze(1).to_broadcast([128, nrep, ln])
                P.I('dve', 'tensor_tensor', v(BR), w(pr[:]), tv(COS), op=ALU.mult, reads=[kpr, kCOS], writes=[kBR])
                P.I('dve', 'tensor_tensor', w(T1), w(pi[:]), tv(SIN), op=ALU.mult, reads=[kpi, kSIN], writes=[kT1])
                P.I('pool', 'tensor_tensor', v(BR), v(BR), w(T1), op=(ALU.add if d == 0 else ALU.subtract), reads=[kBR, kT1], writes=[kBR])
                P.I('dve', 'tensor_tensor', v(BI), w(pi[:]), tv(COS), op=ALU.mult, reads=[kpi, kCOS], writes=[kBI])
                P.I('dve', 'tensor_tensor', w(T2), w(pr[:]), tv(SIN), op=ALU.mult, reads=[kpr, kSIN], writes=[kT2])
                P.I('pool', 'tensor_tensor', v(BI), v(BI), w(T2), op=(ALU.subtract if d == 0 else ALU.add), reads=[kBI, kT2], writes=[kBI])

        def main(k):
            sc, d = pairs[k]
            col = d * 16 + sc
            cs = lambda n: S[n][:, col:col + 1]
            (MX, kMX), (COS, kCOS), (SIN, kSIN) = MXs[k % 2], COSs[k % 2], SINs[k % 2]
            ESN, ECS, kee = ESNs[k % 2], ECSs[k % 2], ('s5_e', k % 2)
            mxb = MX[:, 512:1024].bitcast(BF16).rearrange("p (a b) -> p a b", b=128)
            CCTr, CCTi = mxb[:, 4, :], mxb[:, 5, :]
            if d == 0:
                cph, sph = COS[:, 1:2], SIN[:, 1:2]
            else:
                cph, sph = ECS[:, 64:65], ESN[:, 64:65]
            ki = 's5_ini'
            P.I('dve', 'tensor_tensor', INI[:, 0:1], cs('X0R'), cph, op=ALU.mult, reads=[kp, kCOS, kee], writes=[ki])
            P.I('dve', 'tensor_tensor', INI[:, 1:2], cs('X0I'), sph, op=ALU.mult, reads=[kp, kSIN, kee], writes=[ki])
            P.I('dve', 'tensor_tensor', INI[:, 0:1], INI[:, 0:1], INI[:, 1:2], op=ALU.subtract, reads=[ki], writes=[ki])
            P.I('dve', 'tensor_tensor', INI[:, 2:3], cs('X0I'), cph, op=ALU.mult, reads=[kp, kCOS, kee], writes=[ki])
            P.I('dve', 'tensor_tensor', INI[:, 3:4], cs('X0R'), sph, op=ALU.mult, reads=[kp, kSIN, kee], writes=[ki])
            P.I('dve', 'tensor_tensor', INI[:, 2:3], INI[:, 2:3], INI[:, 3:4], op=ALU.add, reads=[ki], writes=[ki])
            for si, (s0, sl, row) in enumerate(SEQS):
                rho = cs('MAG').to_broadcast([128, sl])
                for buf, kb, ini in ((BR, kBR, INI[:, 0:1]), (BI, kBI, INI[:, 2:3])):
                    seg = buf[:, s0:s0 + sl]
                    if d == 1:
                        seg = seg[:, ::-1]
                    P.I('dve', 'tensor_tensor_scan', seg, rho, seg, (ini if row == 1 else 0.0), op0=ALU.mult, op1=ALU.add,
                        reads=[kb, kp, ki], writes=[kb])
                if row == 0:
                    fcol = ((si * 2 + l) * 2 + d) * 16 + sc
                    if d == 1:
                        P.I('dve', 'tensor_copy', fs[0][:, fcol:fcol + 1], BR[:, s0:s0 + 1], reads=[kBR], writes=[('s5_fs', 0)])
                        P.I('dve', 'tensor_copy', fs[1][:, fcol:fcol + 1], BI[:, s0:s0 + 1], reads=[kBI], writes=[('s5_fs', 1)])
                    else:
                        e = s0 + sl - 1
                        cL, sL = COS[:, sl - 1:sl], SIN[:, sl - 1:sl]
                        P.I('dve', 'tensor_tensor', INI[:, 4:5], BR[:, e:e + 1], cL, op=ALU.mult, reads=[kBR, kCOS], writes=[ki])
                        P.I('dve', 'tensor_tensor', INI[:, 5:6], BI[:, e:e + 1], sL, op=ALU.mult, reads=[kBI, kSIN], writes=[ki])
                        P.I('dve', 'tensor_tensor', fs[0][:, fcol:fcol + 1], INI[:, 4:5], INI[:, 5:6], op=ALU.subtract, reads=[ki], writes=[('s5_fs', 0)])
                        P.I('dve', 'tensor_tensor', INI[:, 6:7], BR[:, e:e + 1], sL, op=ALU.mult, reads=[kBR, kSIN], writes=[ki])
                        P.I('dve', 'tensor_tensor', INI[:, 7:8], BI[:, e:e + 1], cL, op=ALU.mult, reads=[kBI, kCOS], writes=[ki])
                        P.I('dve', 'tensor_tensor', fs[1][:, fcol:fcol + 1], INI[:, 6:7], INI[:, 7:8], op=ALU.add, reads=[ki], writes=[('s5_fs', 1)])
            for t0, nrep, ln, ta0 in segs:
                v = lambda ap: ap[:, t0:t0 + nrep * ln].rearrange("p (r n) -> p r n", n=ln)
                w = lambda ap: ap.rearrange("p (r n) -> p r n", n=ln)
                tv = lambda tab: tab[:, ta0:ta0 + ln].unsqueeze(1).to_broadcast([128, nrep, ln])
                P.I('dve', 'tensor_tensor', w(T1), v(BR), tv(COS), op=ALU.mult, reads=[kBR, kCOS], writes=[kT1])
                P.I('dve', 'tensor_tensor', w(T2), v(BI), tv(SIN), op=ALU.mult, reads=[kBI, kSIN], writes=[kT2])
                P.I('dve', 'tensor_tensor', v(XR), w(T1), w(T2), op=(ALU.subtract if d == 0 else ALU.add), reads=[kT1, kT2], writes=[kXR])
                P.I('pool', 'tensor_tensor', w(P1), v(BI), tv(COS), op=ALU.mult, reads=[kBI, kCOS], writes=[kP1])
                P.I('pool', 'tensor_tensor', w(P2), v(BR), tv(SIN), op=ALU.mult, reads=[kBR, kSIN], writes=[kP2])
                P.I('pool', 'tensor_tensor', v(XI), w(P1), w(P2), op=(ALU.add if d == 0 else ALU.subtract), reads=[kP1, kP2], writes=[kXI])
            for tb in range(3):
                yb, kyb = ybanks[tb]
                P.I('pe', 'matmul', yb[:], CCTr, XR[:, tb * 512:(tb + 1) * 512], start=(k == 0), stop=False, reads=[kMX, kXR], writes=[kyb])
                P.I('pe', 'matmul', yb[:], CCTi, XI[:, tb * 512:(tb + 1) * 512], start=False, stop=(k == 7), reads=[kMX, kXI], writes=[kyb])

        prep(0)
        for k in range(8):
            demod(k)
            if k < 7:
                prep(k + 1)
            main(k)
        dcol = ctx['s5_d'][:, l, 0, cc:cc + 1]
        for tb in range(3):
            yb, kyb = ybanks[tb]
            sl_ = slice(tb * 512, (tb + 1) * 512)
            P.I('dve', 'scalar_tensor_tensor', GT1[:, sl_], UTF[:, sl_], dcol, yb[:], op0=ALU.mult, op1=ALU.add, reads=[kUTF, kyb, 's5_d'], writes=[kGT1])
            P.I('act', 'activation', GT2[:, sl_], GT1[:, sl_], AF.Square, reads=[kGT1], writes=[kGT2])
            P.I('dve', 'tensor_scalar', GT2[:, sl_], GT2[:, sl_], 0.044715, 1.0, op0=ALU.mult, op1=ALU.add, reads=[kGT2], writes=[kGT2])
            P.I('dve', 'tensor_tensor', GT2[:, sl_], GT2[:, sl_], GT1[:, sl_], op=ALU.mult, reads=[kGT2, kGT1], writes=[kGT2])
            P.I('act', 'activation', GT2[:, sl_], GT2[:, sl_], AF.Sigmoid, scale=2.0 * 0.7978845608028654, reads=[kGT2], writes=[kGT2])
            P.I('dve', 'tensor_tensor', catT[:, 12 + cc, sl_], GT1[:, sl_], GT2[:, sl_], op=ALU.mult, reads=[kGT1, kGT2], writes=[('cat', 12 + cc)])
    gates = [R(0, 1536, 'utf'), R(6400, 1536, 'br'), R(7936, 1536, 'bi'), R(2304, 1536, 'cos0')]
    for oc in range(4):
        wv, wkey = W.next(('glu', l, oc))
        gb = ctx['s5_d'][:, l, 1, oc:oc + 1]
        for tb in range(3):
            bk, bkey = bank()
            for kc in range(4):
                P.I('pe', 'matmul', bk[:], wv[:, kc, :], catT[:, 12 + kc, tb * 512:(tb + 1) * 512], start=(kc == 0), stop=(kc == 3),
                    reads=[wkey, ('cat', 12 + kc)], writes=[bkey])
            P.I('act', 'activation', gates[oc][0][:, tb * 512:(tb + 1) * 512], bk[:], AF.Sigmoid, bias=gb, reads=[bkey, 's5_d'], writes=[gates[oc][1]])
    for oc in range(4):
        P.I('dve', 'tensor_tensor', catT[:, 12 + oc, :], catT[:, 12 + oc, :], gates[oc][0], op=ALU.mult,
            reads=[('cat', 12 + oc), gates[oc][1]], writes=[('cat', 12 + oc)])


def s5_finish(ctx):
    P, bank, ident_f = ctx['P'], ctx['bank'], ctx['ident_f']
    for i in range(2):
        bk, bkey = bank()
        P.I('pe', 'transpose', bk[:, 0:128], ctx['s5_fs'][i][:], ident_f, reads=[('s5_fs', i), 'consts'], writes=[bkey])
        P.I('act', 'copy', ctx['s5_fs'][i][:], bk[:, 0:128], reads=[bkey], writes=[('s5_fs', i)])
        P.dma('sp', ctx['new_s5'][i], ctx['s5_fs'][i][:], reads=[('s5_fs', i)], writes=[('new_s5', i)])


def host_inputs(inputs, core):
    f = np.float32
    xp, xsm = inputs['x_prompt'], inputs['x_sample']
    xin = np.concatenate([xp[2 * core], xp[2 * core + 1], xsm[core]], axis=0).astype(f)
    c2 = np.stack([inputs['c_ctx'], inputs['c'][core]], axis=0)
    cT = np.ascontiguousarray(c2.reshape(2, KC, 128).transpose(2, 1, 0))

    def fm(v):
        return np.ascontiguousarray(v.reshape(v.shape[:-1] + (KC, 128)).swapaxes(-1, -2))
    m = {
        'xin': xin, 'cT': cT, 'n1w': fm(inputs['norm1_w']), 'n2w': fm(inputs['norm2_w']), 'fnw': fm(inputs['final_norm_w']),
        'adab': np.ascontiguousarray(inputs['ada_b'].reshape(2, 96, 128).swapaxes(1, 2)),
        'consts': make_consts(),
    }
    for k in ('ada_w', 'in_proj', 'out_proj', 'ffn_w1', 'ffn_w3', 'ffn_w2'):
        m[k] = np.asarray(inputs[k], dtype=f)
    rc, rs = make_rope()
    m.update(rm=make_rm(), rope_c=rc, rope_s=rs)
    m['rdl'] = np.ascontiguousarray(np.broadcast_to(inputs['ret_decay_logit'].reshape(1, 16), (128, 16)), f)
    m['hlb'] = np.ascontiguousarray(inputs['hg_lb_param'].reshape(2, 2, 4, 128).transpose(3, 0, 1, 2).reshape(128, 2, 8), f)
    m['hgnw'] = np.ascontiguousarray(inputs['hg_norm_w'].T, f)
    gpar = np.stack([inputs['gdn_dt_bias'].reshape(2, 8), inputs['gdn_a_log'].reshape(2, 8)], axis=1)
    m['gdnp'] = np.ascontiguousarray(np.broadcast_to(gpar[None], (128, 2, 2, 8)), f)
    m['gdncw'] = np.ascontiguousarray(inputs['gdn_conv'].reshape(2, 3, 12, 128).transpose(3, 0, 1, 2), f)
    m['gdnnw'] = np.ascontiguousarray(inputs['gdn_norm_w'].T, f)
    def sm(v):
        v = v.reshape(v.shape[:-2] + (16, 128))
        return np.ascontiguousarray(np.moveaxis(v, -1, 0), f)
    ls = np.broadcast_to(inputs['s5_log_step'][..., None], (2, 2, 32, 64))
    par = np.stack([sm(inputs['s5_a_re']), sm(inputs['s5_a_im']), sm(ls)], axis=2)
    m['s5p'] = np.ascontiguousarray(par.reshape(128, 2, 3, 32), f)
    x0 = np.stack([sm(inputs['state_s5_re'][core]), sm(inputs['state_s5_im'][core])], axis=2)
    m['s5x0'] = np.ascontiguousarray(x0.reshape(128, 2, 2, 32), f)
    bx = np.zeros((2, 2, 2, 32, 64, 128), f)
    cx = np.zeros((2, 2, 2, 32, 64, 128), f)
    for g in range(32):
        c0 = 16 * (g % 8)
        bx[0, :, :, g, :, c0:c0 + 16] = inputs['s5_b_re'][:, :, g]
        bx[1, :, :, g, :, c0:c0 + 16] = inputs['s5_b_im'][:, :, g]
        cx[0, :, :, g, :, c0:c0 + 16] = np.swapaxes(inputs['s5_c_re'][:, :, g], -1, -2)
        cx[1, :, :, g, :, c0:c0 + 16] = np.swapaxes(inputs['s5_c_im'][:, :, g], -1, -2)
    m['s5bx'] = bx.reshape(2, 2, 2, 16, 128, 128)
    m['s5cx'] = cx.reshape(2, 2, 2, 16, 128, 128)
    dg = np.stack([inputs['s5_d'].reshape(2, 4, 128), inputs['s5_glu_b'].reshape(2, 4, 128)], axis=1)
    m['s5dg'] = np.ascontiguousarray(dg.transpose(3, 0, 1, 2), f)
    tau = np.concatenate([np.arange(32), 32 * np.arange(33)]).astype(f)
    m['s5tau'] = np.ascontiguousarray(np.broadcast_to(tau, (128, 65)), f)
    m['s5_glu_w'] = np.asarray(inputs['s5_glu_w'], f)
    m['st_gdn'] = np.ascontiguousarray(inputs['state_gdn'][core], f)
    m['st_ret'] = np.ascontiguousarray(inputs['state_ret'][core], f)
    m['st_hg'] = np.ascontiguousarray(inputs['state_hgrn'][core], f)
    return m


def kernel(**inputs):
    inputs = {k: np.asarray(v) for k, v in inputs.items()}
    nc = build_program()
    in_maps = [host_inputs(inputs, i) for i in range(8)]
    res = run_bass_kernel_spmd(nc, in_maps, core_ids=list(range(8)))
    return assemble(res.results, inputs)


def assemble(results, inputs):
    f = np.float32
    yp = np.zeros((16, 256, D), f)
    ysm = np.zeros((8, 1024, D), f)
    st = {k: np.zeros((16, 2, 2, 4, 128, 128), f) for k in ('new_ret', 'new_gdn', 'new_hg')}
    s5 = {k: np.zeros((16, 2, 2, 32, 64), f) for k in ('new_s5re', 'new_s5im')}
    for i, r in enumerate(results):
        y = np.asarray(r['y'], f)
        yp[2 * i] = y[0:256]
        yp[2 * i + 1] = y[256:512]
        ysm[i] = y[512:]
        for k in st:
            st[k][2 * i:2 * i + 2] = np.asarray(r[k], f)
        for k in s5:
            s5[k][2 * i:2 * i + 2] = np.asarray(r[k], f).reshape(2, 2, 2, 32, 64)
    return (yp, ysm, st['new_ret'], st['new_gdn'], st['new_hg'], s5['new_s5re'], s5['new_s5im'])
```

```python
import numpy as np
import ml_dtypes
from contextlib import ExitStack
import concourse.bass as bass
import concourse.mybir as mybir
from concourse.bass_utils import run_bass_kernel_spmd

F32 = mybir.dt.float32
BF16 = mybir.dt.bfloat16
I32 = mybir.dt.int32
AF = mybir.ActivationFunctionType
ALU = mybir.AluOpType


class Prog:
    ENG = ('pe', 'dve', 'act', 'pool', 'sp')
    NDMA = {'sp': 12}
    EPOCH = 16384
    NEPOCH = {'pe': 6, 'dve': 3, 'act': 3, 'pool': 2, 'sp': 1}

    def __init__(self, nc):
        self.nc = nc
        self.es = ExitStack()
        self.eng = {'pe': nc.tensor, 'dve': nc.vector, 'act': nc.scalar, 'pool': nc.gpsimd, 'sp': nc.sync}
        self.csem = {e: [self.es.enter_context(nc.semaphore("c_%s%d" % (e, i))) for i in range(self.NEPOCH[e])] for e in self.ENG}
        self.dsem = {e: [self.es.enter_context(nc.semaphore("d_%s%d" % (e, i))) for i in range(n)]
                     for e, n in self.NDMA.items()}
        self.dval = {e: [0] * n for e, n in self.NDMA.items()}
        self.dnext = {e: 0 for e in self.NDMA}
        self.ops = {e: [] for e in self.ENG}
        self.seen = {e: {} for e in self.ENG}
        self.lastw = {}
        self.readers = {}
        self.pending = {}
        self.cnt_c = {}

    def sb(self, name, shape, dt):
        return self.es.enter_context(self.nc.sbuf_tensor("sb_" + name, list(shape), dt))

    def ps(self, name, shape, dt):
        return self.es.enter_context(self.nc.psum_tensor("ps_" + name, list(shape), dt))

    def _resolve(self, sid, val):
        if sid[0] == 'c':
            r = self.rank[sid[1]][val]
            ep = (r - 1) // self.EPOCH
            return self.csem[sid[1]][ep], r - ep * self.EPOCH
        return self.dsem[sid[1]][sid[2]], val

    def _need(self, e, ev, waits):
        if ev is None:
            return
        sid, val = ev
        if e == 'pe' and sid == ('c', 'pe'):
            return
        if self.seen[e].get(sid, 0) >= val:
            return
        self.seen[e][sid] = val
        waits[sid] = max(waits.get(sid, 0), val)

    def _deps(self, e, reads, writes, self_war=False):
        waits = {}
        for k in reads:
            self._need(e, self.lastw.get(k), waits)
        for k in writes:
            self._need(e, self.lastw.get(k), waits)
            for ev in self.readers.get(k, {}).items():
                if ev[0] == ('c', e) and not self_war:
                    continue
                self._need(e, ev, waits)
        return waits

    def _commit(self, ev, reads, writes):
        for k in reads:
            d = self.readers.setdefault(k, {})
            d[ev[0]] = max(d.get(ev[0], 0), ev[1])
        for k in writes:
            self.lastw[k] = ev
            self.readers[k] = {}

    def barrier(self):
        evs = [(('c', e), self.cnt_c[e]) for e in self.ENG if self.cnt_c.get(e, 0)]
        for e in self.NDMA:
            for i in range(self.NDMA[e]):
                if self.dval[e][i]:
                    evs.append((('d', e, i), self.dval[e][i]))
        for e in self.ENG:
            self.pending[e] = list(evs)

    def _pend(self, e, waits):
        for ev in self.pending.pop(e, []):
            self._need(e, ev, waits)

    def op(self, e, fn, reads=(), writes=()):
        if e != 'pe':
            bk = [k for k in reads if isinstance(k, tuple) and k and k[0] in ('bank', 'stat', 'psr')]
            if bk:
                writes = list(writes) + [k for k in bk if k not in writes]
        waits = self._deps(e, reads, writes, self_war=(e != 'pe'))
        self._pend(e, waits)
        n = self.cnt_c.get(e, 0) + 1
        self.cnt_c[e] = n
        ev = (('c', e), n)
        self.ops[e].append((waits, fn, n, 1))
        self._commit(ev, reads, writes)

    def I(self, e, meth, *args, reads=(), writes=(), **kw):
        self.op(e, lambda en: getattr(en, meth)(*args, **kw), reads=reads, writes=writes)

    def dma(self, e, out, in_, reads=(), writes=(), **kw):
        waits = self._deps(e, reads, writes, self_war=True)
        self._pend(e, waits)
        i = self.dnext[e]
        self.dnext[e] = (i + 1) % self.NDMA[e]
        sem = self.dsem[e][i]
        prev = self.dval[e][i]
        if prev:
            self._need(e, (('d', e, i), prev), waits)
        self.dval[e][i] = prev + 16
        ev = (('d', e, i), prev + 16)
        self.ops[e].append((waits, lambda en: en.dma_start(out=out, in_=in_, **kw), sem, 16))
        self._commit(ev, reads, writes)

    def finish(self):
        fin = []
        for e in self.NDMA:
            for i in range(self.NDMA[e]):
                if self.dval[e][i]:
                    fin.append((('d', e, i), self.dval[e][i]))
        fin += [(('c', e), self.cnt_c[e]) for e in self.ENG if self.cnt_c.get(e, 0)]
        needed = {e: set() for e in self.ENG}
        for e in self.ENG:
            for waits, fn, n, inc in self.ops[e]:
                for sid, val in waits.items():
                    if sid[0] == 'c':
                        needed[sid[1]].add(val)
        for sid, val in fin:
            if sid[0] == 'c':
                needed[sid[1]].add(val)
        self.rank = {e: {n: i + 1 for i, n in enumerate(sorted(needed[e]))} for e in self.ENG}
        nc = self.nc
        prog = self
        with nc.Block() as block:
            def run(name):
                def body(en):
                    for waits, fn, n, inc in prog.ops[name]:
                        for sid, val in waits.items():
                            s, v = prog._resolve(sid, val)
                            en.wait_ge(s, v)
                        ins = fn(en)
                        if inc == 16:
                            ins.then_inc(n, 16)
                        elif n in prog.rank[name]:
                            r = prog.rank[name][n]
                            ins.then_inc(prog.csem[name][(r - 1) // prog.EPOCH], 1)
                    if name == 'sp':
                        for sid, val in fin:
                            s, v = prog._resolve(sid, val)
                            en.wait_ge(s, v)
                return body
            block.tensor(run('pe'))
            block.vector(run('dve'))
            block.scalar(run('act'))
            block.gpsimd(run('pool'))
            block.sync(run('sp'))
        self.es.close()


AW = 28928
SMALL0 = 12288 + 10 * 1536
T = 1536
NT = 12
D = 2048
KC = 16
FF = 5632
NJ = 44
PW = 7184
EPS = 1e-6
SEQS = [(0, 256, 0), (256, 256, 0), (512, 1024, 1)]
TBMOD = [0, 1, 1]


class WStream:
    def __init__(self, P, nstage=3, nbf=5):
        self.P = P
        self.ns, self.nb = nstage, nbf
        self.stage = [P.sb("wst%d" % i, [128, 2048], F32) for i in range(nstage)]
        self.bf = [P.sb("wbf%d" % i, [128, 2048], BF16) for i in range(nbf)]
        self.plan = []
        self.issued = 0
        self.used = 0

    def add(self, name, ap, a, b):
        self.plan.append((name, ap, a, b))

    def _issue(self):
        n = self.issued
        name, ap, a, b = self.plan[n]
        s, t = n % self.ns, n % self.nb
        st = self.stage[s][:, 0:a * b]
        P = self.P
        P.dma('sp', st.rearrange("p (a b) -> p a b", b=b), ap, reads=[], writes=[('wst', s)])
        bfv = self.bf[t][:, 0:a * b]
        if name[0] == 'ada' or n % 2 == 1:
            P.I('act', 'copy', bfv, st, reads=[('wst', s)], writes=[('wbf', t)])
        else:
            P.I('pool', 'tensor_copy', bfv, st, reads=[('wst', s)], writes=[('wbf', t)])
        self.issued += 1

    def next(self, name):
        n = self.used
        assert self.plan[n][0] == name, (self.plan[n][0], name)
        while self.issued < min(len(self.plan), n + self.nb - 1):
            self._issue()
        _, _, a, b = self.plan[n]
        t = n % self.nb
        self.used += 1
        return self.bf[t][:, 0:a * b].rearrange("p (a b) -> p a b", b=b), ('wbf', t)


def build_program(opts=None):
    opts = opts or {}
    mixers = opts.get('mixers', ('ret', 'gdn', 'hg', 's5'))
    nlayers = opts.get('nlayers', 2)
    GDN_HEADS[0] = opts.get('gdn_heads', 4)
    nc = bass.Bass("TRN2", target_bir_lowering=False)
    P = Prog(nc)

    def din(name, shape, dt=F32):
        return nc.dram_tensor(name, list(shape), dt, kind="ExternalInput").ap()

    def dout(name, shape, dt=F32):
        return nc.dram_tensor(name, list(shape), dt, kind="ExternalOutput").ap()

    xin = din("xin", [T, D])
    cT_d = din("cT", [128, KC, 2])
    n1w_d = din("n1w", [2, 128, KC])
    n2w_d = din("n2w", [2, 128, KC])
    fnw_d = din("fnw", [128, KC])
    adab_d = din("adab", [2, 128, 96])
    consts_d = din("consts", [128, NCONST])
    ada_w = din("ada_w", [2, D, 6 * D])
    in_proj = din("in_proj", [2, D, PW])
    out_proj = din("out_proj", [2, D, D])
    ffn_w1 = din("ffn_w1", [2, D, FF])
    ffn_w3 = din("ffn_w3", [2, D, FF])
    ffn_w2 = din("ffn_w2", [2, FF, D])
    y_d = dout("y", [T, D])
    xT = nc.dram_tensor("xT_scr", [D, T], F32).ap()

    hT = P.sb("hT", [128, KC, T], BF16)
    arena = P.sb("arena", [128, AW], F32)
    big2 = arena[:, 0:16896].bitcast(BF16).rearrange("p (j t) -> p j t", t=T)
    catT = big2[:, 0:16, :]
    W = WStream(P, nstage=2, nbf=3)
    consts = P.sb("consts", [128, NCONST], F32)
    ident_f = consts[:, C_IDENT:C_IDENT + 128]
    ident_b = P.sb("ident_b", [128, 128], BF16)
    ones_b = P.sb("ones_b", [128, 128], BF16)
    cT = P.sb("cTs", [128, KC, 2], F32)
    csil = P.sb("csil", [128, KC, 2], BF16)
    n1w = P.sb("n1w", [128, 2, KC], F32)
    n2w = P.sb("n2w", [128, 2, KC], F32)
    fnw = P.sb("fnw", [128, KC], F32)
    adab = P.sb("adab", [128, 2, 96], F32)
    mod = P.sb("mod", [128, 96, 2], F32)
    ws = P.sb("ws", [128, 2, KC, 2], F32)
    rstd = arena[:, 24064:25600].rearrange("p (a b) -> p a b", b=512)
    xs = [arena[:, 16896 + 512 * i:16896 + 512 * (i + 1)] for i in range(3)]
    xt = [arena[:, 18432 + 512 * i:18432 + 512 * (i + 1)] for i in range(2)]
    sq = [arena[:, 19456 + 256 * i:19456 + 256 * (i + 1)].bitcast(BF16) for i in range(2)]
    tokbuf = [arena[:, 19968 + 2048 * i:19968 + 2048 * (i + 1)] for i in range(2)]
    banks = [P.ps("bank%d" % i, [128, 512], F32) for i in range(8)]
    st = {'bank': 0, 'xs': 0, 'xt': 0, 'sq': 0, 'alt': 0, 'nbanks': 4}

    def bank():
        b = st['bank']
        st['bank'] = (b + 1) % st['nbanks']
        return banks[b], ('bank', b)

    def rr(name, n):
        i = st.get(name, 0)
        st[name] = (i + 1) % n
        return i

    def evac_eng():
        st['alt'] ^= 1
        return 'act' if st['alt'] else 'dve'

    def copy_op(out, in_, reads, writes, eng=None):
        eng = eng or evac_eng()
        if eng == 'act':
            P.I('act', 'copy', out, in_, reads=reads, writes=writes)
        else:
            P.I(eng, 'tensor_copy', out, in_, reads=reads, writes=writes)

    ctx_pre = {}
    def wview(ap2d):
        return ap2d.rearrange("(kc p) n -> p kc n", p=128)

    def add_ada(l, cb):
        W.add(('ada', l, cb), wview(ada_w[l])[:, :, cb * 128:(cb + 1) * 128], KC, 128)

    for l in range(nlayers):
        if l == 0:
            for cb in range(96):
                add_ada(l, cb)
        for name, c0, n in unit_cols(mixers):
            W.add(('in', l, name), wview(in_proj[l])[:, :, c0:c0 + n], KC, n)
        if 's5' in mixers:
            if 'glu_w' not in ctx_pre:
                ctx_pre['glu_w'] = din("s5_glu_w", [2, 512, 512])
            glu_w = ctx_pre['glu_w']
            for oc in range(4):
                W.add(('glu', l, oc), glu_w[l][:, oc * 128:(oc + 1) * 128].rearrange("(kc p) n -> p kc n", p=128), 4, 128)
        for fc in range(KC):
            W.add(('out', l, fc), wview(out_proj[l])[:, :, fc * 128:(fc + 1) * 128], KC, 128)
        for g in range(2):
            for j in range(22):
                jj = g * 22 + j
                W.add(('w1', l, jj), wview(ffn_w1[l])[:, :, jj * 128:(jj + 1) * 128], KC, 128)
                W.add(('w3', l, jj), wview(ffn_w3[l])[:, :, jj * 128:(jj + 1) * 128], KC, 128)
                if l + 1 < nlayers and jj < 32:
                    for cb in range(3 * jj, 3 * jj + 3):
                        add_ada(l + 1, cb)
            for fc in range(KC):
                for hf in range(2):
                    r0 = (g * 22 + hf * 11) * 128
                    W.add(('w2', l, g, fc, hf),
                          ffn_w2[l][r0:r0 + 11 * 128, fc * 128:(fc + 1) * 128].rearrange("(j p) n -> p j n", p=128), 11, 128)

    P.dma('sp', consts[:], consts_d, writes=['consts'])
    P.dma('sp', cT[:], cT_d, writes=['cT'])
    P.dma('sp', n1w[:], n1w_d.rearrange("l p k -> p l k"), writes=['n1w'])
    P.dma('sp', n2w[:], n2w_d.rearrange("l p k -> p l k"), writes=['n2w'])
    P.dma('sp', fnw[:], fnw_d, writes=['fnw'])
    P.dma('sp', adab[:], adab_d.rearrange("l p k -> p l k"), writes=['adab'])
    P.I('dve', 'tensor_copy', ident_b[:], ident_f, reads=['consts'], writes=['ident_b'])
    P.I('dve', 'memset', ones_b[:], 1.0, writes=['ones_b'])
    P.I('act', 'activation', csil[:], cT[:], AF.Silu, reads=['cT'], writes=['csil'])

    def stats_add(tb, src, srckeys, c0, n, first, last):
        i = rr('sq', 2)
        sqt = sq[i]
        P.I('act', 'activation', sqt[:, 0:n], src, AF.Square, reads=srckeys, writes=[('sq', i)])
        bk = banks[5 + tb]
        P.I('pe', 'matmul', bk[:, c0:c0 + n], ones_b[:], sqt[:, 0:n], start=first, stop=last,
            reads=[('sq', i), 'ones_b'], writes=[('stat', tb)])

    def stats_fin(tb):
        bk = banks[5 + tb]
        r = rstd[:, tb, :]
        P.I('act', 'activation', r, bk[:], AF.Ln, bias=EPS, scale=1.0 / D, reads=[('stat', tb)], writes=[('rstd', tb)])
        P.I('act', 'activation', r, r, AF.Exp, scale=-0.5, reads=[('rstd', tb)], writes=[('rstd', tb)])

    for tt in range(NT):
        tb, c0 = tt // 4, (tt % 4) * 128
        i = tt % 2
        tk = tokbuf[i]
        P.dma('sp', tk[:], xin[tt * 128:(tt + 1) * 128, :], writes=[('tok', i)])
        for q in range(4):
            bk, bkey = bank()
            for a in range(4):
                kc = q * 4 + a
                P.I('pe', 'transpose', bk[:, a * 128:(a + 1) * 128], tk[:, kc * 128:(kc + 1) * 128], ident_f,
                    reads=[('tok', i), 'consts'], writes=[bkey])
            j = rr('xs', 3)
            xsj = xs[j]
            copy_op(xsj[:], bk[:], reads=[bkey], writes=[('xs', j)])
            for a in range(4):
                kc = q * 4 + a
                stats_add(tb, xsj[:, a * 128:(a + 1) * 128], [('xs', j)], c0, 128, kc == 0, kc == KC - 1)
            P.dma('sp', xT[q * 512:(q + 1) * 512, tt * 128:(tt + 1) * 128].rearrange("(a p) t -> p a t", p=128),
                  xsj[:].rearrange("p (a t) -> p a t", t=128), reads=[('xs', j)],
                  writes=[('xT', q * 4 + a, tb) for a in range(4)])
        if tt % 4 == 3:
            stats_fin(tb)

    MODBANK = (banks[4], ('bank', 4))

    def mods_blocks(l, cbs):
        bk, bkey = MODBANK
        for cb in cbs:
            wv, wkey = W.next(('ada', l, cb))
            for kc in range(KC):
                P.I('pe', 'matmul', bk[:, cb * 2:cb * 2 + 2], wv[:, kc, :], csil[:, kc, :], start=(kc == 0), stop=(kc == KC - 1),
                    reads=[wkey, 'csil'], writes=[bkey])

    def mods(l):
        bk, bkey = MODBANK
        if l == 0:
            mods_blocks(l, range(96))
        P.I('dve', 'tensor_tensor', mod[:], bk[:, 0:192].rearrange("p (m r) -> p m r", r=2),
            adab[:, l, :].unsqueeze(2).to_broadcast([128, 96, 2]), op=ALU.add, reads=[bkey, 'adab'], writes=['mod'])
        for sub, nw in ((0, n1w), (1, n2w)):
            m0 = (1 + 3 * sub) * 16
            P.I('dve', 'scalar_tensor_tensor', ws[:, sub, :, :], mod[:, m0:m0 + 16, :], 1.0,
                nw[:, l, :].unsqueeze(2).to_broadcast([128, KC, 2]), op0=ALU.add, op1=ALU.mult,
                reads=['mod', 'n1w', 'n2w'], writes=['ws'])

    def normalize(l, sub):
        for tb in range(3):
            r = TBMOD[tb]
            for fc in range(KC):
                j = rr('xs', 3)
                xsj = xs[j]
                P.dma('sp', xsj[:], xT[fc * 128:(fc + 1) * 128, tb * 512:(tb + 1) * 512], reads=[('xT', fc, tb)], writes=[('xs', j)])
                k = rr('xt', 2)
                xtk = xt[k]
                P.I('dve', 'tensor_tensor', xtk[:], xsj[:], rstd[:, tb, :], op=ALU.mult,
                    reads=[('xs', j), ('rstd', tb)], writes=[('xt', k)])
                sh = mod[:, (3 * sub) * 16 + fc, r:r + 1]
                sc = ws[:, sub, fc, r:r + 1]
                P.I('act', 'activation', hT[:, fc, tb * 512:(tb + 1) * 512], xtk[:], AF.Identity, bias=sh, scale=sc,
                    reads=[('xt', k), 'mod', 'ws'], writes=[('hT', fc, tb)])

    def residual_epilogue(bk, bkey, gate_m, fc, tb, do_stats):
        r = TBMOD[tb]
        j = rr('xs', 3)
        xsj = xs[j]
        P.dma('sp', xsj[:], xT[fc * 128:(fc + 1) * 128, tb * 512:(tb + 1) * 512], reads=[('xT', fc, tb)], writes=[('xs', j)])
        g = mod[:, gate_m * 16 + fc, r:r + 1]
        P.I('dve', 'scalar_tensor_tensor', xsj[:], bk[:], g, xsj[:], op0=ALU.mult, op1=ALU.add,
            reads=[bkey, ('xs', j), 'mod'], writes=[('xs', j)])
        P.dma('sp', xT[fc * 128:(fc + 1) * 128, tb * 512:(tb + 1) * 512], xsj[:], reads=[('xs', j)], writes=[('xT', fc, tb)])
        if do_stats:
            stats_add(tb, xsj[:], [('xs', j)], 0, 512, fc == 0, fc == KC - 1)
            if fc == KC - 1:
                stats_fin(tb)

    def out_projection(l):
        for fc in range(KC):
            wv, wkey = W.next(('out', l, fc))
            for tb in range(3):
                bk, bkey = bank()
                for kc in range(KC):
                    P.I('pe', 'matmul', bk[:], wv[:, kc, :], catT[:, kc, tb * 512:(tb + 1) * 512], start=(kc == 0), stop=(kc == KC - 1),
                        reads=[wkey, ('cat', kc)], writes=[bkey])
                residual_epilogue(bk, bkey, 2, fc, tb, True)

    def ffn(l):
        for g in range(2):
            for j in range(22):
                jj = g * 22 + j
                w1, k1 = W.next(('w1', l, jj))
                w3, k3 = W.next(('w3', l, jj))
                for tb in range(3):
                    ba, ka = bank()
                    bb, kb = bank()
                    for wv, wk, bk, bkey in ((w1, k1, ba, ka), (w3, k3, bb, kb)):
                        for kc in range(KC):
                            P.I('pe', 'matmul', bk[:], wv[:, kc, :], hT[:, kc, tb * 512:(tb + 1) * 512], start=(kc == 0), stop=(kc == KC - 1),
                                reads=[wk, ('hT', kc, tb)], writes=[bkey])
                    k = rr('xt', 2)
                    xtk = xt[k]
                    P.I('act', 'activation', xtk[:], ba[:], AF.Silu, reads=[ka], writes=[('xt', k)])
                    P.I('dve', 'tensor_tensor', big2[:, j, tb * 512:(tb + 1) * 512], xtk[:], bb[:], op=ALU.mult,
                        reads=[kb, ('xt', k)], writes=[('u', j, tb)])
                if l + 1 < nlayers and jj < 32:
                    mods_blocks(l + 1, range(3 * jj, 3 * jj + 3))
            for fc in range(KC):
                wa, ka = W.next(('w2', l, g, fc, 0))
                wb, kb = W.next(('w2', l, g, fc, 1))
                for tb in range(3):
                    bk, bkey = bank()
                    for j in range(22):
                        wv, wk = (wa, ka) if j < 11 else (wb, kb)
                        P.I('pe', 'matmul', bk[:], wv[:, j % 11, :], big2[:, j, tb * 512:(tb + 1) * 512], start=(j == 0), stop=(j == 21),
                            reads=[wk, ('u', j, tb)], writes=[bkey])
                    residual_epilogue(bk, bkey, 5, fc, tb, g == 1)

    def final_out():
        for tt in range(NT):
            tb, c0 = tt // 4, (tt % 4) * 128
            i = tt % 2
            tk = tokbuf[i]
            for q in range(4):
                j = rr('xs', 3)
                xsj = xs[j]
                P.dma('sp', xsj[:].rearrange("p (a t) -> p a t", t=128),
                      xT[q * 512:(q + 1) * 512, tt * 128:(tt + 1) * 128].rearrange("(a p) t -> p a t", p=128),
                      reads=[('xT', q * 4 + a, tb) for a in range(4)], writes=[('xs', j)])
                bk, bkey = bank()
                for a in range(4):
                    kc = q * 4 + a
                    P.I('dve', 'scalar_tensor_tensor', xsj[:, a * 128:(a + 1) * 128], xsj[:, a * 128:(a + 1) * 128], fnw[:, kc:kc + 1],
                        rstd[:, tb, c0:c0 + 128], op0=ALU.mult, op1=ALU.mult, reads=[('xs', j), 'fnw', ('rstd', tb)], writes=[('xs', j)])
                    P.I('pe', 'transpose', bk[:, a * 128:(a + 1) * 128], xsj[:, a * 128:(a + 1) * 128], ident_f,
                        reads=[('xs', j), 'consts'], writes=[bkey])
                P.I('act', 'copy', tk[:, q * 512:(q + 1) * 512], bk[:], reads=[bkey], writes=[('tok', i)])
            P.dma('sp', y_d[tt * 128:(tt + 1) * 128, :], tk[:], reads=[('tok', i)], writes=[('y', tt)])

    ctx = dict(P=P, nc=nc, W=W, hT=hT, catT=catT, big2=big2, banks=banks, bank=bank, consts=consts, ident_b=ident_b,
               ident_f=ident_f, ones_b=ones_b, din=din, dout=dout, rr=rr, st=st, evac_eng=evac_eng, copy_op=copy_op,
               arena=arena, sq=sq, opts=opts)
    mixer_setup(ctx, mixers)
    dbg = opts.get('debug')
    for l in range(nlayers):
        mods(l)
        if dbg and l == 0:
            P.dma('sp', dout("dbg_mod", [128, 96, 2]), mod[:], reads=['mod'], writes=['dbg_mod'])
        normalize(l, 0)
        mixer_units(ctx, l, mixers)
        out_projection(l)
        normalize(l, 1)
        if dbg and l == 0:
            dh = dout("dbg_h2", [128, KC, T], BF16)
            P.dma('sp', dh, hT[:], reads=[('hT', fc, tb) for fc in range(KC) for tb in range(3)], writes=['dbg_h2'])
        ffn(l)
        if dbg and l == 0:
            dx = dout("dbg_x", [D, T])
            P.dma('sp', dx, xT, reads=[('xT', fc, tb) for fc in range(KC) for tb in range(3)], writes=['dbg_x'])
    final_out()
    if 's5' in mixers:
        s5_finish(ctx)
    P.finish()
    return nc


C_IDENT, C_MF, C_MB, C_PERM, C_MFS, C_MBS, C_NEGCM = 0, 128, 256, 384, 512, 640, 768
NCONST = 1280
HD = 128
QSCALE = HD ** -0.5
OFF = dict(rq=0, rk=512, rv=1024, rg=1536, dq=2048, dk=2560, dv=3072, dg=3584, dab=4096, hq=4112, hf=4624, hi=5648,
           hg=6160, su=6672)


def make_consts():
    c = np.zeros((128, NCONST), np.float32)
    c[:, C_IDENT:C_IDENT + 128] = np.eye(128, dtype=np.float32)
    j = np.arange(128)[:, None]
    i = np.arange(128)[None, :]
    same = (j // 32) == (i // 32)
    c[:, C_MF:C_MF + 128] = (same & (j <= i))
    c[:, C_MB:C_MB + 128] = (same & (j >= i))
    c[:, C_PERM:C_PERM + 128] = (j == (i + 64) % 128)
    c[:, C_MFS:C_MFS + 128] = (same & (j < i))
    c[:, C_MBS:C_MBS + 128] = (same & (j > i))
    for cc in range(4):
        c[:, C_NEGCM + 128 * cc + 32 * cc:C_NEGCM + 128 * cc + 32 * cc + 32] = -1.0
    return c


def make_rope():
    l, gw = 1024, 64
    n_freq = HD // 4
    t_row = np.repeat(np.arange(l // gw, dtype=np.float32), gw)
    t_col = np.tile(np.arange(gw, dtype=np.float32), l // gw)
    inv = (10000.0 ** (-np.arange(n_freq, dtype=np.float32) / n_freq)).astype(np.float32)
    ang = np.concatenate([t_row[:, None] * inv, t_col[:, None] * inv], axis=-1).astype(np.float32)
    cos, sin = np.cos(ang).T, np.sin(ang).T
    C = np.concatenate([cos, cos], 0)
    S = np.concatenate([-sin, sin], 0)
    return np.ascontiguousarray(C, np.float32), np.ascontiguousarray(S, np.float32)


def make_rm():
    rm = np.ones((128, T), np.float32)
    rm[:, 0::32] = 0.0
    return rm


GDN_HEADS = [4]


def unit_cols(mixers):
    cols = []
    for h in range(4):
        if 'ret' in mixers:
            for nm in ('rq', 'rk', 'rv', 'rg'):
                cols.append(((nm, h), OFF[nm] + 128 * h, 128))
    if 'gdn' in mixers:
        cols.append((('dab', 0), OFF['dab'], 16))
        for h in range(GDN_HEADS[0]):
            for nm in ('dq', 'dk', 'dv', 'dg'):
                cols.append(((nm, h), OFF[nm] + 128 * h, 128))
    for h in range(4):
        if 'hg' in mixers:
            cols.append((('hq', h), OFF['hq'] + 128 * h, 128))
            cols.append((('hi', h), OFF['hi'] + 128 * h, 128))
            cols.append((('hg', h), OFF['hg'] + 128 * h, 128))
            cols.append((('hf0', h), OFF['hf'] + 128 * h, 128))
            cols.append((('hf1', h), OFF['hf'] + 512 + 128 * h, 128))
    if 's5' in mixers:
        for cc in range(4):
            cols.append((('su', cc), OFF['su'] + 128 * cc, 128))
    return cols


def mixer_setup(ctx, mixers):
    P, din, dout = ctx['P'], ctx['din'], ctx['dout']
    rm_d = din("rm", [128, T])
    ctx['rope_c'] = din("rope_c", [128, 1024])
    ctx['rope_s'] = din("rope_s", [128, 1024])
    rmb = P.sb("rmb", [128, T], BF16)
    ctx['rmb'] = rmb
    slot0 = ctx['arena'][:, 12288:12288 + T]
    P.dma('sp', slot0, rm_d, writes=[('slot', 0)])
    P.I('dve', 'tensor_copy', rmb[:], slot0, reads=[('slot', 0)], writes=['rmb'])
    rdl_d = din("rdl", [128, 16])
    lg = P.sb("ret_lg", [128, 16], F32)
    ctx['ret_lg'] = lg
    P.dma('sp', lg[:], rdl_d, writes=['ret_lg'])
    P.I('act', 'activation', lg[:], lg[:], AF.Exp, scale=-1.0, reads=['ret_lg'], writes=['ret_lg'])
    P.I('act', 'activation', lg[:], lg[:], AF.Ln, bias=1.0, reads=['ret_lg'], writes=['ret_lg'])
    P.I('dve', 'tensor_scalar', lg[:], lg[:], -1.0, None, op0=ALU.mult, reads=['ret_lg'], writes=['ret_lg'])
    hlb_d = din("hlb", [128, 2, 8])
    hp = P.sb("hg_p", [128, 2, 8], F32)
    oml = P.sb("hg_oml", [128, 2, 8], F32)
    noml = P.sb("hg_noml", [128, 2, 8], F32)
    lbf = P.sb("hg_lbf", [128, 2, 8], F32)
    ctx.update(hg_oml=oml, hg_noml=noml, hg_lbf=lbf)
    P.dma('sp', hp[:], hlb_d, writes=['hg_p'])
    P.I('dve', 'tensor_tensor', hp[:, 1, :], hp[:, 1, :], hp[:, 0, :], op=ALU.subtract, reads=['hg_p'], writes=['hg_p'])
    P.I('act', 'activation', hp[:, 1, :], hp[:, 1, :], AF.Sigmoid, reads=['hg_p'], writes=['hg_p'])
    P.I('dve', 'memset', hp[:, 0, :], 0.0, reads=['hg_p'], writes=['hg_p'])
    P.I('dve', 'tensor_scalar', oml[:], hp[:], -1.0, 1.0, op0=ALU.mult, op1=ALU.add, reads=['hg_p'], writes=['hg_oml'])
    P.I('dve', 'tensor_scalar', noml[:], hp[:], 1.0, -1.0, op0=ALU.mult, op1=ALU.add, reads=['hg_p'], writes=['hg_noml'])
    P.I('dve', 'tensor_scalar', lbf[:], hp[:], 1e-30, None, op0=ALU.max, reads=['hg_p'], writes=['hg_lbf'])
    hnw = P.sb("hg_nw", [128, 2], F32)
    ctx['hg_nw'] = hnw
    P.dma('sp', hnw[:], din("hgnw", [128, 2]), writes=['hg_nw'])
    gp = P.sb("gdn_par", [128, 2, 2, 8], F32)
    ctx['gdn_par'] = gp
    P.dma('sp', gp[:], din("gdnp", [128, 2, 2, 8]), writes=['gdn_par'])
    P.I('act', 'activation', gp[:, :, 1, :], gp[:, :, 1, :], AF.Exp, reads=['gdn_par'], writes=['gdn_par'])
    P.I('dve', 'tensor_scalar', gp[:, :, 1, :], gp[:, :, 1, :], -1.0, None, op0=ALU.mult, reads=['gdn_par'], writes=['gdn_par'])
    gcw = P.sb("gdn_cw", [128, 2, 3, 12], F32)
    ctx['gdn_cw'] = gcw
    P.dma('sp', gcw[:], din("gdncw", [128, 2, 3, 12]), writes=['gdn_cw'])
    gnw = P.sb("gdn_nw", [128, 2], F32)
    ctx['gdn_nw'] = gnw
    P.dma('sp', gnw[:], din("gdnnw", [128, 2]), writes=['gdn_nw'])
    ones_f = P.sb("ones_f", [128, 128], F32)
    ctx['ones_f'] = ones_f
    P.I('dve', 'memset', ones_f[:], 1.0, writes=['ones_f'])
    ctx['st_gdn'] = din("st_gdn", [2, 2, 4, 128, 128])
    ctx['new_gdn'] = dout("new_gdn", [2, 2, 2, 4, 128, 128])
    ctx['s5p_d'] = din("s5p", [128, 2, 3, 32])
    ctx['s5x0_d'] = din("s5x0", [128, 2, 2, 32])
    ctx['s5bx'] = din("s5bx", [2, 2, 2, 16, 128, 128])
    ctx['s5cx'] = din("s5cx", [2, 2, 2, 16, 128, 128])
    s5d = P.sb("s5_d", [128, 2, 2, 4], F32)
    ctx['s5_d'] = s5d
    P.dma('sp', s5d[:], din("s5dg", [128, 2, 2, 4]), writes=['s5_d'])
    tau = P.sb("s5_tau", [128, 65], F32)
    ctx['s5_tau'] = tau
    P.dma('sp', tau[:], din("s5tau", [128, 65]), writes=['s5_tau'])
    ctx['s5_fs'] = [P.sb("s5_fs%d" % i, [128, 128], F32) for i in range(2)]
    for i in range(2):
        P.I('dve', 'memset', ctx['s5_fs'][i][:], 0.0, writes=[('s5_fs', i)])
    ctx['new_s5'] = [dout("new_s5re", [128, 128]), dout("new_s5im", [128, 128])]
    ctx['st_ret'] = din("st_ret", [2, 2, 4, 128, 128])
    ctx['st_hg'] = din("st_hg", [2, 2, 4, 128, 128])
    ctx['new_ret'] = dout("new_ret", [2, 2, 2, 4, 128, 128])
    ctx['new_hg'] = dout("new_hg", [2, 2, 2, 4, 128, 128])


def persist(ctx, name, shape, dt):
    if name not in ctx:
        ctx[name] = ctx['P'].sb(name, shape, dt)
    return ctx[name]


def mixer_units(ctx, l, mixers):
    P, catT, st = ctx['P'], ctx['catT'], ctx['st']
    P.barrier()
    st['nbanks'] = 8
    for name, kc0 in (('ret', 0), ('gdn', 4), ('hg', 8), ('s5', 12)):
        if name not in mixers:
            for kc in range(kc0, kc0 + 4):
                P.I('dve', 'memset', catT[:, kc, :], 0.0, writes=[('cat', kc)])
    if 'ret' in mixers:
        for h in range(4):
            gla_unit(ctx, l, 'ret', h)
    if 'gdn' in mixers:
        P.barrier()
        gdn_gates(ctx, l)
        for h in range(4):
            gdn_unit(ctx, l, h)
            if ctx['opts'].get('gdn_heads', 4) <= h + 1:
                break
        P.barrier()
    if 'hg' in mixers:
        for h in range(4):
            gla_unit(ctx, l, 'hg', h)
    if 's5' in mixers:
        P.barrier()
        st['nbanks'] = 5
        st['bank'] = 0
        s5_layer(ctx, l)
    P.barrier()
    st['nbanks'] = 4
    st['bank'] = 0


def gla_unit(ctx, l, kind, h):
    P, W, hT, catT, arena, bank = ctx['P'], ctx['W'], ctx['hT'], ctx['catT'], ctx['arena'], ctx['bank']
    consts, ident_b, ones_b, rmb, sq, rr, copy_op = ctx['consts'], ctx['ident_b'], ctx['ones_b'], ctx['rmb'], ctx['sq'], ctx['rr'], ctx['copy_op']

    def SF(i):
        return arena[:, 12288 + T * i:12288 + T * (i + 1)], ('slot', i)

    def SB(i, half):
        return arena[:, 12288 + T * i:12288 + T * (i + 1)].bitcast(BF16)[:, half * T:(half + 1) * T], ('slot', i)
    small = arena[:, SMALL0:SMALL0 + 1280]
    QS, kQS = SF(0)
    KIN, kKIN = SF(1)
    LOGF, kLOGF = SF(2)
    OACC, kOACC = SF(2)
    CUM, kCUM = SF(3)
    TMP, kTMP = SF(4)
    SG1, kSG1 = SF(5)
    QT = [SB(6, 0), SB(7, 0)]
    KT = [SB(6, 1), SB(7, 1)]
    VTOK, kVTOK = SB(8, 0)
    GS, kGS = SB(8, 1)
    KTOK = [SB(9, 0), SB(9, 1)]
    Sst = [small[:, 128 * i:128 * (i + 1)] for i in range(6)]
    Sbf = [small[:, 768 + 64 * i:768 + 64 * (i + 1)].bitcast(BF16) for i in range(6)]
    TOT = [persist(ctx, 'gla_tot%d' % d, [128, 48], F32) for d in range(2)]
    CD = [persist(ctx, 'gla_cd%d' % d, [128, 48], F32) for d in range(2)]
    PTR = [persist(ctx, 'gla_pt%d' % i, [128, 128], BF16) for i in range(6)]
    MASK = [consts[:, C_MF:C_MF + 128], consts[:, C_MB:C_MB + 128]]
    kc_out = (0 if kind == 'ret' else 8) + h
    sfx = 'r' if kind == 'ret' else 'h'

    def proj(name, evac):
        wv, wkey = W.next(('in', l, (name, h)))
        for tb in range(3):
            bk, bkey = bank()
            for kc in range(KC):
                P.I('pe', 'matmul', bk[:], wv[:, kc, :], hT[:, kc, tb * 512:(tb + 1) * 512], start=(kc == 0), stop=(kc == KC - 1),
                    reads=[wkey, ('hT', kc, tb)], writes=[bkey])
            evac(tb, bk, bkey)

    def tbs(ap, tb):
        return ap[:, tb * 512:(tb + 1) * 512]

    def to_tok(srcb, ksrc, dst, kdst):
        for g in range(3):
            bk, bkey = bank()
            bkb = bk[:].bitcast(BF16)
            for a in range(4):
                tt = g * 4 + a
                P.I('pe', 'transpose', bkb[:, a * 128:(a + 1) * 128], srcb[:, tt * 128:(tt + 1) * 128], ident_b[:],
                    reads=[ksrc, 'ident_b'], writes=[bkey])
            copy_op(dst[:, g * 512:(g + 1) * 512], bkb[:, 0:512], reads=[bkey], writes=[kdst])

    if kind == 'ret':
        def rope_evac(dst, kdst, scale):
            def f(tb, bk, bkey):
                P.I('act', 'activation', tbs(dst, tb), bk[:], AF.Identity, scale=scale, reads=[bkey], writes=[kdst])
                if tb == 0:
                    return
                c0 = (tb - 1) * 512
                b2, k2 = bank()
                P.I('pe', 'matmul', b2[:], consts[:, C_PERM:C_PERM + 128], tbs(dst, tb), start=True, stop=True,
                    reads=[kdst, 'consts'], writes=[k2])
                P.I('dve', 'tensor_tensor', tbs(TMP, tb), tbs(dst, tb), RC[:, c0:c0 + 512], op=ALU.mult, reads=[kdst, kRC], writes=[kTMP])
                P.I('dve', 'tensor_tensor', tbs(dst, tb), b2[:], RS[:, c0:c0 + 512], op=ALU.mult, reads=[k2, kRS], writes=[kdst])
                P.I('dve', 'tensor_tensor', tbs(dst, tb), tbs(dst, tb), tbs(TMP, tb), op=ALU.add, reads=[kdst, kTMP], writes=[kdst])
            return f
        RC, kRC = LOGF[:, 0:1024], kLOGF
        RS, kRS = CUM[:, 0:1024], kCUM
        P.dma('sp', RC, ctx['rope_c'], writes=[kRC])
        P.dma('sp', RS, ctx['rope_s'], writes=[kRS])
        proj('rq', rope_evac(QS, kQS, 1.0))
        proj('rk', rope_evac(KIN, kKIN, QSCALE))
        qscale = 1.0
        vname, gname = 'rv', 'rg'
    else:
        proj('hq', lambda tb, bk, bkey: P.I('act', 'activation', tbs(QS, tb), bk[:], AF.Silu, reads=[bkey], writes=[kQS]))
        qscale = QSCALE
        vname, gname = 'hi', 'hg'
    VTB, kVTB = TMP.bitcast(BF16)[:, 0:T], kTMP
    proj(vname, lambda tb, bk, bkey: copy_op(tbs(VTB, tb), bk[:], reads=[bkey], writes=[kVTB]))
    to_tok(VTB, kVTB, VTOK, kVTOK)
    proj(gname, lambda tb, bk, bkey: P.I('act', 'activation', tbs(GS, tb), bk[:], AF.Silu, reads=[bkey], writes=[kGS]))

    for d in range(2):
        pidx = l * 8 + d * 4 + h
        if kind == 'ret':
            lgap = ctx['ret_lg'][:, pidx:pidx + 1]
            P.I('act', 'activation', LOGF, rmb[:], AF.Identity, scale=0.0, bias=lgap, reads=['rmb', 'ret_lg'], writes=[kLOGF])
            kin, kkin = KIN, kKIN
        else:
            sg, ksg = SG1, kSG1
            proj('hf%d' % d, lambda tb, bk, bkey: P.I('act', 'activation', tbs(SG1, tb), bk[:], AF.Sigmoid, reads=[bkey], writes=[kSG1]))
            oml = ctx['hg_oml'][:, l, d * 4 + h:d * 4 + h + 1]
            noml = ctx['hg_noml'][:, l, d * 4 + h:d * 4 + h + 1]
            lbf = ctx['hg_lbf'][:, l, d * 4 + h:d * 4 + h + 1]
            P.I('dve', 'tensor_scalar', LOGF, sg, oml, lbf, op0=ALU.mult, op1=ALU.add, reads=[ksg, 'hg_oml', 'hg_lbf'], writes=[kLOGF])
            P.I('act', 'activation', LOGF, LOGF, AF.Ln, reads=[kLOGF], writes=[kLOGF])
            P.I('dve', 'tensor_scalar', KIN, sg, noml, oml, op0=ALU.mult, op1=ALU.add, reads=[ksg, 'hg_oml', 'hg_noml'], writes=[kKIN])
            kin, kkin = KIN, kKIN
        P.I('dve', 'tensor_tensor_scan', CUM, rmb[:], LOGF, 0.0, op0=ALU.mult, op1=ALU.add, reads=['rmb', kLOGF], writes=[kCUM])
        c3 = CUM.rearrange("p (c k) -> p c k", k=32)
        kt = ('gla_tot', d)
        P.I('dve', 'tensor_copy', TOT[d][:], CUM[:, 31::32], reads=[kCUM], writes=[kt])
        totb = TOT[d][:].unsqueeze(2).to_broadcast([128, 48, 32])
        if d == 1:
            P.I('dve', 'tensor_tensor', TMP, LOGF, CUM, op=ALU.subtract, reads=[kLOGF, kCUM], writes=[kTMP])
            P.I('dve', 'tensor_tensor', c3, TMP.rearrange("p (c k) -> p c k", k=32), totb, op=ALU.add, reads=[kTMP, kt], writes=[kCUM])
        P.I('act', 'activation', TMP, CUM, AF.Exp, reads=[kCUM], writes=[kTMP])
        P.I('dve', 'scalar_tensor_tensor', QT[d][0], QS, qscale, TMP, op0=ALU.mult, op1=ALU.mult, reads=[kQS, kTMP], writes=[QT[d][1]])
        P.I('act', 'activation', TMP, CUM, AF.Exp, scale=-1.0, reads=[kCUM, QT[d][1]], writes=[kTMP])
        P.I('dve', 'tensor_tensor', KT[d][0], kin, TMP, op=ALU.mult, reads=[kkin, kTMP], writes=[KT[d][1]])
        P.I('dve', 'tensor_tensor', TMP.rearrange("p (c k) -> p c k", k=32), totb, c3, op=ALU.subtract, reads=[kCUM, kt, KT[d][1]], writes=[kTMP])
        P.I('act', 'activation', TMP, TMP, AF.Exp, reads=[kTMP], writes=[kTMP])
        KTLb, kKTLb = LOGF.bitcast(BF16)[:, 0:T], kLOGF
        P.I('dve', 'tensor_tensor', KTLb, kin, TMP, op=ALU.mult, reads=[kkin, kTMP, kLOGF], writes=[kKTLb])
        P.I('act', 'activation', CD[d][:], TOT[d][:], AF.Exp, reads=[kt], writes=[('gla_cd', d)])
        to_tok(KTLb, kKTLb, KTOK[d][0], KTOK[d][1])

    VTOK3, kVTOK3 = SB(5, 0)
    P.I('dve', 'tensor_scalar', VTOK3, VTOK, consts[:, C_MF + 127:C_MF + 128], None, op0=ALU.mult,
        reads=[kVTOK, 'consts'], writes=[kVTOK3])
    P.I('dve', 'memset', OACC[:, 0:2], 0.0, reads=[], writes=[kOACC])
    st_in = ctx['st_ret'] if kind == 'ret' else ctx['st_hg']
    st_out = ctx['new_ret'] if kind == 'ret' else ctx['new_hg']
    written = set()

    def chain(d, si):
        s0, sl, row = SEQS[si]
        ci = d * 3 + si
        S, kS = Sst[ci], ('gla_S', ci)
        Sb, kSb = Sbf[ci], ('gla_Sb', ci)
        if row == 0:
            P.I('dve', 'memset', S, 0.0, writes=[kS])
        else:
            P.dma('sp', S, st_in[l, d, h], writes=[kS])
        copy_op(Sb, S, reads=[kS], writes=[kSb], eng='act')
        tiles = list(range(s0 // 128, (s0 + sl) // 128))
        if d == 1:
            tiles = tiles[::-1]
        for tt in tiles:
            t0 = tt * 128
            scb, ksc = ctx['banks'][ci // 4][:, (ci % 4) * 128:(ci % 4) * 128 + 128], ('bank', ci // 4)
            P.I('pe', 'matmul', scb[:, 0:128], KT[d][0][:, t0:t0 + 128], QT[d][0][:, t0:t0 + 128], start=True, stop=True,
                reads=[KT[d][1], QT[d][1]], writes=[ksc])
            PT, kPT = PTR[ci], ('gla_pt', ci)
            yield
            P.I('dve', 'tensor_tensor', PT[:], scb[:, 0:128], MASK[d], op=ALU.mult, reads=[ksc, 'consts'], writes=[kPT])
            yield
            ob, kob = ctx['banks'][2 + ci], ('bank', 2 + ci)
            P.I('pe', 'matmul', ob[:, 0:128], VTOK[:, t0:t0 + 128], PT[:], start=True, stop=False, reads=[kVTOK, kPT], writes=[kob])
            chunks = [0, 1, 2, 3] if d == 0 else [3, 2, 1, 0]
            for n, c in enumerate(chunks):
                gc = tt * 4 + c
                P.I('pe', 'matmul', ob[:, 32 * c:32 * c + 32], Sb, QT[d][0][:, t0 + 32 * c:t0 + 32 * c + 32], start=False, stop=(n == 3),
                    reads=[kSb, QT[d][1]], writes=[kob])
                kvb, kkv = scb, ksc
                if c < 3:
                    P.I('pe', 'matmul', kvb[:, 0:128], KTOK[d][0][32 * c:32 * c + 32, t0:t0 + 128], VTOK[32 * c:32 * c + 32, t0:t0 + 128],
                        start=True, stop=True, reads=[KTOK[d][1], kVTOK], writes=[kkv])
                else:
                    P.I('pe', 'matmul', kvb[:, 0:128], KTOK[d][0][64:128, t0:t0 + 128], VTOK3[64:128, t0:t0 + 128],
                        start=True, stop=True, reads=[KTOK[d][1], kVTOK3], writes=[kkv])
                yield
                P.I('dve', 'scalar_tensor_tensor', S, S, CD[d][:, gc:gc + 1], kvb[:, 0:128], op0=ALU.mult, op1=ALU.add,
                    reads=[kS, kkv, ('gla_cd', d)], writes=[kS])
                yield
                copy_op(Sb, S, reads=[kS], writes=[kSb], eng='act')
                yield
            if tt not in written:
                written.add(tt)
                copy_op(OACC[:, t0:t0 + 128], ob[:, 0:128], reads=[kob, kOACC], writes=[(kOACC[0], kOACC[1], tt)], eng='act')
            else:
                P.I('dve', 'tensor_tensor', OACC[:, t0:t0 + 128], OACC[:, t0:t0 + 128], ob[:, 0:128], op=ALU.add,
                    reads=[kob, kOACC, (kOACC[0], kOACC[1], tt)], writes=[(kOACC[0], kOACC[1], tt)])
            yield
        if row == 0:
            P.dma('sp', st_out[si, l, d, h], S, reads=[kS], writes=[('st_out', kind, si, l, d, h)])

    gens = [chain(d, si) for si in range(3) for d in range(2)]
    ctx['st']['nbanks'], ctx['st']['bank'] = 2, 0
    while gens:
        for g in list(gens):
            try:
                next(g)
            except StopIteration:
                gens.remove(g)
    ctx['st']['nbanks'] = 8

    for tb in range(3):
        okeys = [(kOACC[0], kOACC[1], tt) for tt in range(tb * 4, tb * 4 + 4)]
        okr = okeys + [kOACC]
        o = tbs(OACC, tb)
        if kind == 'ret':
            ob16 = tbs(TMP.bitcast(BF16)[:, 0:T], tb)
            copy_op(ob16, o, reads=okr, writes=[kTMP], eng='dve')
            bm, kbm = bank()
            P.I('pe', 'matmul', bm[:], ones_b[:], ob16, start=True, stop=True, reads=[kTMP, 'ones_b'], writes=[kbm])
            P.I('dve', 'scalar_tensor_tensor', o, bm[:], -1.0 / HD, o, op0=ALU.mult, op1=ALU.add, reads=[kbm] + okr, writes=okeys)
        i = rr('sq', 2)
        P.I('act', 'activation', sq[i][:, 0:512], o, AF.Square, reads=okr, writes=[('sq', i)])
        bs, kbs = bank()
        P.I('pe', 'matmul', bs[:], ones_b[:], sq[i][:, 0:512], start=True, stop=True, reads=[('sq', i), 'ones_b'], writes=[kbs])
        rt = tbs(CUM, tb)
        P.I('act', 'activation', rt, bs[:], AF.Ln, bias=EPS, scale=1.0 / HD, reads=[kbs], writes=[kCUM])
        P.I('act', 'activation', rt, rt, AF.Exp, scale=-0.5, reads=[kCUM], writes=[kCUM])
        if kind == 'ret':
            P.I('dve', 'tensor_tensor', o, o, rt, op=ALU.mult, reads=okr + [kCUM], writes=okeys)
        else:
            P.I('dve', 'scalar_tensor_tensor', o, o, ctx['hg_nw'][:, l:l + 1], rt, op0=ALU.mult, op1=ALU.mult,
                reads=okr + [kCUM, 'hg_nw'], writes=okeys)
        P.I('dve', 'tensor_tensor', catT[:, kc_out, tb * 512:(tb + 1) * 512], o, tbs(GS, tb), op=ALU.mult,
            reads=okr + [kGS], writes=[('cat', kc_out)])


def gdn_gates(ctx, l):
    P, W, hT, arena, bank, consts, ident_f = ctx['P'], ctx['W'], ctx['hT'], ctx['arena'], ctx['bank'], ctx['consts'], ctx['ident_f']
    small = arena[:, SMALL0:SMALL0 + 1280]
    names = ('LA', 'BETA', 'EG', 'EGL', 'BEG', 'EGL3')
    G = {n: small[:, 96 * i:96 * (i + 1)].rearrange("p (t c) -> p t c", c=8) for i, n in enumerate(names)}
    ctx['gdn_g'] = G
    ABT = arena[0:16, 12288:12288 + T]
    kABT = ('slot', 0)
    ABK = arena[:, 12288 + T:12288 + T + 192].rearrange("p (t c) -> p t c", c=16)
    kABK = ('slot', 1)
    wv, wkey = W.next(('in', l, ('dab', 0)))
    for tb in range(3):
        bk, bkey = bank()
        for kc in range(KC):
            P.I('pe', 'matmul', bk[0:16, :], wv[:, kc, :], hT[:, kc, tb * 512:(tb + 1) * 512], start=(kc == 0), stop=(kc == KC - 1),
                reads=[wkey, ('hT', kc, tb)], writes=[bkey])
        P.I('act', 'copy', ABT[:, tb * 512:(tb + 1) * 512], bk[0:16, :], reads=[bkey], writes=[kABT])
    bk, bkey = bank()
    for tt in range(NT):
        P.I('pe', 'transpose', bk[:, tt * 16:(tt + 1) * 16], ABT[:, tt * 128:(tt + 1) * 128], ident_f[0:16, 0:16],
            reads=[kABT, 'consts'], writes=[bkey])
    P.I('dve', 'tensor_copy', ABK, bk[:, 0:192].rearrange("p (t c) -> p t c", c=16), reads=[bkey], writes=[kABK])
    gp = ctx['gdn_par']
    LA, BETA = G['LA'], G['BETA']
    dtb = gp[:, l, 0, :].unsqueeze(1).to_broadcast([128, NT, 8])
    nea = gp[:, l, 1, :].unsqueeze(1).to_broadcast([128, NT, 8])
    kg = 'gdn_gate'
    P.I('dve', 'tensor_tensor', LA, ABK[:, :, 0:8], dtb, op=ALU.add, reads=[kABK, 'gdn_par'], writes=[kg])
    P.I('act', 'activation', LA, LA, AF.Exp, reads=[kg], writes=[kg])
    P.I('act', 'activation', LA, LA, AF.Ln, bias=1.0, reads=[kg], writes=[kg])
    P.I('dve', 'tensor_tensor', LA, LA, nea, op=ALU.mult, reads=[kg, 'gdn_par'], writes=[kg])
    P.I('act', 'activation', BETA, ABK[:, :, 8:16], AF.Sigmoid, reads=[kABK], writes=[kg])
    b1, k1 = bank()
    U = [consts[:, C_MF:C_MF + 128], consts[:, C_MB:C_MB + 128]]
    LS = [consts[:, C_MBS:C_MBS + 128], consts[:, C_MFS:C_MFS + 128]]
    for tt in range(NT):
        for d in range(2):
            P.I('pe', 'matmul', b1[:, tt * 8 + d * 4:tt * 8 + d * 4 + 4], U[d], LA[:, tt, d * 4:d * 4 + 4], start=True, stop=True,
                reads=[kg, 'consts'], writes=[k1])
            P.I('pe', 'matmul', b1[:, 96 + tt * 8 + d * 4:96 + tt * 8 + d * 4 + 4], LS[d], LA[:, tt, d * 4:d * 4 + 4], start=True, stop=True,
                reads=[kg, 'consts'], writes=[k1])
    P.I('act', 'activation', G['EG'], b1[:, 0:96].rearrange("p (t c) -> p t c", c=8), AF.Exp, reads=[k1], writes=[kg])
    P.I('act', 'activation', G['EGL'], b1[:, 96:192].rearrange("p (t c) -> p t c", c=8), AF.Exp, reads=[k1], writes=[kg])
    P.I('dve', 'tensor_tensor', G['BEG'], G['EG'], BETA, op=ALU.mult, reads=[kg], writes=[kg])
    P.I('dve', 'tensor_scalar', G['EGL3'], G['EGL'], consts[:, C_MF + 127:C_MF + 128], None, op0=ALU.mult, reads=[kg, 'consts'], writes=[kg])


def gdn_unit(ctx, l, h):
    P, W, hT, catT, arena, bank = ctx['P'], ctx['W'], ctx['hT'], ctx['catT'], ctx['arena'], ctx['bank']
    consts, ident_b, ones_b, ones_f, sq, rr, copy_op = ctx['consts'], ctx['ident_b'], ctx['ones_b'], ctx['ones_f'], ctx['sq'], ctx['rr'], ctx['copy_op']
    ident_f = ctx['ident_f']
    G = ctx['gdn_g']
    kg = 'gdn_gate'

    def SF(i):
        return arena[:, 12288 + T * i:12288 + T * (i + 1)], ('slot', i)

    def SB(i, half):
        return arena[:, 12288 + T * i:12288 + T * (i + 1)].bitcast(BF16)[:, half * T:(half + 1) * T], ('slot', i)
    X, kX = SF(0)
    Y, kY = SF(1)
    OACC, kOACC = SF(2)
    QT, kQT = SB(3, 0)
    KT, kKT = SB(3, 1)
    VTOK, kVTOK = SB(4, 0)
    KTOK, kKTOK = SB(4, 1)
    GS, kGS = SB(5, 0)
    VTB, kVTB = SB(5, 1)
    BV, kBV = SB(6, 0)
    KBG, kKBG = SB(6, 1)
    KTL, kKTL = SB(7, 0)
    KTL3, kKTL3 = SB(7, 1)
    mat = arena[:, 12288 + 8 * T:12288 + 10 * T]
    smallr = arena[:, SMALL0:SMALL0 + 1280]
    cnt = {'n': 0}

    def MF32(name):
        o = cnt['n']
        cnt['n'] += 128
        return mat[:, o:o + 128], ('gm', name)

    def MB16(name, n=128):
        o = cnt['n']
        cnt['n'] += n // 2
        return mat[:, o:o + n // 2].bitcast(BF16), ('gm', name)
    cw = ctx['gdn_cw']
    MINC = [consts[:, C_MF:C_MF + 128], consts[:, C_MB:C_MB + 128]]
    MSTR = [consts[:, C_MFS:C_MFS + 128], consts[:, C_MBS:C_MBS + 128]]
    U = MINC
    LS = [consts[:, C_MBS:C_MBS + 128], consts[:, C_MFS:C_MFS + 128]]

    def tbs(ap, tb):
        return ap[:, tb * 512:(tb + 1) * 512]

    def proj(name, evac):
        wv, wkey = W.next(('in', l, (name, h)))
        for tb in range(3):
            bk, bkey = bank()
            for kc in range(KC):
                P.I('pe', 'matmul', bk[:], wv[:, kc, :], hT[:, kc, tb * 512:(tb + 1) * 512], start=(kc == 0), stop=(kc == KC - 1),
                    reads=[wkey, ('hT', kc, tb)], writes=[bkey])
            evac(tb, bk, bkey)

    def to_tok(srcb, ksrc, dst, kdst):
        for g in range(3):
            bk, bkey = bank()
            bkb = bk[:].bitcast(BF16)
            for a in range(4):
                tt = g * 4 + a
                P.I('pe', 'transpose', bkb[:, a * 128:(a + 1) * 128], srcb[:, tt * 128:(tt + 1) * 128], ident_b[:],
                    reads=[ksrc, 'ident_b'], writes=[bkey])
            copy_op(dst[:, g * 512:(g + 1) * 512], bkb[:, 0:512], reads=[bkey], writes=[kdst])

    def conv_silu(which):
        ci = which * 4 + h
        w0, w1, w2 = (cw[:, l, t, ci:ci + 1] for t in range(3))
        P.I('act', 'activation', Y, X, AF.Identity, scale=w1, reads=[kX, 'gdn_cw'], writes=[kY])
        for s0, sl, _ in SEQS:
            a, b = s0, s0 + sl
            P.I('dve', 'scalar_tensor_tensor', Y[:, a + 1:b], X[:, a:b - 1], w0, Y[:, a + 1:b], op0=ALU.mult, op1=ALU.add,
                reads=[kX, kY, 'gdn_cw'], writes=[kY])
            P.I('dve', 'scalar_tensor_tensor', Y[:, a:b - 1], X[:, a + 1:b], w2, Y[:, a:b - 1], op0=ALU.mult, op1=ALU.add,
                reads=[kX, kY, 'gdn_cw'], writes=[kY])
        P.I('act', 'activation', Y, Y, AF.Silu, reads=[kY], writes=[kY])

    def l2n(dst, kdst, scale):
        for tb in range(3):
            i = rr('sq', 2)
            P.I('act', 'activation', sq[i][:, 0:512], tbs(Y, tb), AF.Square, reads=[kY], writes=[('sq', i)])
            bs, kbs = bank()
            P.I('pe', 'matmul', bs[:], ones_b[:], sq[i][:, 0:512], start=True, stop=True, reads=[('sq', i), 'ones_b'], writes=[kbs])
            P.I('act', 'activation', tbs(X, tb), bs[:], AF.Ln, bias=EPS, reads=[kbs], writes=[kX])
            P.I('act', 'activation', tbs(X, tb), tbs(X, tb), AF.Exp, scale=-0.5, reads=[kX], writes=[kX])
            P.I('dve', 'scalar_tensor_tensor', tbs(dst, tb), tbs(Y, tb), scale, tbs(X, tb), op0=ALU.mult, op1=ALU.mult,
                reads=[kY, kX], writes=[kdst])

    xevac = lambda tb, bk, bkey: copy_op(tbs(X, tb), bk[:], reads=[bkey], writes=[kX])
    P.barrier()
    if ctx['opts'].get('gdn_stop', 99) <= 1:
        for nm in ('dq', 'dk', 'dv', 'dg'):
            W.next(('in', l, (nm, h)))
        return
    proj('dq', xevac)
    conv_silu(0)
    l2n(QT, kQT, QSCALE)
    proj('dk', xevac)
    conv_silu(1)
    l2n(KT, kKT, 1.0)
    proj('dv', xevac)
    conv_silu(2)
    P.I('dve', 'tensor_copy', VTB, Y, reads=[kY], writes=[kVTB])
    to_tok(VTB, kVTB, VTOK, kVTOK)
    to_tok(KT, kKT, KTOK, kKTOK)
    proj('dg', lambda tb, bk, bkey: P.I('act', 'activation', tbs(GS, tb), bk[:], AF.Silu, reads=[bkey], writes=[kGS]))

    STOP = ctx['opts'].get('gdn_stop', 99)
    if STOP <= 2:
        return
    st_in, st_out = ctx['st_gdn'], ctx['new_gdn']
    written = set()
    P.I('dve', 'memset', OACC[:, 0:2], 0.0, writes=[kOACC])
    v3 = lambda ap: ap.rearrange("p (t e) -> p t e", e=128)

    P.barrier()
    regA = arena[:, 12288:12288 + 2 * T]
    regB = arena[:, 12288 + 8 * T:12288 + 10 * T]

    def carve(reg, off, n):
        return reg[:, off:off + n]

    def mkset(i):
        if i == 0:
            sc, rs = carve(regA, 0, 1088), carve(regA, 1088, 640)
        elif i == 1:
            sc, rs = carve(regB, 0, 1088), carve(regB, 1088, 640)
        else:
            sc, rs = carve(regA, 1728, 1088), carve(regB, 1728, 640)
        k = lambda n: ('gm', i, n)
        f32 = lambda r, o: r[:, o:o + 128]
        b16 = lambda r, o, n=128: r[:, o:o + n // 2].bitcast(BF16)
        return dict(idx=i,
            LAU=(f32(sc, 0), k('lau')), IDB=(f32(sc, 128), k('idb')), E=(f32(sc, 256), k('e')), EI=(f32(sc, 384), k('ei')),
            ES=(f32(sc, 512), k('es')), PF=(f32(sc, 640), k('pf')),
            Bm=[(b16(sc, 768), k('b0')), (b16(sc, 832), k('b1'))], Am=[(b16(sc, 896), k('a0')), (b16(sc, 960), k('a1'))],
            PB=(b16(sc, 1024), k('pb')),
            TT=(b16(rs, 0), k('tt')), ATT=(b16(rs, 64), k('att')), NW=(b16(rs, 128, 512), k('nw')), QG=(b16(rs, 384), k('qg')),
            EGB=(f32(rs, 448), k('egb')), VN=(b16(rs, 576), k('vn')))
    SETS = [mkset(i) for i in range(3)]
    STS = [dict(S=(smallr[:, 576 + 128 * si:576 + 128 * (si + 1)], ('gm', 'S%d' % si)),
                SB=(smallr[:, 960 + 64 * si:960 + 64 * (si + 1)].bitcast(BF16), ('gm', 'sb%d' % si))) for si in range(3)]

    for d in range(2):
        col = d * 4 + h
        bc = lambda n: G[n][:, :, col:col + 1].to_broadcast([128, NT, 128])
        P.I('dve', 'tensor_tensor', v3(BV), v3(VTOK), bc('BETA'), op=ALU.mult, reads=[kVTOK, kg], writes=[kBV])
        P.I('dve', 'tensor_tensor', v3(KBG), v3(KTOK), bc('BEG'), op=ALU.mult, reads=[kKTOK, kg], writes=[kKBG])
        P.I('dve', 'tensor_tensor', v3(KTL), v3(KTOK), bc('EGL'), op=ALU.mult, reads=[kKTOK, kg], writes=[kKTL])
        P.I('dve', 'tensor_tensor', v3(KTL3), v3(KTOK), bc('EGL3'), op=ALU.mult, reads=[kKTOK, kg], writes=[kKTL3])

        def prep(tt, Z, d=d, col=col):
            t0 = tt * 128
            la = G['LA'][:, tt, col:col + 1]
            beta = G['BETA'][:, tt, col:col + 1]
            (LAU, kLAU), (IDB, kIDB), (E, kE), (EI, kEI), (ES, kES), (PF, kPF) = Z['LAU'], Z['IDB'], Z['E'], Z['EI'], Z['ES'], Z['PF']
            Bm, Am = Z['Bm'], Z['Am']
            PB, kPB = Z['PB']
            P.I('act', 'activation', LAU, U[d], AF.Identity, scale=la, reads=['consts', kg], writes=[kLAU])
            P.I('act', 'activation', IDB, ident_f, AF.Identity, scale=beta, reads=['consts', kg], writes=[kIDB])
            bg, kbg = bank()
            P.I('pe', 'matmul', bg[:, 0:128], LS[d], LAU, start=True, stop=True, reads=['consts', kLAU], writes=[kbg])
            P.I('pe', 'matmul', bg[:, 128:256], ones_f[:], IDB, start=True, stop=True, reads=['ones_f', kIDB], writes=[kbg])
            P.I('pe', 'matmul', bg[:, 256:384], ones_f[:], LAU, start=True, stop=True, reads=['ones_f', kLAU], writes=[kbg])
            bq, kbq = bank()
            P.I('pe', 'matmul', bq[:, 0:128], KT[:, t0:t0 + 128], KT[:, t0:t0 + 128], start=True, stop=True, reads=[kKT], writes=[kbq])
            P.I('pe', 'matmul', bq[:, 128:256], KT[:, t0:t0 + 128], QT[:, t0:t0 + 128], start=True, stop=True, reads=[kKT, kQT], writes=[kbq])
            yield
            P.I('act', 'activation', E, bg[:, 0:128], AF.Exp, reads=[kbg], writes=[kE])
            EGB, kEGB = Z['EGB']
            P.I('act', 'activation', EGB, bg[:, 256:384], AF.Exp, reads=[kbg], writes=[kEGB])
            P.I('pool', 'tensor_tensor', EI, E, MINC[d], op=ALU.mult, reads=[kE, 'consts'], writes=[kEI])
            P.I('pool', 'tensor_tensor', ES, E, MSTR[d], op=ALU.mult, reads=[kE, 'consts'], writes=[kES])
            yield
            P.I('dve', 'tensor_tensor', ES, ES, bg[:, 128:256], op=ALU.mult, reads=[kES, kbg], writes=[kES])
            (B0, kB0), (A0, kA0) = Bm[0], Am[0]
            P.I('dve', 'tensor_tensor', B0, bq[:, 0:128], ES, op=ALU.mult, reads=[kbq, kES], writes=[kB0])
            ATT, kATT = Z['ATT']
            P.I('dve', 'tensor_tensor', ATT, bq[:, 128:256], EI, op=ALU.mult, reads=[kbq, kEI], writes=[kATT])
            QG, kQG = Z['QG']
            P.I('dve', 'tensor_tensor', QG, QT[:, t0:t0 + 128], EGB, op=ALU.mult, reads=[kQT, kEGB], writes=[kQG])
            bt, kbt = bank()
            btb = bt[:].bitcast(BF16)
            P.I('pe', 'transpose', btb[:, 0:128], B0, ident_b[:], reads=[kB0, 'ident_b'], writes=[kbt])
            yield
            copy_op(A0, btb[:, 0:128], reads=[kbt], writes=[kA0], eng='act')
            P.I('dve', 'tensor_tensor', PF, ident_f, B0, op=ALU.subtract, reads=['consts', kB0], writes=[kPF])
            P.I('act', 'copy', PB, PF, reads=[kPF], writes=[kPB])
            cur = 0
            for stg in range(4):
                (Bc, kBc), (Ac, kAc) = Bm[cur], Am[cur]
                (Bn, kBn), (An, kAn) = Bm[1 - cur], Am[1 - cur]
                bn, kbn = bank()
                P.I('pe', 'matmul', bn[:, 0:128], Bc, Ac, start=True, stop=True, reads=[kBc, kAc], writes=[kbn])
                bn2, kbn2 = bank()
                if stg < 3:
                    P.I('pe', 'matmul', bn2[:, 0:128], Ac, Bc, start=True, stop=True, reads=[kBc, kAc], writes=[kbn2])
                yield
                copy_op(An, bn[:, 0:128], reads=[kbn], writes=[kAn], eng='act')
                if stg < 3:
                    copy_op(Bn, bn2[:, 0:128], reads=[kbn2], writes=[kBn], eng='dve')
                bp, kbp = bank()
                P.I('pe', 'matmul', bp[:, 0:128], An, PB, start=True, stop=True, reads=[kAn, kPB], writes=[kbp])
                yield
                P.I('dve', 'tensor_tensor', PF, PF, bp[:, 0:128], op=ALU.add, reads=[kPF, kbp], writes=[kPF])
                if stg < 3:
                    P.I('act', 'copy', PB, PF, reads=[kPF], writes=[kPB])
                cur = 1 - cur
            TT, kTT = Z['TT']
            P.I('act', 'copy', TT, PF, reads=[kPF], writes=[kTT])
            bw, kbw = bank()
            P.I('pe', 'matmul', bw[:, 0:128], KBG[:, t0:t0 + 128], TT, start=True, stop=True, reads=[kKBG, kTT], writes=[kbw])
            yield
            NW, kNW = Z['NW']
            P.I('dve', 'tensor_tensor', NW.rearrange("p (c i) -> p c i", i=128), bw[:, 0:128].unsqueeze(1).to_broadcast([128, 4, 128]),
                consts[:, C_NEGCM:C_NEGCM + 512].rearrange("p (c i) -> p c i", i=128), op=ALU.mult, reads=[kbw, 'consts'], writes=[kNW])

        def recur(tt, Z, ST, d=d):
            t0 = tt * 128
            (S, kS), (Sb, kSb) = ST['S'], ST['SB']
            (TT, kTT), (ATT, kATT), (NW, kNW), (QG, kQG), (EGB, kEGB), (VN, kVN) = Z['TT'], Z['ATT'], Z['NW'], Z['QG'], Z['EGB'], Z['VN']
            bi = Z['idx']
            vn, kvn = ctx['banks'][2 + 2 * bi], ('bank', 2 + 2 * bi)
            ob, kob = ctx['banks'][3 + 2 * bi], ('bank', 3 + 2 * bi)
            P.I('pe', 'matmul', vn[:, 0:128], TT, BV[:, t0:t0 + 128], start=True, stop=True, reads=[kTT, kBV], writes=[kvn])
            chunks = [0, 1, 2, 3] if d == 0 else [3, 2, 1, 0]
            for n, c in enumerate(chunks):
                P.I('pe', 'matmul', vn[:, 0:128], NW[:, 128 * c:128 * c + 128], Sb, start=False, stop=True, skip_group_check=True,
                    reads=[kNW, kSb], writes=[kvn])
                P.I('pe', 'matmul', ob[:, 32 * c:32 * c + 32], Sb, QG[:, 32 * c:32 * c + 32], start=(n == 0), stop=False,
                    reads=[kSb, kQG], writes=[kob])
                yield
                copy_op(VN, vn[:, 0:128], reads=[kvn], writes=[kVN], eng='act')
                yield
                kvb, kkv = ctx['banks'][0][:, bi * 128:bi * 128 + 128], ('bank', 0)
                if c < 3:
                    P.I('pe', 'matmul', kvb[:, 0:128], KTL[32 * c:32 * c + 32, t0:t0 + 128], VN[32 * c:32 * c + 32, :], start=True, stop=True,
                        reads=[kKTL, kVN], writes=[kkv])
                else:
                    P.I('pe', 'matmul', kvb[:, 0:128], KTL3[64:128, t0:t0 + 128], VN[64:128, :], start=True, stop=True,
                        reads=[kKTL3, kVN], writes=[kkv])
                yield
                cc = 32 * c + (31 if d == 0 else 0)
                P.I('dve', 'scalar_tensor_tensor', S, S, EGB[:, cc:cc + 1], kvb[:, 0:128], op0=ALU.mult, op1=ALU.add,
                    reads=[kS, kkv, kEGB], writes=[kS])
                yield
                copy_op(Sb, S, reads=[kS], writes=[kSb], eng='act')
                yield
            P.I('pe', 'matmul', ob[:, 0:128], VN, ATT, start=False, stop=True, reads=[kVN, kATT], writes=[kob])
            yield
            if tt not in written:
                written.add(tt)
                copy_op(OACC[:, t0:t0 + 128], ob[:, 0:128], reads=[kob, kOACC], writes=[(kOACC[0], kOACC[1], tt)], eng='act')
            else:
                P.I('dve', 'tensor_tensor', OACC[:, t0:t0 + 128], OACC[:, t0:t0 + 128], ob[:, 0:128], op=ALU.add,
                    reads=[kob, kOACC, (kOACC[0], kOACC[1], tt)], writes=[(kOACC[0], kOACC[1], tt)])

        seqt = []
        for si, (s0, sl, row) in enumerate(SEQS):
            tiles = list(range(s0 // 128, (s0 + sl) // 128))
            if d == 1:
                tiles = tiles[::-1]
            seqt.append([(si, tt, j == 0, j == len(tiles) - 1) for j, tt in enumerate(tiles)])
        groups = [[seqt[0][0], seqt[1][0], seqt[2][0]], [seqt[0][1], seqt[1][1], seqt[2][1]],
                  seqt[2][2:5], seqt[2][5:8]]

        def rr_run(gens):
            live = list(gens)
            while live:
                for g in list(live):
                    try:
                        next(g)
                    except StopIteration:
                        live.remove(g)

        def recur_full(i, si, tt, first, last):
            ST = STS[si]
            (S, kS), (Sb, kSb) = ST['S'], ST['SB']
            if first:
                if SEQS[si][2] == 0:
                    P.I('dve', 'memset', S, 0.0, writes=[kS])
                else:
                    P.dma('sp', S, st_in[l, d, h], writes=[kS])
                copy_op(Sb, S, reads=[kS], writes=[kSb], eng='act')
            yield from recur(tt, SETS[i], ST)
            if last and SEQS[si][2] == 0:
                P.dma('sp', st_out[si, l, d, h], S, reads=[kS], writes=[('st_out', 'gdn', si, l, d, h)])

        for grp in groups:
            rr_run([prep(tt, SETS[i]) for i, (si, tt, first, last) in enumerate(grp)])
            rg = [recur_full(i, si, tt, first, last) for i, (si, tt, first, last) in enumerate(grp)]
            ctx['st']['nbanks'], ctx['st']['bank'] = 2, 0
            if len(set(si for si, _, _, _ in grp)) == len(grp):
                rr_run(rg)
            else:
                for g in rg:
                    rr_run([g])
            ctx['st']['nbanks'] = 8

    for tb in range(3):
        if STOP <= 5:
            break
        okeys = [(kOACC[0], kOACC[1], tt) for tt in range(tb * 4, tb * 4 + 4)]
        okr = okeys + [kOACC]
        o = tbs(OACC, tb)
        i = rr('sq', 2)
        P.I('act', 'activation', sq[i][:, 0:512], o, AF.Square, reads=okr, writes=[('sq', i)])
        bs, kbs = bank()
        P.I('pe', 'matmul', bs[:], ones_b[:], sq[i][:, 0:512], start=True, stop=True, reads=[('sq', i), 'ones_b'], writes=[kbs])
        rt = tbs(arena[:, 12288 + 6 * T:12288 + 7 * T], tb)
        P.I('act', 'activation', rt, bs[:], AF.Ln, bias=EPS, scale=1.0 / HD, reads=[kbs], writes=[kBV])
        P.I('act', 'activation', rt, rt, AF.Exp, scale=-0.5, reads=[kBV], writes=[kBV])
        P.I('dve', 'scalar_tensor_tensor', o, o, ctx['gdn_nw'][:, l:l + 1], rt, op0=ALU.mult, op1=ALU.mult,
            reads=okr + [kBV, 'gdn_nw'], writes=okeys)
        P.I('dve', 'tensor_tensor', catT[:, 4 + h, tb * 512:(tb + 1) * 512], o, tbs(GS, tb), op=ALU.mult,
            reads=okr + [kGS], writes=[('cat', 4 + h)])


TWO_PI = float(2 * np.pi)


def s5_layer(ctx, l):
    P, W, hT, catT, arena, bank, banks = ctx['P'], ctx['W'], ctx['hT'], ctx['catT'], ctx['arena'], ctx['bank'], ctx['banks']
    consts, ident_b, copy_op, rr = ctx['consts'], ctx['ident_b'], ctx['copy_op'], ctx['rr']
    tau, fs = ctx['s5_tau'], ctx['s5_fs']

    def SF(i):
        return arena[:, 12288 + T * i:12288 + T * (i + 1)], ('slot', i)

    def SB(i, half):
        return arena[:, 12288 + T * i:12288 + T * (i + 1)].bitcast(BF16)[:, half * T:(half + 1) * T], ('slot', i)
    base = 12288

    def R(o, n, key):
        return arena[:, base + o:base + o + n], ('s5', key)
    UTF, kUTF = R(0, 1536, 'utf')
    UTB, kUTB = arena[:, base + 1536:base + 2304].bitcast(BF16), ('s5', 'utb')
    COSs = [R(2304, 1024, 'cos0'), R(4352, 1024, 'cos1')]
    SINs = [R(3328, 1024, 'sin0'), R(5376, 1024, 'sin1')]
    BR, kBR = R(6400, 1536, 'br')
    BI, kBI = R(7936, 1536, 'bi')
    T1, kT1 = R(9472, 512, 't1')
    T2, kT2 = R(9984, 512, 't2')
    P1, kP1 = R(10496, 512, 'p1')
    P2, kP2 = R(11008, 512, 'p2')
    XR, kXR = arena[:, base + 11520:base + 12288].bitcast(BF16), ('s5', 'xr')
    XI, kXI = arena[:, base + 12288:base + 13056].bitcast(BF16), ('s5', 'xi')
    MXs = [R(13056, 1024, 'mx0'), R(14080, 1024, 'mx1')]
    GT1, kGT1 = R(6400, 1536, 'br')
    GT2, kGT2 = R(7936, 1536, 'bi')
    small = arena[:, SMALL0:SMALL0 + 1280]
    names = ('AR', 'AI', 'DT', 'MAG', 'TH', 'CT', 'STH', 'ABR', 'ABI', 'FR', 'FI', 'NFI', 'X0R', 'X0I', 'TA', 'TB')
    S = {n: small[:, 32 * i:32 * (i + 1)] for i, n in enumerate(names)}
    kp = 's5_par'
    KI = small[:, 512:544].bitcast(I32)
    P.dma('sp', small[:, 0:96].rearrange("p (a b) -> p a b", b=32), ctx['s5p_d'][:, l], writes=[kp])
    P.dma('sp', small[:, 384:448].rearrange("p (a b) -> p a b", b=32), ctx['s5x0_d'][:, l], writes=[kp])

    def E(eng, meth, *a, **k):
        P.I(eng, meth, *a, reads=[kp] + k.pop('r', []), writes=[kp] + k.pop('w', []), **k)

    def sincos(dst_sin, dst_cos, ang, tmp, ki, n):
        for dst, sh in ((dst_sin, 0.0), (dst_cos, 0.25)):
            E('dve', 'tensor_scalar', tmp, ang, 1.0 / TWO_PI, sh, op0=ALU.mult, op1=ALU.add)
            E('dve', 'tensor_copy', ki, tmp)
            E('dve', 'tensor_copy', dst, ki)
            E('dve', 'tensor_tensor', tmp, tmp, dst, op=ALU.subtract)
            E('act', 'activation', dst, tmp, AF.Sin, scale=TWO_PI)
    E('act', 'activation', S['DT'], small[:, 64:96], AF.Exp)
    E('dve', 'tensor_tensor', S['TA'], S['AR'], S['DT'], op=ALU.mult)
    E('act', 'activation', S['MAG'], S['TA'], AF.Exp)
    E('dve', 'tensor_tensor', S['TH'], S['AI'], S['DT'], op=ALU.mult)
    sincos(S['STH'], S['CT'], S['TH'], S['TA'], KI, 32)
    E('dve', 'tensor_scalar', S['TA'], S['TH'], 1.0 / TWO_PI, None, op0=ALU.mult)
    E('dve', 'tensor_copy', KI, S['TA'])
    E('dve', 'tensor_copy', S['TB'], KI)
    E('dve', 'scalar_tensor_tensor', S['TH'], S['TB'], -TWO_PI, S['TH'], op0=ALU.mult, op1=ALU.add)
    E('dve', 'tensor_tensor', S['ABR'], S['MAG'], S['CT'], op=ALU.mult)
    E('dve', 'tensor_tensor', S['ABI'], S['MAG'], S['STH'], op=ALU.mult)
    E('dve', 'tensor_tensor', S['TA'], S['AR'], S['AR'], op=ALU.mult)
    E('dve', 'tensor_tensor', S['TB'], S['AI'], S['AI'], op=ALU.mult)
    E('dve', 'tensor_tensor', S['TA'], S['TA'], S['TB'], op=ALU.add)
    E('dve', 'reciprocal', S['TA'], S['TA'])
    E('dve', 'tensor_scalar', S['TB'], S['ABR'], -1.0, None, op0=ALU.add)
    E('dve', 'tensor_tensor', S['FR'], S['TB'], S['AR'], op=ALU.mult)
    E('dve', 'tensor_tensor', S['FI'], S['ABI'], S['AI'], op=ALU.mult)
    E('dve', 'tensor_tensor', S['FR'], S['FR'], S['FI'], op=ALU.add)
    E('dve', 'tensor_tensor', S['FR'], S['FR'], S['TA'], op=ALU.mult)
    E('dve', 'tensor_tensor', S['FI'], S['ABI'], S['AR'], op=ALU.mult)
    E('dve', 'tensor_tensor', S['TB'], S['TB'], S['AI'], op=ALU.mult)
    E('dve', 'tensor_tensor', S['FI'], S['FI'], S['TB'], op=ALU.subtract)
    E('dve', 'tensor_tensor', S['FI'], S['FI'], S['TA'], op=ALU.mult)
    E('dve', 'tensor_scalar', S['NFI'], S['FI'], -1.0, None, op0=ALU.mult)
    ETM = small[:, 576:641]
    EKI = small[:, 648:713].bitcast(I32)
    EAN = small[:, 720:785]
    ESNs = [small[:, 792:857], small[:, 936:1001]]
    ECSs = [small[:, 864:929], small[:, 1008:1073]]
    INI = small[:, 1080:1088]
    ke = 's5_e'
    ybanks = [(banks[5 + tb], ('bank', 5 + tb)) for tb in range(3)]

    for cc in range(4):
        wv, wkey = W.next(('in', l, (('su', cc))))
        for tb in range(3):
            bk, bkey = bank()
            for kc in range(KC):
                P.I('pe', 'matmul', bk[:], wv[:, kc, :], hT[:, kc, tb * 512:(tb + 1) * 512], start=(kc == 0), stop=(kc == KC - 1),
                    reads=[wkey, ('hT', kc, tb)], writes=[bkey])
            P.I('act', 'copy', UTF[:, tb * 512:(tb + 1) * 512], bk[:], reads=[bkey], writes=[kUTF])
            P.I('dve', 'tensor_copy', UTB[:, tb * 512:(tb + 1) * 512], UTF[:, tb * 512:(tb + 1) * 512], reads=[kUTF], writes=[kUTB])
        pairs = [(sc, d) for sc in range(4 * cc, 4 * cc + 4) for d in range(2)]
        segs = [(0, 2, 256, 0), (512, 1, 512, 0), (1024, 1, 512, 512)]

        def prep(k):
            sc, d = pairs[k]
            col = d * 16 + sc
            cs = lambda n: S[n][:, col:col + 1]
            (MX, kMX), (COS, kCOS), (SIN, kSIN) = MXs[k % 2], COSs[k % 2], SINs[k % 2]
            ESN, ECS, kee = ESNs[k % 2], ECSs[k % 2], ('s5_e', k % 2)
            mxf = MX[:, 0:512].rearrange("p (a b) -> p a b", b=128)
            mxb = MX[:, 512:1024].bitcast(BF16).rearrange("p (a b) -> p a b", b=128)
            P.dma('sp', mxf[:, 0:2, :], ctx['s5bx'][:, l, d, sc].rearrange("r s c -> s r c"), writes=[kMX])
            P.dma('sp', mxf[:, 2:4, :], ctx['s5cx'][:, l, d, sc].rearrange("r s c -> s r c"), writes=[kMX])
            P.I('act', 'activation', P1[:, 0:128], mxf[:, 0, :], AF.Identity, scale=cs('FR'), reads=[kMX, kp], writes=[kP1])
            P.I('dve', 'scalar_tensor_tensor', mxb[:, 0, :], mxf[:, 1, :], cs('NFI'), P1[:, 0:128], op0=ALU.mult, op1=ALU.add,
                reads=[kMX, kP1, kp], writes=[kMX])
            P.I('act', 'activation', P1[:, 128:256], mxf[:, 1, :], AF.Identity, scale=cs('FR'), reads=[kMX, kp], writes=[kP1])
            P.I('dve', 'scalar_tensor_tensor', mxb[:, 1, :], mxf[:, 0, :], cs('FI'), P1[:, 128:256], op0=ALU.mult, op1=ALU.add,
                reads=[kMX, kP1, kp], writes=[kMX])
            bk, bkey = bank()
            bkb = bk[:].bitcast(BF16)
            for i in range(2):
                P.I('pe', 'transpose', bkb[:, i * 128:(i + 1) * 128], mxb[:, i, :], ident_b[:], reads=[kMX, 'ident_b'], writes=[bkey])
            P.I('act', 'copy', mxb[:, 2:4, :], bkb[:, 0:256].rearrange("p (a b) -> p a b", b=128), reads=[bkey], writes=[kMX])
            P.I('dve', 'tensor_copy', mxb[:, 4, :], mxf[:, 2, :], reads=[kMX], writes=[kMX])
            P.I('dve', 'tensor_scalar', mxb[:, 5, :], mxf[:, 3, :], -1.0, None, op0=ALU.mult, reads=[kMX], writes=[kMX])
            P.I('dve', 'tensor_scalar', EAN, tau[:], cs('TH'), None, op0=ALU.mult, reads=['s5_tau', kp], writes=[ke])
            for dst, sh in ((ESN, 0.0), (ECS, 0.25)):
                P.I('dve', 'tensor_scalar', ETM, EAN, 1.0 / TWO_PI, sh, op0=ALU.mult, op1=ALU.add, reads=[ke], writes=[ke])
                P.I('dve', 'tensor_copy', EKI, ETM, reads=[ke], writes=[ke])
                P.I('dve', 'tensor_copy', dst, EKI, reads=[ke], writes=[kee])
                P.I('dve', 'tensor_tensor', ETM, ETM, dst, op=ALU.subtract, reads=[ke, kee], writes=[ke])
                P.I('act', 'activation', dst, ETM, AF.Sin, scale=TWO_PI, reads=[ke], writes=[kee])
            hi = lambda t: t[:, 32:64].unsqueeze(2).to_broadcast([128, 32, 32])
            lo = lambda t: t[:, 0:32].unsqueeze(1).to_broadcast([128, 32, 32])
            c3 = COS.rearrange("p (a b) -> p a b", b=32)
            s3 = SIN.rearrange("p (a b) -> p a b", b=32)
            q1 = P1.rearrange("p (a b) -> p a b", b=32)[:, 0:16, :]
            for hh in range(2):
                hs = slice(16 * hh, 16 * hh + 16)
                hi_ = lambda t: t[:, 32 + 16 * hh:48 + 16 * hh].unsqueeze(2).to_broadcast([128, 16, 32])
                lo_ = lambda t: t[:, 0:32].unsqueeze(1).to_broadcast([128, 16, 32])
                p1v = P1.rearrange("p (a b) -> p a b", b=32)
                p2v = P2.rearrange("p (a b) -> p a b", b=32)
                P.I('pool', 'tensor_tensor', c3[:, hs, :], hi_(ECS), lo_(ECS), op=ALU.mult, reads=[kee], writes=[kCOS])
                P.I('pool', 'tensor_tensor', p1v, hi_(ESN), lo_(ESN), op=ALU.mult, reads=[kee], writes=[kP1])
                P.I('pool', 'tensor_tensor', c3[:, hs, :], c3[:, hs, :], p1v, op=ALU.subtract, reads=[kP1, kCOS], writes=[kCOS])
                P.I('pool', 'tensor_tensor', s3[:, hs, :], hi_(ESN), lo_(ECS), op=ALU.mult, reads=[kee], writes=[kSIN])
                P.I('pool', 'tensor_tensor', p2v, hi_(ECS), lo_(ESN), op=ALU.mult, reads=[kee], writes=[kP2])
                P.I('pool', 'tensor_tensor', s3[:, hs, :], s3[:, hs, :], p2v, op=ALU.add, reads=[kP2, kSIN], writes=[kSIN])

        def demod(k):
            sc, d = pairs[k]
            (MX, kMX), (COS, kCOS), (SIN, kSIN) = MXs[k % 2], COSs[k % 2], SINs[k % 2]
            mxb = MX[:, 512:1024].bitcast(BF16).rearrange("p (a b) -> p a b", b=128)
            BBTr, BBTi = mxb[:, 2, :], mxb[:, 3, :]
            for tb in range(3):
                pr, kpr = bank()
                pi, kpi = bank()
                P.I('pe', 'matmul', pr[:], BBTr, UTB[:, tb * 512:(tb + 1) * 512], start=True, stop=True, reads=[kMX, kUTB], writes=[kpr])
                P.I('pe', 'matmul', pi[:], BBTi, UTB[:, tb * 512:(tb + 1) * 512], start=True, stop=True, reads=[kMX, kUTB], writes=[kpi])
                t0, nrep, ln, ta0 = segs[tb]
                v = lambda ap: ap[:, t0:t0 + nrep * ln].rearrange("p (r n) -> p r n", n=ln)
                w = lambda ap: ap.rearrange("p (r n) -> p r n", n=ln)
                tv = lambda tab: tab[:, ta0:ta0 + ln].unsqueeze(1).to_broadcast([128, nrep, ln])
                P.I('dve', 'tensor_tensor', v(BR), w(pr[:]), tv(COS), op=ALU.mult, reads=[kpr, kCOS], writes=[kBR])
                P.I('dve', 'tensor_tensor', w(T1), w(pi[:]), tv(SIN), op=ALU.mult, reads=[kpi, kSIN], writes=[kT1])
                P.I('pool', 'tensor_tensor', v(BR), v(BR), w(T1), op=(ALU.add if d == 0 else ALU.subtract), reads=[kBR, kT1], writes=[kBR])
                P.I('dve', 'tensor_tensor', v(BI), w(pi[:]), tv(COS), op=ALU.mult, reads=[kpi, kCOS], writes=[kBI])
                P.I('dve', 'tensor_tensor', w(T2), w(pr[:]), tv(SIN), op=ALU.mult, reads=[kpr, kSIN], writes=[kT2])
                P.I('pool', 'tensor_tensor', v(BI), v(BI), w(T2), op=(ALU.subtract if d == 0 else ALU.add), reads=[kBI, kT2], writes=[kBI])

        def main(k):
            sc, d = pairs[k]
            col = d * 16 + sc
            cs = lambda n: S[n][:, col:col + 1]
            (MX, kMX), (COS, kCOS), (SIN, kSIN) = MXs[k % 2], COSs[k % 2], SINs[k % 2]
            ESN, ECS, kee = ESNs[k % 2], ECSs[k % 2], ('s5_e', k % 2)
            mxb = MX[:, 512:1024].bitcast(BF16).rearrange("p (a b) -> p a b", b=128)
            CCTr, CCTi = mxb[:, 4, :], mxb[:, 5, :]
            if d == 0:
                cph, sph = COS[:, 1:2], SIN[:, 1:2]
            else:
                cph, sph = ECS[:, 64:65], ESN[:, 64:65]
            ki = 's5_ini'
            P.I('dve', 'tensor_tensor', INI[:, 0:1], cs('X0R'), cph, op=ALU.mult, reads=[kp, kCOS, kee], writes=[ki])
            P.I('dve', 'tensor_tensor', INI[:, 1:2], cs('X0I'), sph, op=ALU.mult, reads=[kp, kSIN, kee], writes=[ki])
            P.I('dve', 'tensor_tensor', INI[:, 0:1], INI[:, 0:1], INI[:, 1:2], op=ALU.subtract, reads=[ki], writes=[ki])
            P.I('dve', 'tensor_tensor', INI[:, 2:3], cs('X0I'), cph, op=ALU.mult, reads=[kp, kCOS, kee], writes=[ki])
            P.I('dve', 'tensor_tensor', INI[:, 3:4], cs('X0R'), sph, op=ALU.mult, reads=[kp, kSIN, kee], writes=[ki])
            P.I('dve', 'tensor_tensor', INI[:, 2:3], INI[:, 2:3], INI[:, 3:4], op=ALU.add, reads=[ki], writes=[ki])
            for si, (s0, sl, row) in enumerate(SEQS):
                rho = cs('MAG').to_broadcast([128, sl])
                for buf, kb, ini in ((BR, kBR, INI[:, 0:1]), (BI, kBI, INI[:, 2:3])):
                    seg = buf[:, s0:s0 + sl]
                    if d == 1:
                        seg = seg[:, ::-1]
                    P.I('dve', 'tensor_tensor_scan', seg, rho, seg, (ini if row == 1 else 0.0), op0=ALU.mult, op1=ALU.add,
                        reads=[kb, kp, ki], writes=[kb])
                if row == 0:
                    fcol = ((si * 2 + l) * 2 + d) * 16 + sc
                    if d == 1:
                        P.I('dve', 'tensor_copy', fs[0][:, fcol:fcol + 1], BR[:, s0:s0 + 1], reads=[kBR], writes=[('s5_fs', 0)])
                        P.I('dve', 'tensor_copy', fs[1][:, fcol:fcol + 1], BI[:, s0:s0 + 1], reads=[kBI], writes=[('s5_fs', 1)])
                    else:
                        e = s0 + sl - 1
                        cL, sL = COS[:, sl - 1:sl], SIN[:, sl - 1:sl]
                        P.I('dve', 'tensor_tensor', INI[:, 4:5], BR[:, e:e + 1], cL, op=ALU.mult, reads=[kBR, kCOS], writes=[ki])
                        P.I('dve', 'tensor_tensor', INI[:, 5:6], BI[:, e:e + 1], sL, op=ALU.mult, reads=[kBI, kSIN], writes=[ki])
                        P.I('dve', 'tensor_tensor', fs[0][:, fcol:fcol + 1], INI[:, 4:5], INI[:, 5:6], op=ALU.subtract, reads=[ki], writes=[('s5_fs', 0)])
                        P.I('dve', 'tensor_tensor', INI[:, 6:7], BR[:, e:e + 1], sL, op=ALU.mult, reads=[kBR, kSIN], writes=[ki])
                        P.I('dve', 'tensor_tensor', INI[:, 7:8], BI[:, e:e + 1], cL, op=ALU.mult, reads=[kBI, kCOS], writes=[ki])
                        P.I('dve', 'tensor_tensor', fs[1][:, fcol:fcol + 1], INI[:, 6:7], INI[:, 7:8], op=ALU.add, reads=[ki], writes=[('s5_fs', 1)])
            for t0, nrep, ln, ta0 in segs:
                v = lambda ap: ap[:, t0:t0 + nrep * ln].rearrange("p (r n) -> p r n", n=ln)
                w = lambda ap: ap.rearrange("p (r n) -> p r n", n=ln)
                tv = lambda tab: tab[:, ta0:ta0 + ln].unsqueeze(1).to_broadcast([128, nrep, ln])
                P.I('dve', 'tensor_tensor', w(T1), v(BR), tv(COS), op=ALU.mult, reads=[kBR, kCOS], writes=[kT1])
                P.I('dve', 'tensor_tensor', w(T2), v(BI), tv(SIN), op=ALU.mult, reads=[kBI, kSIN], writes=[kT2])
                P.I('dve', 'tensor_tensor', v(XR), w(T1), w(T2), op=(ALU.subtract if d == 0 else ALU.add), reads=[kT1, kT2], writes=[kXR])
                P.I('pool', 'tensor_tensor', w(P1), v(BI), tv(COS), op=ALU.mult, reads=[kBI, kCOS], writes=[kP1])
                P.I('pool', 'tensor_tensor', w(P2), v(BR), tv(SIN), op=ALU.mult, reads=[kBR, kSIN], writes=[kP2])
                P.I('pool', 'tensor_tensor', v(XI), w(P1), w(P2), op=(ALU.add if d == 0 else ALU.subtract), reads=[kP1, kP2], writes=[kXI])
            for tb in range(3):
                yb, kyb = ybanks[tb]
                P.I('pe', 'matmul', yb[:], CCTr, XR[:, tb * 512:(tb + 1) * 512], start=(k == 0), stop=False, reads=[kMX, kXR], writes=[kyb])
                P.I('pe', 'matmul', yb[:], CCTi, XI[:, tb * 512:(tb + 1) * 512], start=False, stop=(k == 7), reads=[kMX, kXI], writes=[kyb])

        prep(0)
        for k in range(8):
            demod(k)
            if k < 7:
                prep(k + 1)
            main(k)
        dcol = ctx['s5_d'][:, l, 0, cc:cc + 1]
        for tb in range(3):
            yb, kyb = ybanks[tb]
            sl_ = slice(tb * 512, (tb + 1) * 512)
            P.I('dve', 'scalar_tensor_tensor', GT1[:, sl_], UTF[:, sl_], dcol, yb[:], op0=ALU.mult, op1=ALU.add, reads=[kUTF, kyb, 's5_d'], writes=[kGT1])
            P.I('act', 'activation', GT2[:, sl_], GT1[:, sl_], AF.Square, reads=[kGT1], writes=[kGT2])
            P.I('dve', 'tensor_scalar', GT2[:, sl_], GT2[:, sl_], 0.044715, 1.0, op0=ALU.mult, op1=ALU.add, reads=[kGT2], writes=[kGT2])
            P.I('dve', 'tensor_tensor', GT2[:, sl_], GT2[:, sl_], GT1[:, sl_], op=ALU.mult, reads=[kGT2, kGT1], writes=[kGT2])
            P.I('act', 'activation', GT2[:, sl_], GT2[:, sl_], AF.Sigmoid, scale=2.0 * 0.7978845608028654, reads=[kGT2], writes=[kGT2])
            P.I('dve', 'tensor_tensor', catT[:, 12 + cc, sl_], GT1[:, sl_], GT2[:, sl_], op=ALU.mult, reads=[kGT1, kGT2], writes=[('cat', 12 + cc)])
    gates = [R(0, 1536, 'utf'), R(6400, 1536, 'br'), R(7936, 1536, 'bi'), R(2304, 1536, 'cos0')]
    for oc in range(4):
        wv, wkey = W.next(('glu', l, oc))
        gb = ctx['s5_d'][:, l, 1, oc:oc + 1]
        for tb in range(3):
            bk, bkey = bank()
            for kc in range(4):
                P.I('pe', 'matmul', bk[:], wv[:, kc, :], catT[:, 12 + kc, tb * 512:(tb + 1) * 512], start=(kc == 0), stop=(kc == 3),
                    reads=[wkey, ('cat', 12 + kc)], writes=[bkey])
            P.I('act', 'activation', gates[oc][0][:, tb * 512:(tb + 1) * 512], bk[:], AF.Sigmoid, bias=gb, reads=[bkey, 's5_d'], writes=[gates[oc][1]])
    for oc in range(4):
        P.I('dve', 'tensor_tensor', catT[:, 12 + oc, :], catT[:, 12 + oc, :], gates[oc][0], op=ALU.mult,
            reads=[('cat', 12 + oc), gates[oc][1]], writes=[('cat', 12 + oc)])


def s5_finish(ctx):
    P, bank, ident_f = ctx['P'], ctx['bank'], ctx['ident_f']
    for i in range(2):
        bk, bkey = bank()
        P.I('pe', 'transpose', bk[:, 0:128], ctx['s5_fs'][i][:], ident_f, reads=[('s5_fs', i), 'consts'], writes=[bkey])
        P.I('act', 'copy', ctx['s5_fs'][i][:], bk[:, 0:128], reads=[bkey], writes=[('s5_fs', i)])
        P.dma('sp', ctx['new_s5'][i], ctx['s5_fs'][i][:], reads=[('s5_fs', i)], writes=[('new_s5', i)])


def host_inputs(inputs, core):
    f = np.float32
    xp, xsm = inputs['x_prompt'], inputs['x_sample']
    xin = np.concatenate([xp[2 * core], xp[2 * core + 1], xsm[core]], axis=0).astype(f)
    c2 = np.stack([inputs['c_ctx'], inputs['c'][core]], axis=0)
    cT = np.ascontiguousarray(c2.reshape(2, KC, 128).transpose(2, 1, 0))

    def fm(v):
        return np.ascontiguousarray(v.reshape(v.shape[:-1] + (KC, 128)).swapaxes(-1, -2))
    m = {
        'xin': xin, 'cT': cT, 'n1w': fm(inputs['norm1_w']), 'n2w': fm(inputs['norm2_w']), 'fnw': fm(inputs['final_norm_w']),
        'adab': np.ascontiguousarray(inputs['ada_b'].reshape(2, 96, 128).swapaxes(1, 2)),
        'consts': make_consts(),
    }
    for k in ('ada_w', 'in_proj', 'out_proj', 'ffn_w1', 'ffn_w3', 'ffn_w2'):
        m[k] = np.asarray(inputs[k], dtype=f)
    rc, rs = make_rope()
    m.update(rm=make_rm(), rope_c=rc, rope_s=rs)
    m['rdl'] = np.ascontiguousarray(np.broadcast_to(inputs['ret_decay_logit'].reshape(1, 16), (128, 16)), f)
    m['hlb'] = np.ascontiguousarray(inputs['hg_lb_param'].reshape(2, 2, 4, 128).transpose(3, 0, 1, 2).reshape(128, 2, 8), f)
    m['hgnw'] = np.ascontiguousarray(inputs['hg_norm_w'].T, f)
    gpar = np.stack([inputs['gdn_dt_bias'].reshape(2, 8), inputs['gdn_a_log'].reshape(2, 8)], axis=1)
    m['gdnp'] = np.ascontiguousarray(np.broadcast_to(gpar[None], (128, 2, 2, 8)), f)
    m['gdncw'] = np.ascontiguousarray(inputs['gdn_conv'].reshape(2, 3, 12, 128).transpose(3, 0, 1, 2), f)
    m['gdnnw'] = np.ascontiguousarray(inputs['gdn_norm_w'].T, f)
    def sm(v):
        v = v.reshape(v.shape[:-2] + (16, 128))
        return np.ascontiguousarray(np.moveaxis(v, -1, 0), f)
    ls = np.broadcast_to(inputs['s5_log_step'][..., None], (2, 2, 32, 64))
    par = np.stack([sm(inputs['s5_a_re']), sm(inputs['s5_a_im']), sm(ls)], axis=2)
    m['s5p'] = np.ascontiguousarray(par.reshape(128, 2, 3, 32), f)
    x0 = np.stack([sm(inputs['state_s5_re'][core]), sm(inputs['state_s5_im'][core])], axis=2)
    m['s5x0'] = np.ascontiguousarray(x0.reshape(128, 2, 2, 32), f)
    bx = np.zeros((2, 2, 2, 32, 64, 128), f)
    cx = np.zeros((2, 2, 2, 32, 64, 128), f)
    for g in range(32):
        c0 = 16 * (g % 8)
        bx[0, :, :, g, :, c0:c0 + 16] = inputs['s5_b_re'][:, :, g]
        bx[1, :, :, g, :, c0:c0 + 16] = inputs['s5_b_im'][:, :, g]
        cx[0, :, :, g, :, c0:c0 + 16] = np.swapaxes(inputs['s5_c_re'][:, :, g], -1, -2)
        cx[1, :, :, g, :, c0:c0 + 16] = np.swapaxes(inputs['s5_c_im'][:, :, g], -1, -2)
    m['s5bx'] = bx.reshape(2, 2, 2, 16, 128, 128)
    m['s5cx'] = cx.reshape(2, 2, 2, 16, 128, 128)
    dg = np.stack([inputs['s5_d'].reshape(2, 4, 128), inputs['s5_glu_b'].reshape(2, 4, 128)], axis=1)
    m['s5dg'] = np.ascontiguousarray(dg.transpose(3, 0, 1, 2), f)
    tau = np.concatenate([np.arange(32), 32 * np.arange(33)]).astype(f)
    m['s5tau'] = np.ascontiguousarray(np.broadcast_to(tau, (128, 65)), f)
    m['s5_glu_w'] = np.asarray(inputs['s5_glu_w'], f)
    m['st_gdn'] = np.ascontiguousarray(inputs['state_gdn'][core], f)
    m['st_ret'] = np.ascontiguousarray(inputs['state_ret'][core], f)
    m['st_hg'] = np.ascontiguousarray(inputs['state_hgrn'][core], f)
    return m


def kernel(**inputs):
    inputs = {k: np.asarray(v) for k, v in inputs.items()}
    nc = build_program()
    in_maps = [host_inputs(inputs, i) for i in range(8)]
    res = run_bass_kernel_spmd(nc, in_maps, core_ids=list(range(8)))
    return assemble(res.results, inputs)


def assemble(results, inputs):
    f = np.float32
    yp = np.zeros((16, 256, D), f)
    ysm = np.zeros((8, 1024, D), f)
    st = {k: np.zeros((16, 2, 2, 4, 128, 128), f) for k in ('new_ret', 'new_gdn', 'new_hg')}
    s5 = {k: np.zeros((16, 2, 2, 32, 64), f) for k in ('new_s5re', 'new_s5im')}
    for i, r in enumerate(results):
        y = np.asarray(r['y'], f)
        yp[2 * i] = y[0:256]
        yp[2 * i + 1] = y[256:512]
        ysm[i] = y[512:]
        for k in st:
            st[k][2 * i:2 * i + 2] = np.asarray(r[k], f)
        for k in s5:
            s5[k][2 * i:2 * i + 2] = np.asarray(r[k], f).reshape(2, 2, 2, 32, 64)
    return (yp, ysm, st['new_ret'], st['new_gdn'], st['new_hg'], s5['new_s5re'], s5['new_s5im'])
```
